# Optimizing a Trainium2 kernel written in Bass

```python
import jax, jax.numpy as jnp
from jax import lax
import numpy as np

D_MODEL = 2048
BATCH = 4
SEQ = 2048
DEPTH = 1
DEC_BATCH = 128
DEC_SEQ = 1
PAST_LEN = 16384
PAGE_SIZE = 128

H_A = 8
CH_A = D_MODEL // (2 * H_A)
W_A = H_A * CH_A
CHUNK = 128
H_B = 8
DK_B = D_MODEL // (2 * H_B)
DV_B = DK_B
W_B = H_B * DV_B
RET_CHUNK = 128
ROPE_THETA = 10000.0
IN_COLS = 2 * W_A + 4 * W_B
N_MEM = 256
H_X = 4
DH_X = D_MODEL // H_X
D_FF = 4 * D_MODEL
EPS = 1e-6

kernel_name = "hymba_gmlp_retnet_memxattn_step"


def rmsnorm(x, g):
    xf = x.astype(jnp.float32)
    y = xf * lax.rsqrt(jnp.mean(xf * xf, axis=-1, keepdims=True) + EPS)
    return (y * g.astype(jnp.float32)).astype(x.dtype)


def layernorm(x, g):
    xf = x.astype(jnp.float32)
    mu = jnp.mean(xf, axis=-1, keepdims=True)
    var = jnp.mean(jnp.square(xf - mu), axis=-1, keepdims=True)
    return ((xf - mu) * lax.rsqrt(var + EPS) * g.astype(jnp.float32)).astype(x.dtype)


def rotary(x, pos):
    half = x.shape[-1] // 2
    freqs = ROPE_THETA ** (-jnp.arange(half, dtype=jnp.float32) / half)
    ang = pos[:, None] * freqs[None, :]
    cos = jnp.cos(ang)[None, :, None, :]
    sin = jnp.sin(ang)[None, :, None, :]
    xf = x.astype(jnp.float32)
    x1, x2 = xf[..., :half], xf[..., half:]
    return jnp.concatenate([x1 * cos - x2 * sin, x1 * sin + x2 * cos], axis=-1)


def chunk_spatial_mix(v, w_s, b):
    B, S, H, C = v.shape
    c = CHUNK if S >= CHUNK else S
    nc = -(-S // c)
    pad = nc * c - S
    vp = jnp.pad(v, ((0, 0), (0, pad), (0, 0), (0, 0))).reshape(B, nc, c, H, C)
    w = w_s[:, :c, :c] * jnp.tril(jnp.ones((c, c), w_s.dtype))
    out = jnp.einsum('hij,bnjhc->bnihc', w, vp) + jnp.transpose(b[:, :c])[None, None, :, :, None]
    return out.reshape(B, nc * c, H, C)[:, :S]


def retention(q, k, v, log_g, s0):
    B, S, H, dk = q.shape
    dv = v.shape[-1]
    c = RET_CHUNK if S % RET_CHUNK == 0 else S
    nc = S // c
    idx = jnp.arange(c, dtype=jnp.float32)
    diff = idx[:, None] - idx[None, :]
    decay_mask = jnp.where(diff[None] >= 0,
                           jnp.exp(log_g[:, None, None] * jnp.maximum(diff, 0.0)[None]), 0.0)
    q_decay = jnp.transpose(jnp.exp(log_g[:, None] * (idx[None, :] + 1.0)))[None, :, :, None]
    k_decay = jnp.transpose(jnp.exp(log_g[:, None] * (c - 1.0 - idx[None, :])))[None, :, :, None]
    chunk_decay = jnp.exp(log_g * c)[None, :, None, None]

    def to_chunks(t):
        return t.reshape(B, nc, c, H, t.shape[-1]).transpose(1, 0, 2, 3, 4)

    def step(state, inp):
        qc, kc, vc = inp
        scores = jnp.einsum('bihd,bjhd->bhij', qc, kc) * decay_mask[None]
        intra = jnp.einsum('bhij,bjhe->bihe', scores, vc)
        cross = jnp.einsum('bihd,bhde->bihe', qc, state) * q_decay
        new_state = state * chunk_decay + jnp.einsum('bjhd,bjhe->bhde', kc * k_decay, vc)
        return new_state, intra + cross

    s_final, out = lax.scan(step, s0, (to_chunks(q), to_chunks(k), to_chunks(v)))
    out = out.transpose(1, 0, 2, 3, 4).reshape(B, S, H, dv)
    return out, s_final


def mixing_sublayer(xn, pos, s0, w_in, sgu_norm_g, sgu_w_s, sgu_b, ret_gn_g, w_out):
    B, S, _ = xn.shape
    proj = xn @ w_in
    z_u, z_v, q, k, v, g = jnp.split(
        proj, [W_A, 2 * W_A, 2 * W_A + W_B, 2 * W_A + 2 * W_B, 2 * W_A + 3 * W_B], axis=-1)
    z_u = jax.nn.gelu(z_u)
    z_v = layernorm(jax.nn.gelu(z_v), sgu_norm_g)
    v_rows = z_v.reshape(B, S, H_A, CH_A)
    out_a = z_u * chunk_spatial_mix(v_rows, sgu_w_s, sgu_b).reshape(B, S, W_A)
    log_g = jnp.log(1.0 - jnp.exp2(-5.0 - jnp.arange(H_B, dtype=jnp.float32)))
    qh = rotary(q.reshape(B, S, H_B, DK_B), pos)
    kh = rotary(k.reshape(B, S, H_B, DK_B), pos) * (DK_B ** -0.5)
    vh = v.reshape(B, S, H_B, DV_B).astype(jnp.float32)
    ret, s_new = retention(qh, kh, vh, log_g, s0.astype(jnp.float32))
    mu = jnp.mean(ret, axis=-1, keepdims=True)
    var = jnp.mean(jnp.square(ret - mu), axis=-1, keepdims=True)
    ret = ((ret - mu) * lax.rsqrt(var + EPS)).reshape(B, S, W_B) * ret_gn_g.astype(jnp.float32)
    out_b = jax.nn.silu(g) * ret.astype(xn.dtype)
    out = jnp.concatenate([out_a, out_b], axis=-1) @ w_out
    return out, s_new.astype(s0.dtype), v_rows


def memory_kv(mem, mem_norm_g, w_ck, w_cv):
    B = mem.shape[0]
    mn = rmsnorm(mem, mem_norm_g)
    mk = (mn @ w_ck).reshape(B, N_MEM, H_X, DH_X)
    mv = (mn @ w_cv).reshape(B, N_MEM, H_X, DH_X)
    return mk, mv


def cross_attend(xn, mk, mv, w_cq, w_co):
    B, S, _ = xn.shape
    q = (xn @ w_cq).reshape(B, S, H_X, DH_X)
    s = jnp.einsum('bshd,bmhd->bhsm', q, mk).astype(jnp.float32) * (DH_X ** -0.5)
    p = jax.nn.softmax(s, axis=-1).astype(xn.dtype)
    o = jnp.einsum('bhsm,bmhd->bshd', p, mv).reshape(B, S, D_MODEL)
    return o @ w_co


def decoder_layer(x, pos, s0, mk, mv, norm1_g, w_in, sgu_norm_g, sgu_w_s, sgu_b, ret_gn_g,
                  w_out, norm2_g, w_cq, w_co, norm3_g, w_ff1, w_ff2):
    mix, s_new, v_rows = mixing_sublayer(rmsnorm(x, norm1_g), pos, s0, w_in, sgu_norm_g,
                                         sgu_w_s, sgu_b, ret_gn_g, w_out)
    h = x + mix
    h = h + cross_attend(rmsnorm(h, norm2_g), mk, mv, w_cq, w_co)
    h = h + jnp.square(jax.nn.relu(rmsnorm(h, norm3_g) @ w_ff1)) @ w_ff2
    return h, s_new, v_rows


def setup_inputs(seed: int = 0) -> dict:
    key = jax.random.key(seed)
    ks = jax.random.split(key, 24)
    f32 = jnp.float32

    def nrm(k, shape, scale):
        return jax.random.normal(k, shape, f32) * scale

    def gain(k, shape):
        return 1.0 + 0.01 * jax.random.normal(k, shape, f32)

    return {
        "x_prompt": nrm(ks[0], (BATCH, SEQ, D_MODEL), 1.0),
        "x_sample": nrm(ks[1], (DEC_BATCH, DEC_SEQ, D_MODEL), 1.0),
        "mem_prompt": nrm(ks[2], (BATCH, N_MEM, D_MODEL), 1.0),
        "cache_mem_k": nrm(ks[3], (DEPTH, DEC_BATCH, N_MEM, H_X, DH_X), 1.0),
        "cache_mem_v": nrm(ks[4], (DEPTH, DEC_BATCH, N_MEM, H_X, DH_X), 1.0),
        "state_ret": nrm(ks[5], (DEPTH, DEC_BATCH, H_B, DK_B, DV_B), 0.1),
        "norm1_g": gain(ks[6], (DEPTH, D_MODEL)),
        "w_in": nrm(ks[7], (DEPTH, D_MODEL, IN_COLS), D_MODEL ** -0.5),
        "sgu_norm_g": gain(ks[8], (DEPTH, W_A)),
        "sgu_w_s": nrm(ks[9], (DEPTH, H_A, CHUNK, CHUNK), 0.5 * CHUNK ** -0.5),
        "sgu_b": 1.0 + 0.01 * jax.random.normal(ks[10], (DEPTH, H_A, CHUNK), f32),
        "ret_gn_g": gain(ks[11], (DEPTH, W_B)),
        "w_out": nrm(ks[12], (DEPTH, W_A + W_B, D_MODEL), (W_A + W_B) ** -0.5),
        "norm2_g": gain(ks[13], (DEPTH, D_MODEL)),
        "mem_norm_g": gain(ks[14], (DEPTH, D_MODEL)),
        "w_cq": nrm(ks[15], (DEPTH, D_MODEL, D_MODEL), D_MODEL ** -0.5),
        "w_ck": nrm(ks[16], (DEPTH, D_MODEL, D_MODEL), D_MODEL ** -0.5),
        "w_cv": nrm(ks[17], (DEPTH, D_MODEL, D_MODEL), D_MODEL ** -0.5),
        "w_co": nrm(ks[18], (DEPTH, D_MODEL, D_MODEL), D_MODEL ** -0.5),
        "norm3_g": gain(ks[19], (DEPTH, D_MODEL)),
        "w_ff1": nrm(ks[20], (DEPTH, D_MODEL, D_FF), D_MODEL ** -0.5),
        "w_ff2": nrm(ks[21], (DEPTH, D_FF, D_MODEL), D_FF ** -0.5),
        "final_norm_g": gain(ks[22], (D_MODEL,)),
    }


def reference(x_prompt, x_sample, mem_prompt, cache_mem_k, cache_mem_v, state_ret,
              norm1_g, w_in, sgu_norm_g, sgu_w_s, sgu_b, ret_gn_g, w_out, norm2_g,
              mem_norm_g, w_cq, w_ck, w_cv, w_co, norm3_g, w_ff1, w_ff2, final_norm_g):
    pos_prompt = jnp.arange(SEQ, dtype=jnp.float32)
    pos_sample = jnp.arange(DEC_SEQ, dtype=jnp.float32) + PAST_LEN
    h_p, h_s = x_prompt, x_sample
    mk_list, mv_list, sp_list, ss_list, vs_list = [], [], [], [], []
    for l in range(DEPTH):
        mk_p, mv_p = memory_kv(mem_prompt, mem_norm_g[l], w_ck[l], w_cv[l])
        s0_p = jnp.zeros((BATCH, H_B, DK_B, DV_B), state_ret.dtype)
        h_p, s_p, _ = decoder_layer(h_p, pos_prompt, s0_p, mk_p, mv_p, norm1_g[l], w_in[l],
                                    sgu_norm_g[l], sgu_w_s[l], sgu_b[l], ret_gn_g[l], w_out[l],
                                    norm2_g[l], w_cq[l], w_co[l], norm3_g[l], w_ff1[l], w_ff2[l])
        h_s, s_s, v_s = decoder_layer(h_s, pos_sample, state_ret[l], cache_mem_k[l], cache_mem_v[l],
                                      norm1_g[l], w_in[l], sgu_norm_g[l], sgu_w_s[l], sgu_b[l],
                                      ret_gn_g[l], w_out[l], norm2_g[l], w_cq[l], w_co[l],
                                      norm3_g[l], w_ff1[l], w_ff2[l])
        mk_list.append(mk_p)
        mv_list.append(mv_p)
        sp_list.append(s_p)
        ss_list.append(s_s)
        vs_list.append(v_s)
    y_prompt = rmsnorm(h_p, final_norm_g)
    y_sample = rmsnorm(h_s, final_norm_g)
    mem_k_prompt = jnp.stack(mk_list, axis=0)
    mem_v_prompt = jnp.stack(mv_list, axis=0)
    state_ret_prompt = jnp.stack(sp_list, axis=0)
    state_ret_sample = jnp.stack(ss_list, axis=0)
    chunk_v_sample = jnp.stack(vs_list, axis=0)
    return (y_prompt, y_sample, mem_k_prompt, mem_v_prompt, state_ret_prompt, state_ret_sample, chunk_v_sample)
```

```python
import contextlib
import numpy as np
import concourse.bass as bass
import concourse.mybir as mybir
from concourse.bass_utils import run_bass_kernel_spmd

F32 = mybir.dt.float32
BF16 = mybir.dt.bfloat16
AF = mybir.ActivationFunctionType
ALU = mybir.AluOpType
AX = mybir.AxisListType

D = 2048
T = 1024
NT = 8
TS = 16
EPS = 1e-6
GAM = [1.0 - 2.0 ** (-5 - h) for h in range(8)]
ENGS = ("pe", "act", "dve", "pool", "sp")

CO = {}
_c = 0
for _n, _w in (("ident", 128), ("maskT", 256), ("gpk", 80), ("gA", 8), ("gnB", 8),
               ("ws00", 8), ("b0", 8), ("cosc", 512), ("sinc", 512),
               ("coss", 64), ("sins", 64), ("qs", 64), ("ks", 64), ("gfin", 8), ("gini", 8),
               ("eye16", 16), ("eyeq", 256)):
    CO[_n] = _c
    _c += _w
NCST = _c


class Op:
    __slots__ = ("eng", "fn", "deps", "signal", "sigval", "dma", "dval", "idx")


class _Rec:
    def __init__(self):
        self.call = None

    def __getattr__(self, name):
        def f(*a, **k):
            self.call = (name, a, k)
            return None
        return f


class Sched:
    def __init__(self):
        self.ops = []
        self.by_eng = {e: [] for e in ENGS}
        self.lastw = {}
        self.readers = {}
        self.dcount = {}
        self.last_real = {e: None for e in ENGS}
        self.dma_since = []

    def add(self, eng, fn, r=(), w=(), dma=None):
        op = Op()
        if fn is not None:
            rec = _Rec()
            fn(rec)
            assert rec.call is not None
            call = rec.call
            fn = lambda e, call=call: getattr(e, call[0])(*call[1], **call[2])
        op.eng, op.fn, op.dma, op.signal, op.sigval, op.dval = eng, fn, dma, False, 0, 0
        op.idx = len(self.ops)
        deps = {}

        def adddep(d):
            if d is None or d is op:
                return
            if d.dma is None and dma is None and d.eng == "pe" and eng == "pe":
                return
            key = ("d", id(d)) if d.dma is not None else ("e", d.eng)
            o = deps.get(key)
            if o is None or o.idx < d.idx:
                deps[key] = d

        for k in list(r) + list(w):
            adddep(self.lastw.get(k))
        for k in w:
            rd = self.readers.get(k)
            if rd:
                for d in rd.values():
                    adddep(d)
        op.deps = list(deps.values())
        for d in op.deps:
            d.signal = True
        for k in w:
            self.lastw[k] = op
            self.readers[k] = {}
        for k in r:
            rk = ("d", op.idx) if dma is not None else ("e", eng)
            self.readers.setdefault(k, {})[rk] = op
        if dma is not None:
            self.dcount[dma] = self.dcount.get(dma, 0) + 1
            op.dval = 16 * self.dcount[dma]
            self.dma_since.append(op)
        self.ops.append(op)
        self.by_eng[eng].append(op)
        if fn is not None:
            self.last_real[eng] = op
        return op

    def barrier(self):
        lasts = [self.last_real[e] for e in ENGS if self.last_real[e] is not None]
        dmas = list(self.dma_since)
        self.dma_since = []
        for e in ENGS:
            op = Op()
            op.eng, op.fn, op.dma, op.signal, op.sigval, op.dval = e, None, None, False, 0, 0
            op.idx = len(self.ops)
            deps = []
            for d in lasts:
                if d.eng != e or d.dma is not None:
                    deps.append(d)
            chan = {}
            for d in dmas:
                if d.dma not in chan or chan[d.dma].idx < d.idx:
                    chan[d.dma] = d
            deps += [d for d in chan.values() if d not in deps]
            op.deps = deps
            for d in deps:
                d.signal = True
            self.ops.append(op)
            self.by_eng[e].append(op)

    def emit(self, nc, es):
        esem = {e: es.enter_context(nc.semaphore("s_" + e)) for e in ENGS}
        dsem = {ch: es.enter_context(nc.semaphore("d_%s" % str(ch))) for ch in self.dcount}
        for e in ENGS:
            c = 0
            for op in self.by_eng[e]:
                if op.signal and op.dma is None and op.fn is not None:
                    c += 1
                    op.sigval = c
        block = es.enter_context(nc.Block())

        def run(ename, eng):
            waited = {}
            for op in self.by_eng[ename]:
                for d in op.deps:
                    if d.dma is not None:
                        sem, val = dsem[d.dma], d.dval
                    else:
                        sem, val = esem[d.eng], d.sigval
                    if waited.get(id(sem), 0) < val:
                        eng.wait_ge(sem, val)
                        waited[id(sem)] = val
                if op.fn is None:
                    continue
                ins = op.fn(eng)
                if op.dma is not None:
                    ins.then_inc(dsem[op.dma], 16)
                elif op.signal:
                    ins.then_inc(esem[ename], 1)
            if ename == "sp":
                for ch, n in self.dcount.items():
                    eng.wait_ge(dsem[ch], 16 * n)

        block.tensor(lambda t: run("pe", t))
        block.scalar(lambda t: run("act", t))
        block.vector(lambda t: run("dve", t))
        block.gpsimd(lambda t: run("pool", t))
        block.sync(lambda t: run("sp", t))


def vw(base, dims, npart=None):
    p = base.ap[0]
    return bass.AP(base.tensor, base.offset, [[p[0], npart if npart else p[1]]] + [list(d) for d in dims])


def build(stop_after=None):
    nc = bass.Bass("TRN2", target_bir_lowering=False)
    es = contextlib.ExitStack()
    sc = Sched()

    def din(name, shape):
        return nc.dram_tensor(name, list(shape), F32, kind="ExternalInput").ap()

    def dout(name, shape):
        return nc.dram_tensor(name, list(shape), F32, kind="ExternalOutput").ap()

    x_cur = din("x_cur", (T, D)); x_prev = din("x_prev", (T, D)); x_smp = din("x_smp", (TS, D))
    mem = din("mem", (256, D)); ck = din("ck", (TS, 256, D)); cv = din("cv", (TS, 256, D))
    st_in = din("st_in", (TS, 8, 128, 128)); cst_d = din("cst", (128, NCST)); wsT_d = din("wsT", (128, 1024)); ropep_d = din("ropep", (128, 1024)); bbc_d = din("b_bc", (128, 1024))
    w_in = din("w_in", (D, 6144)); w_out = din("w_out", (D, D)); w_cq = din("w_cq", (D, D))
    w_ck = din("w_ck", (D, D)); w_cv = din("w_cv", (D, D)); w_co = din("w_co", (D, D))
    w_ff1 = din("w_ff1", (D, 8192)); w_ff2 = din("w_ff2", (8192, D))
    y_cur = dout("y_cur", (T, D)); y_smp = dout("y_smp", (TS, D)); mk_o = dout("mk_o", (256, D)); mv_o = dout("mv_o", (256, D))
    stp_o = dout("stp_o", (8, 128, 128)); sts_o = dout("sts_o", (TS, 8, 128, 128)); cvs_o = dout("cvs_o", (TS, 1024))

    def sb(name, shape, dt):
        return es.enter_context(nc.sbuf_tensor(name, list(shape), dt))

    H = sb("H", (128, 16, T), F32)
    X = sb("X", (128, 16, T), BF16)
    S = sb("S", (128, 28672), BF16)
    W = sb("W", (128, 3, 4096), BF16)
    CS = sb("CS", (128, NCST), F32)
    wsTb = sb("wsTb", (128, 8, 128), BF16)
    identb = sb("identb", (128, 128), BF16)
    onesb = sb("onesb", (128, 128), BF16)
    ones32 = sb("ones32", (16, 128), F32)
    Hs = sb("Hs", (128, 16, TS), F32)
    Xs = sb("Xs", (128, 16, TS), BF16)
    ccs = sb("ccs", (128, 8, TS), BF16)
    smx = sb("smx", (128, 64), F32)
    M = sb("M", (128, 2560), F32)
    rstb = sb("rstb", (128, 512), F32)
    ONESB[0] = onesb

    def Mv(o, n, dt=F32, npart=128):
        a = M[0:npart, o:o + n]
        return a.bitcast(BF16) if dt == BF16 else a
    P = [es.enter_context(nc.psum_tensor("P%d" % i, [128, 512], F32)) for i in range(8)]

    def cs(name, w, c0=0, npart=128):
        o = CO[name] + c0
        return CS[0:npart, o:o + w]

    ident = cs("ident", 128)

    def Sv(kb0, kb1, dt=BF16):
        a = S[:, kb0 * 512:kb1 * 512]
        return a.bitcast(F32) if dt == F32 else a

    rr = {"main": 0, "misc": 0, "smp": 0}

    def mainbank():
        b = rr["main"] % 4
        rr["main"] += 1
        return b

    def miscbank():
        b = 5 + rr["misc"] % 2
        rr["misc"] += 1
        return b

    def smpreg():
        r_ = rr["smp"] % 2
        rr["smp"] += 1
        return (4, 7)[r_]

    tog = {"n": 0}

    def evac_eng():
        tog["n"] += 1
        return "act" if tog["n"] % 2 else "dve"

    def copy_op(eng, out, in_):
        if eng == "act":
            return lambda e: e.copy(out, in_)
        return lambda e: e.tensor_copy(out, in_)

    wplan = []
    wstate = {"issued": 0, "used": 0}

    def wdeclare(dram, r0, kc, c0, fc):
        wplan.append((dram, r0, kc, c0, fc))

    def wissue_upto(n):
        while wstate["issued"] < min(n, len(wplan)):
            i = wstate["issued"]
            dram, r0, kc, c0, fc = wplan[i]
            slot = i % 3
            dst = W[:, slot, :].rearrange("p (k f) -> p k f", k=kc)
            src = dram[r0:r0 + kc * 128, c0:c0 + fc].rearrange("(k p) f -> p k f", p=128)
            sc.add("pool", lambda e, dst=dst, src=src: e.dma_start(out=dst, in_=src), w=[("W", slot)], dma=("W", slot))
            wstate["issued"] += 1

    def wnext(dram, r0, kc, c0, fc):
        i = wstate["used"]
        assert wplan[i][1:] == (r0, kc, c0, fc) and wplan[i][0].tensor.name == dram.tensor.name, (i, wplan[i][1:], (r0, kc, c0, fc))
        wissue_upto(i + 3)
        wstate["used"] += 1
        slot = i % 3
        return W[:, slot, :].rearrange("p (k f) -> p k f", k=kc), ("W", slot)

    def gemm_b(dram, r0, kc, c0, ncols, xk, epi, xs=None, epis=None, sform="b"):
        fc = 4096 // kc
        for s in range(ncols // fc):
            wv, wkey = wnext(dram, r0, kc, c0 + s * fc, fc)
            for j in range(fc // 128):
                f = s * (fc // 128) + j
                banks = [mainbank(), mainbank()]
                sreg = smpreg() if (xs is not None and sform == "b") else None
                for k in range(kc):
                    lhsT = wv[:, k, j * 128:(j + 1) * 128]
                    for th in range(2):
                        xa, xkeys = xk(k, th)
                        sc.add("pe", lambda e, o=P[banks[th]][:], l=lhsT, r_=xa, st=(k == 0), sp=(k == kc - 1):
                               e.matmul(o, l, r_, start=st, stop=sp), r=[wkey] + xkeys, w=[("P", banks[th])])
                    if sreg is not None:
                        xa, xkeys = xs(k)
                        sc.add("pe", lambda e, o=P[sreg][:, 0:16], l=lhsT, r_=xa, st=(k == 0), sp=(k == kc - 1):
                               e.matmul(o, l, r_, start=st, stop=sp), r=[wkey] + xkeys, w=[("P", sreg)])
                for th in range(2):
                    epi(f, th, P[banks[th]][:], ("P", banks[th]))
                if sreg is not None:
                    epis(f, P[sreg][:, 0:16], ("P", sreg))
            if xs is not None and sform == "a":
                b = miscbank()
                for k in range(kc):
                    xa, xkeys = xs(k)
                    sc.add("pe", lambda e, o=P[b][0:TS, 0:fc], l=xa, r_=wv[:, k, :], st=(k == 0), sp=(k == kc - 1):
                           e.matmul(o, l, r_, start=st, stop=sp), r=[wkey] + xkeys, w=[("P", b)])
                epis(s, P[b][0:TS, 0:fc], ("P", b))

    def gemm_a(dram, r0, kc, c0, ncols, ntiles, xt, epi, xs=None, epis=None):
        fc = 4096 // kc
        for s in range(ncols // fc):
            wv, wkey = wnext(dram, r0, kc, c0 + s * fc, fc)
            for tt in range(ntiles):
                b = mainbank()
                for k in range(kc):
                    xa, xkeys = xt(k, tt)
                    sc.add("pe", lambda e, o=P[b][:, 0:fc], l=xa, r_=wv[:, k, :], st=(k == 0), sp=(k == kc - 1):
                           e.matmul(o, l, r_, start=st, stop=sp), r=[wkey] + xkeys, w=[("P", b)])
                epi(s, tt, P[b][:, 0:fc], ("P", b))
            if xs is not None:
                b = miscbank()
                for k in range(kc):
                    xa, xkeys = xs(k)
                    sc.add("pe", lambda e, o=P[b][0:TS, 0:fc], l=xa, r_=wv[:, k, :], st=(k == 0), sp=(k == kc - 1):
                           e.matmul(o, l, r_, start=st, stop=sp), r=[wkey] + xkeys, w=[("P", b)])
                epis(s, P[b][0:TS, 0:fc], ("P", b))

    def transposes_to(dst_of, src_of, n, dt, srckeys, dstkeys, np_in=128, np_out=128, evac=None):
        grp = 4
        for i0 in range(0, n, grp):
            cnt = min(grp, n - i0)
            b = miscbank()
            if dt == F32:
                pv = P[b][0:np_out, :]
                idn = CS[0:np_in, CO["ident"]:CO["ident"] + np_in]
            else:
                pv = P[b][:].bitcast(BF16)[0:np_out, 0:512]
                idn = identb[0:np_in, 0:np_in]
            for i in range(cnt):
                sc.add("pe", lambda e, o=pv[:, i * np_in:(i + 1) * np_in], s_=src_of(i0 + i), idn=idn: e.transpose(o, s_, idn),
                       r=srckeys(i0 + i), w=[("P", b)])
            dst_of(i0, cnt, pv[:, 0:cnt * np_in], ("P", b))

    sc.add("sp", lambda e: e.dma_start(out=CS[:], in_=cst_d[:, :]), w=["CS"], dma="cst")
    wst32 = Sv(16, 20, F32)
    sc.add("sp", lambda e: e.dma_start(out=wst32, in_=wsT_d[:, :]), w=["wst32"], dma="cst2")
    sc.add("dve", lambda e: e.tensor_copy(identb[:], ident), r=["CS"], w=["identb"])
    sc.add("dve", lambda e: e.memset(onesb[:], 1.0), w=["onesb"])
    sc.add("dve", lambda e: e.memset(ones32[:], 1.0), w=["ones32"])
    sc.add("dve", lambda e: e.tensor_tensor(wsTb[:], wst32.rearrange("p (h i) -> p h i", h=8),
                                            vw(cs("maskT", 128), [[0, 8], [1, 128]]), op=ALU.mult), r=["CS", "wst32"], w=["wsTb"])

    sc.barrier()
    ropep = Sv(32, 36, F32)
    sc.add("sp", lambda e: e.dma_start(out=ropep, in_=ropep_d[:, :]), w=["ropep"], dma="cst3")
    for hp in range(4):
        wdeclare(w_in, 0, 16, 3072 + hp * 256, 256)
        wdeclare(w_in, 0, 16, 4096 + hp * 256, 256)
    for s in range(4):
        wdeclare(w_in, 0, 16, 1024 + s * 256, 256)
    for s in range(4):
        wdeclare(w_in, 0, 16, s * 256, 256)
    for s in range(4):
        wdeclare(w_out, 0, 8, s * 512, 512)
    for hp in range(4):
        for base in (2048, 3072, 4096, 5120):
            wdeclare(w_in, 0, 16, base + hp * 256, 256)
    for s in range(4):
        wdeclare(w_out, 1024, 8, s * 512, 512)
    for wd in (w_cq, w_ck, w_cv, w_co):
        for s in range(8):
            wdeclare(wd, 0, 16, s * 256, 256)
    for g in range(4):
        for s in range(8):
            wdeclare(w_ff1, 0, 16, g * 2048 + s * 256, 256)
        for s in range(8):
            wdeclare(w_ff2, g * 2048, 16, s * 256, 256)
    wissue_upto(2)

    def load_transpose(src, ntok_tiles, rows_per_tile, dst, dkey, stage_kb=(0, 16), gain=None, ssq=None):
        st = Sv(stage_kb[0], stage_kb[1], F32).rearrange("p (s f) -> p s f", s=2)
        for tt in range(ntok_tiles):
            slot = tt % 2
            stv = st[0:rows_per_tile, slot, :]
            sc.add("sp", lambda e, o=stv, i=src[tt * rows_per_tile:(tt + 1) * rows_per_tile, :]: e.dma_start(out=o, in_=i),
                   w=[("xst", slot)], dma=("xst", slot))
            if ssq is not None:
                ssq(tt, stv, ("xst", slot))

            def dst_of(i0, cnt, pv, pkey, tt=tt):
                dst(i0, cnt, tt, pv, pkey)
            transposes_to(dst_of, lambda i, stv=stv: stv[:, i * 128:(i + 1) * 128], 16, F32,
                          lambda i, slot=slot: [("xst", slot), "CS"], None, np_in=rows_per_tile, np_out=128)

    def rmsnorm_fm(Hbuf, hkey, Xbuf, xkey, gi, ncols, nhalf, out32=False):
        for th in range(nhalf):
            cols = slice(th * ncols, (th + 1) * ncols)
            b = miscbank()
            for k in range(16):
                sc.add("act", lambda e, o=Xbuf[:, k, cols], i=Hbuf[:, k, cols]: e.activation(out=o, in_=i, func=AF.Square),
                       r=[hkey(k, th)], w=[xkey(k, th)])
                sc.add("pe", lambda e, o=P[b][:, 0:ncols], r_=Xbuf[:, k, cols], st=(k == 0), sp=(k == 15):
                       e.matmul(o, onesb[:], r_, start=st, stop=sp), r=[xkey(k, th), "onesb"], w=[("P", b)])
            rv = rstb[:, 0:ncols]
            sc.add("act", lambda e, o=rv, i=P[b][:, 0:ncols]: e.activation(out=o, in_=i, func=AF.Sqrt, bias=EPS, scale=1.0 / D),
                   r=[("P", b)], w=[("rst", 0)])
            sc.add("dve", lambda e, o=rv: e.reciprocal(o, o), r=[("rst", 0)], w=[("rst", 0)])
            for k in range(16):
                g = cs("gpk", 1, gi * 16 + k)
                if out32:
                    sc.add("dve", lambda e, o=Hbuf[:, k, cols], g=g, rv=rv: e.scalar_tensor_tensor(o, o, g, rv, op0=ALU.mult, op1=ALU.mult),
                           r=[hkey(k, th), ("rst", 0), "CS"], w=[hkey(k, th)])
                else:
                    sc.add("dve", lambda e, o=Xbuf[:, k, cols], i=Hbuf[:, k, cols], g=g, rv=rv:
                           e.scalar_tensor_tensor(o, i, g, rv, op0=ALU.mult, op1=ALU.mult),
                           r=[hkey(k, th), ("rst", 0), "CS"], w=[xkey(k, th)])

    Hk = lambda k, th: ("H", k, th)
    Xk = lambda k, th: ("X", k, th)
    Hsk = lambda k, th: ("Hs", k)
    Xsk = lambda k, th: ("Xs", k)

    def xk_X(k, th):
        return X[:, k, th * 512:(th + 1) * 512], [("X", k, th)]

    def xt_X(k, tt):
        return X[:, k, tt * 128:(tt + 1) * 128], [("X", k, tt // 4)]

    def xs_Xs(k):
        return Xs[:, k, :], [("Xs", k)]

    def rope_scale(dst_bf, ps, pkey, cosA, sinA, scaleA, nparts, tmpv, tmpkey, outkeys, imm_scale=None, extra=()):
        xs_ = tmpv[0:nparts, 0, :]
        t1 = tmpv[0:nparts, 1, 0:128]
        t2 = tmpv[0:nparts, 2, 0:128]
        if scaleA is not None:
            sc.add("dve", lambda e: e.tensor_tensor(xs_.rearrange("p (h d) -> p h d", h=2), ps.rearrange("p (h d) -> p h d", h=2),
                                                    vw(scaleA, [[1, 2], [0, 128]]), op=ALU.mult), r=[pkey, "CS"], w=[tmpkey])
        else:
            sc.add("act", lambda e: e.activation(out=xs_, in_=ps, func=AF.Copy, scale=(imm_scale or 1.0)), r=[pkey], w=[tmpkey])
        xv = xs_.rearrange("p (h t d) -> p h t d", h=2, t=2)
        dv = dst_bf.rearrange("p (h t d) -> p h t d", h=2, t=2)
        cb = vw(cosA, [[0, 2], [1, 64]])
        sb_ = vw(sinA, [[0, 2], [1, 64]])
        t1v = t1.rearrange("p (h d) -> p h d", h=2)
        t2v = t2.rearrange("p (h d) -> p h d", h=2)
        rk = [tmpkey, "CS"] + list(extra)
        sc.add("dve", lambda e: e.tensor_tensor(t1v, xv[:, :, 0, :], cb, op=ALU.mult), r=rk, w=[tmpkey + ("a",)])
        sc.add("pool", lambda e: e.tensor_tensor(t2v, xv[:, :, 1, :], sb_, op=ALU.mult), r=rk, w=[tmpkey + ("b",)])
        sc.add("dve", lambda e: e.tensor_tensor(dv[:, :, 0, :], t1v, t2v, op=ALU.subtract), r=[tmpkey + ("a",), tmpkey + ("b",)], w=outkeys)
        t3 = tmpv[0:nparts, 1, 128:256].rearrange("p (h d) -> p h d", h=2)
        t4 = tmpv[0:nparts, 2, 128:256].rearrange("p (h d) -> p h d", h=2)
        sc.add("dve", lambda e: e.tensor_tensor(t3, xv[:, :, 0, :], sb_, op=ALU.mult), r=rk, w=[tmpkey + ("c",)])
        sc.add("pool", lambda e: e.tensor_tensor(t4, xv[:, :, 1, :], cb, op=ALU.mult), r=rk, w=[tmpkey + ("d",)])
        sc.add("dve", lambda e: e.tensor_tensor(dv[:, :, 1, :], t3, t4, op=ALU.add), r=[tmpkey + ("c",), tmpkey + ("d",)], w=outkeys)

    ropetmp = Mv(0, 768).rearrange("p (a f) -> p a f", a=3)

    def ropeslot():
        return ropetmp, ("rtmp", 0)

    Ainit = Sv(52, 56, F32).rearrange("p (h e) -> p h e", h=8)
    kp_tok = Sv(16, 20).rearrange("p (t f) -> p t f", t=8)
    vp_tok = Sv(20, 24).rearrange("p (t f) -> p t f", t=8)
    rstd_p = smx[:, 0:8]
    ssq_p = smx[:, 8:16]
    junkp = Sv(24, 32, F32)

    def ssq_prev(tt, stv, skey):
        sc.add("act", lambda e: e.activation(out=junkp, in_=stv, func=AF.Square, accum_out=ssq_p[:, tt:tt + 1]), r=[skey], w=["junkp", ("ssqp", tt)])

    def dst_prev(i0, cnt, tt, pv, pkey):
        eng = evac_eng()
        for i in range(cnt):
            k = i0 + i
            if eng == "act":
                sc.add("act", lambda e, o=X[:, k, tt * 128:(tt + 1) * 128], i_=pv[:, i * 128:(i + 1) * 128], g=cs("gpk", 1, k):
                       e.activation(out=o, in_=i_, func=AF.Copy, scale=g), r=[pkey, "CS"], w=[("X", k, tt // 4)])
            else:
                sc.add("dve", lambda e, o=X[:, k, tt * 128:(tt + 1) * 128], i_=pv[:, i * 128:(i + 1) * 128], g=cs("gpk", 1, k):
                       e.tensor_scalar(o, i_, g, None, op0=ALU.mult), r=[pkey, "CS"], w=[("X", k, tt // 4)])

    sc.add("dve", lambda e: e.memset(ssq_p, 0.0), w=[("ssqp", t_) for t_ in range(8)])
    load_transpose(x_prev, NT, 128, dst_prev, None, ssq=ssq_prev)
    sc.add("act", lambda e: e.activation(out=rstd_p, in_=ssq_p, func=AF.Sqrt, bias=EPS, scale=1.0 / D), r=[("ssqp", t_) for t_ in range(8)], w=["rstdp"])
    sc.add("dve", lambda e: e.reciprocal(rstd_p, rstd_p), r=["rstdp"], w=["rstdp"])

    for hp in range(4):
        def epi_kp2(s, tt, ps, pkey, hp=hp):
            tv, tk = ropeslot()
            scl = smx[:, 16 + 2 * (tt % 2):18 + 2 * (tt % 2)]
            sc.add("dve", lambda e: e.tensor_scalar(scl, cs("ks", 2, tt * 8 + hp * 2), rstd_p[:, tt:tt + 1], None, op0=ALU.mult),
                   r=["CS", "rstdp"], w=[tk])
            rope_scale(kp_tok[:, tt, :], ps, pkey, ropep[:, tt * 64:(tt + 1) * 64], ropep[:, 512 + tt * 64:512 + (tt + 1) * 64], scl, 128, tv, tk, [("kp", tt)], extra=["ropep"])

        def epi_vp(s, tt, ps, pkey):
            sc.add("act", lambda e: e.activation(out=vp_tok[:, tt, :], in_=ps, func=AF.Copy, scale=rstd_p[:, tt:tt + 1]), r=[pkey, "rstdp"], w=[("vp", tt)])

        gemm_a(w_in, 0, 16, 3072 + hp * 256, 256, NT, xt_X, epi_kp2)
        gemm_a(w_in, 0, 16, 4096 + hp * 256, 256, NT, xt_X, epi_vp)
        b = miscbank()
        for hh in range(2):
            for tt in range(NT):
                sc.add("pe", lambda e, o=P[b][:, hh * 128:(hh + 1) * 128], l=kp_tok[:, tt, hh * 128:(hh + 1) * 128], r_=vp_tok[:, tt, hh * 128:(hh + 1) * 128],
                       st=(tt == 0), sp=(tt == NT - 1): e.matmul(o, l, r_, start=st, stop=sp), r=[("kp", tt), ("vp", tt)], w=[("P", b)])
        for hh in range(2):
            h = hp * 2 + hh
            sc.add("dve", lambda e, o=Ainit[:, h, :], i=P[b][:, hh * 128:(hh + 1) * 128], g=cs("gini", 1, h):
                   e.tensor_scalar(o, i, g, None, op0=ALU.mult), r=[("P", b), "CS"], w=[("Ainit", h)])
    sc.barrier()
    if stop_after == "prefix":
        return finish(nc, es, sc, H, Hs, y_cur, y_smp, S, Sv, P, CS, transposes_to, load_transpose, rmsnorm_fm, debug=True)

    def dst_cur(i0, cnt, tt, pv, pkey):
        eng = evac_eng()
        o = H[:, i0:i0 + cnt, tt * 128:(tt + 1) * 128]
        i_ = pv.rearrange("p (c t) -> p c t", c=cnt)
        sc.add(eng, copy_op(eng, o, i_), r=[pkey], w=[("H", k, tt // 4) for k in range(i0, i0 + cnt)])

    load_transpose(x_cur, NT, 128, dst_cur, None)

    def dst_smp(i0, cnt, tt, pv, pkey):
        o = Hs[:, i0:i0 + cnt, :]
        i_ = pv.rearrange("p (c t) -> p c t", c=cnt)
        sc.add("dve", copy_op("dve", o, i_), r=[pkey], w=[("Hs", k) for k in range(i0, i0 + cnt)])

    load_transpose(x_smp, 1, TS, dst_smp, None)
    rmsnorm_fm(H, Hk, X, Xk, 0, 512, 2)
    rmsnorm_fm(Hs, Hsk, Xs, Xsk, 0, TS, 1)
    sc.barrier()
    if stop_after == "norm1":
        return finish(nc, es, sc, H, Hs, y_cur, y_smp, S, Sv, P, CS, transposes_to, load_transpose, rmsnorm_fm, debug=True)

    cc = Sv(0, 16).rearrange("p (j t) -> p j t", j=8)
    vrows = Sv(16, 32).rearrange("p (t f) -> p t f", t=8)
    g32 = Sv(36, 38, F32).rearrange("p (s f) -> p s f", s=2)
    ug = Sv(38, 42, F32).rearrange("p (s f) -> p s f", s=2)
    t32 = Sv(42, 44, F32)
    vs32 = Sv(32, 36, F32)
    bst = Mv(768, 192).rearrange("p (t s c) -> p t s c", t=8, s=4)
    bsts = Mv(1216, 24, npart=TS).rearrange("p (s c) -> p s c", s=4)
    ugs = Mv(960, 128).rearrange("p (h t) -> p h t", h=8)
    vgs = Mv(1088, 128).rearrange("p (h t) -> p h t", h=8)
    b_bc = Sv(48, 52, F32)
    sc.add("sp", lambda e: e.dma_start(out=b_bc, in_=bbc_d[:, :]), w=["b_bc"], dma="cst4")

    gtog = {"n": 0}

    def epi_vA(s, tt, ps, pkey):
        gtog["n"] += 1
        sl = gtog["n"] % 2
        sc.add("act", lambda e: e.activation(out=g32[:, sl, :], in_=ps, func=AF.Gelu_apprx_tanh), r=[pkey], w=[("g32", sl)])
        sc.add("dve", lambda e: e.bn_stats(bst[:, tt, s, :], g32[:, sl, :]), r=[("g32", sl)], w=[("bst", tt)])
        sc.add("pool", lambda e: e.tensor_copy(vrows[:, tt, s * 256:(s + 1) * 256], g32[:, sl, :]), r=[("g32", sl)], w=[("vrows", tt)])

    def epis_vA(s, ps, pkey):
        sc.add("act", lambda e: e.activation(out=vs32[0:TS, s * 256:(s + 1) * 256], in_=ps, func=AF.Gelu_apprx_tanh), r=[pkey], w=["vs32"])
        sc.add("dve", lambda e: e.bn_stats(bsts[:, s, :], vs32[0:TS, s * 256:(s + 1) * 256]), r=["vs32"], w=["bsts"])

    gemm_a(w_in, 0, 16, 1024, 1024, NT, xt_X, epi_vA, xs=xs_Xs, epis=epis_vA)

    def ln_finish(stats, np_, data_in, data_out, keys_r, keys_w):
        mv = smx[0:np_, 0:2]
        sc.add("dve", lambda e: e.bn_aggr(mv, stats), r=keys_r, w=["mvA"])
        sc.add("act", lambda e: e.activation(out=mv[:, 1:2], in_=mv[:, 1:2], func=AF.Sqrt, bias=EPS, scale=1.0), r=["mvA"], w=["mvA"])
        sc.add("dve", lambda e: e.reciprocal(mv[:, 1:2], mv[:, 1:2]), r=["mvA"], w=["mvA"])
        sc.add("dve", lambda e: e.tensor_scalar(data_out, data_in, mv[:, 0:1], mv[:, 1:2], op0=ALU.subtract, op1=ALU.mult),
               r=["mvA"] + keys_r, w=keys_w)

    for tt in range(NT):
        ln_finish(bst[:, tt, :, :].rearrange("p s c -> p (s c)"), 128, vrows[:, tt, :], vrows[:, tt, :], [("bst", tt), ("vrows", tt)], [("vrows", tt)])
    ln_finish(bsts.rearrange("p s c -> p (s c)"), TS, vs32[0:TS, :], vs32[0:TS, :], ["bsts", "vs32"], ["vs32"])
    def dst_vgs(i0, cnt, pv, pkey):
        for i in range(cnt):
            h = i0 + i
            sc.add("dve", lambda e, o=vgs[:, h, :], i_=pv[:, i * TS:(i + 1) * TS], g=cs("gA", 1, h): e.tensor_scalar(o, i_, g, None, op0=ALU.mult),
                   r=[pkey, "CS"], w=[("vgs", h)])
    transposes_to(dst_vgs, lambda i: vs32[0:TS, i * 128:(i + 1) * 128], 8, F32, lambda i: ["vs32", "CS"], None, np_in=TS, np_out=128)
    cvst = Sv(44, 48, F32)
    def dst_cvs(i0, cnt, pv, pkey):
        sc.add("act", lambda e: e.copy(cvst[0:TS, i0 * 128:(i0 + cnt) * 128], pv), r=[pkey], w=["cvst"])
    transposes_to(dst_cvs, lambda i: vgs[:, i, :], 8, F32, lambda i: [("vgs", i), "CS"], None, np_in=128, np_out=TS)
    sc.add("sp", lambda e: e.dma_start(out=cvs_o[:, :], in_=cvst[0:TS, :]), r=["cvst"], dma="out_cvs")

    utog = {"n": 0}

    def epi_u(f, th, ps, pkey):
        utog["n"] += 1
        sl = utog["n"] % 2
        sc.add("act", lambda e: e.activation(out=ug[:, sl, :], in_=ps, func=AF.Gelu_apprx_tanh), r=[pkey], w=[("ug", sl)])
        b = miscbank()
        for n in range(4):
            tt = th * 4 + n
            sc.add("pe", lambda e, o=P[b][:, n * 128:(n + 1) * 128], l=vrows[:, tt, f * 128:(f + 1) * 128], r_=wsTb[:, f, :]:
                   e.matmul(o, l, r_, start=True, stop=True), r=[("vrows", tt), "wsTb"], w=[("P", b)])
        sc.add("dve", lambda e: e.scalar_tensor_tensor(t32.rearrange("p (n i) -> p n i", n=4), P[b][:].rearrange("p (n i) -> p n i", n=4), cs("gA", 1, f),
                                                       vw(b_bc[:, f * 128:(f + 1) * 128], [[0, 4], [1, 128]]), op0=ALU.mult, op1=ALU.add),
               r=[("P", b), "CS", "b_bc"], w=["t32"])
        sc.add("dve", lambda e: e.tensor_tensor(cc[:, f, th * 512:(th + 1) * 512], t32, ug[:, sl, :], op=ALU.mult), r=["t32", ("ug", sl)], w=[("cc", f, th)])

    def epis_u(f, ps, pkey):
        sc.add("act", lambda e: e.activation(out=ugs[:, f, :], in_=ps, func=AF.Gelu_apprx_tanh), r=[pkey], w=[("ugs", f)])
        tmp = smx[:, 32:48]
        sc.add("dve", lambda e: e.tensor_scalar(tmp, vgs[:, f, :], cs("ws00", 1, f), cs("b0", 1, f), op0=ALU.mult, op1=ALU.add),
               r=[("vgs", f), "CS"], w=["tmps"])
        sc.add("dve", lambda e: e.tensor_tensor(ccs[:, f, :], tmp, ugs[:, f, :], op=ALU.mult), r=["tmps", ("ugs", f)], w=[("ccs", f)])

    gemm_b(w_in, 0, 16, 0, 1024, xk_X, epi_u, xs=xs_Xs, epis=epis_u)

    def xk_cc(k, th):
        return cc[:, k, th * 512:(th + 1) * 512], [("cc", k, th)]

    def xs_ccs(k):
        return ccs[:, k, :], [("ccs", k)]

    def epi_res(f, th, ps, pkey):
        eng = "dve"
        o = H[:, f, th * 512:(th + 1) * 512]
        sc.add(eng, lambda e: e.tensor_tensor(o, o, ps, op=ALU.add), r=[pkey, ("H", f, th)], w=[("H", f, th)])

    def epis_res(f, ps, pkey):
        o = Hs[:, f, :]
        sc.add("dve", lambda e: e.tensor_tensor(o, o, ps, op=ALU.add), r=[pkey, ("Hs", f)], w=[("Hs", f)])

    gemm_b(w_out, 0, 8, 0, 2048, xk_cc, epi_res, xs=xs_ccs, epis=epis_res)
    sc.barrier()
    if stop_after == "mixA":
        return finish(nc, es, sc, H, Hs, y_cur, y_smp, S, Sv, P, CS, transposes_to, load_transpose, rmsnorm_fm, debug=True)

    qT = Sv(16, 20).rearrange("p (h t) -> p h t", h=2)
    kT = Sv(20, 24).rearrange("p (h t) -> p h t", h=2)
    ktok = Sv(24, 28).rearrange("p (t f) -> p t f", t=8)
    vtok = Sv(28, 32).rearrange("p (t f) -> p t f", t=8)
    gT = Sv(32, 36).rearrange("p (h t) -> p h t", h=2)
    ks_s = Sv(36, 40, F32)
    vs_s = Sv(40, 44, F32)
    stsl = Sv(44, 52, F32).rearrange("p (s h e) -> p s h e", s=2, h=8)
    qtok_t = Mv(768, 256, BF16).rearrange("p (s f) -> p s f", s=2)
    qs_s = Sv(48, 52, F32)[0:TS, :]
    qTs = Mv(1920, 128).rearrange("p (h t) -> p h t", h=8)
    gTs = Mv(2048, 128).rearrange("p (h t) -> p h t", h=8)
    Aacc = Mv(1536, 256).rearrange("p (h e) -> p h e", h=2)
    Abf = Mv(1792, 128, BF16).rearrange("p (h e) -> p h e", h=2)
    scm = Mv(1024, 256, BF16).rearrange("p (s f) -> p s f", s=2)
    retn = Mv(1280, 256, BF16).rearrange("p (s f) -> p s f", s=2)
    rstat = Mv(2176, 12).rearrange("p (h c) -> p h c", h=2)
    rmv = Mv(2188, 4).rearrange("p (h c) -> p h c", h=2)
    qtt = {"n": 0}

    for hp in range(4):
        def epi_q(s, tt, ps, pkey, hp=hp, which="q"):
            tv, tk = ropeslot()
            qtt["n"] += 1
            sl = qtt["n"] % 2
            if which == "q":
                dstb, dkey, scl = qtok_t[:, sl, :], ("qtokt", sl), cs("qs", 2, tt * 8 + hp * 2)
            else:
                dstb, dkey, scl = ktok[:, tt, :], ("ktok", tt), cs("ks", 2, tt * 8 + hp * 2)
            rope_scale(dstb, ps, pkey, cs("cosc", 64, tt * 64), cs("sinc", 64, tt * 64), scl, 128, tv, tk, [dkey])
            dT = qT if which == "q" else kT
            nm = "qT" if which == "q" else "kT"

            def dst_t(i0, cnt, pv, pk2):
                sc.add("act", lambda e: e.copy(dT[:, :, tt * 128:(tt + 1) * 128], pv.rearrange("p (h t) -> p h t", h=2)), r=[pk2], w=[(nm, tt)])
            transposes_to(dst_t, lambda i: dstb[:, i * 128:(i + 1) * 128], 2, BF16, lambda i: [dkey, "identb"], None)

        def epis_q(s, ps, pkey, hp=hp, which="q"):
            tv, tk = ropeslot()
            dsts = (qs_s if which == "q" else ks_s)[0:TS, hp * 256:(hp + 1) * 256]
            rope_scale(dsts, ps, pkey, cs("coss", 64, 0, TS), cs("sins", 64, 0, TS), None, TS, tv, tk, [("qs_s" if which == "q" else "ks_s", hp)],
                       imm_scale=(1.0 if which == "q" else 128.0 ** -0.5))

        gemm_a(w_in, 0, 16, 2048 + hp * 256, 256, NT, xt_X, epi_q, xs=xs_Xs, epis=epis_q)
        gemm_a(w_in, 0, 16, 3072 + hp * 256, 256, NT, xt_X, lambda s, tt, ps, pkey, hp=hp: epi_q(s, tt, ps, pkey, hp, "k"),
               xs=xs_Xs, epis=lambda s, ps, pkey, hp=hp: epis_q(s, ps, pkey, hp, "k"))

        def epi_v(s, tt, ps, pkey):
            eng = evac_eng()
            sc.add(eng, copy_op(eng, vtok[:, tt, :], ps), r=[pkey], w=[("vtok", tt)])

        def epis_v(s, ps, pkey, hp=hp):
            sc.add("act", lambda e: e.copy(vs_s[0:TS, hp * 256:(hp + 1) * 256], ps), r=[pkey], w=[("vs_s", hp)])

        gemm_a(w_in, 0, 16, 4096 + hp * 256, 256, NT, xt_X, epi_v, xs=xs_Xs, epis=epis_v)

        def epi_g(f, th, ps, pkey):
            sc.add("act", lambda e: e.activation(out=gT[:, f, th * 512:(th + 1) * 512], in_=ps, func=AF.Silu), r=[pkey], w=[("gT", f, th)])

        def epis_g(f, ps, pkey, hp=hp):
            sc.add("act", lambda e: e.activation(out=gTs[:, hp * 2 + f, :], in_=ps, func=AF.Silu), r=[pkey], w=[("gTs", hp * 2 + f)])

        gemm_b(w_in, 0, 16, 5120 + hp * 256, 256, xk_X, epi_g, xs=xs_Xs, epis=epis_g)

        for hh in range(2):
            h = hp * 2 + hh
            sc.add("dve", lambda e, hh=hh, h=h: e.tensor_copy(Aacc[:, hh, :], Ainit[:, h, :]), r=[("Ainit", h)], w=["Aacc"])
        sc.add("act", lambda e: e.copy(Mv(1792, 128, BF16), Mv(1536, 256)), r=["Aacc"], w=["Abf"])
        for n in range(NT):
            b1 = miscbank()
            for hh in range(2):
                sc.add("pe", lambda e, hh=hh: e.matmul(P[b1][:, hh * 128:(hh + 1) * 128], kT[:, hh, n * 128:(n + 1) * 128], qT[:, hh, n * 128:(n + 1) * 128],
                                                      start=True, stop=True), r=[("kT", n), ("qT", n)], w=[("P", b1)])
            sl = n % 2
            sc.add("dve", lambda e, sl=sl: e.tensor_tensor(scm[:, sl, :], P[b1][:, 0:256], cs("maskT", 256), op=ALU.mult), r=[("P", b1), "CS"], w=[("scm", sl)])
            b2 = miscbank()
            for hh in range(2):
                sc.add("pe", lambda e, hh=hh, sl=sl: e.matmul(P[b2][:, hh * 128:(hh + 1) * 128], scm[:, sl, hh * 128:(hh + 1) * 128], vtok[:, n, hh * 128:(hh + 1) * 128],
                                                             start=True, stop=False), r=[("scm", sl), ("vtok", n)], w=[("P", b2)])
                sc.add("pe", lambda e, hh=hh: e.matmul(P[b2][:, hh * 128:(hh + 1) * 128], qT[:, hh, n * 128:(n + 1) * 128], Abf[:, hh, :],
                                                      start=False, stop=True), r=[("qT", n), "Abf"], w=[("P", b2)])
            b3 = miscbank()
            for hh in range(2):
                sc.add("pe", lambda e, hh=hh: e.matmul(P[b3][:, hh * 128:(hh + 1) * 128], ktok[:, n, hh * 128:(hh + 1) * 128], vtok[:, n, hh * 128:(hh + 1) * 128],
                                                      start=True, stop=True), r=[("ktok", n), ("vtok", n)], w=[("P", b3)])
            sc.add("dve", lambda e: e.tensor_tensor(Mv(1536, 256), Mv(1536, 256), P[b3][:, 0:256], op=ALU.add),
                   r=[("P", b3), "Aacc"], w=["Aacc"])
            sc.add("act", lambda e: e.copy(Mv(1792, 128, BF16), Mv(1536, 256)), r=["Aacc"], w=["Abf"])
            for hh in range(2):
                sc.add("dve", lambda e, hh=hh: e.bn_stats(rstat[:, hh, :], P[b2][:, hh * 128:(hh + 1) * 128]), r=[("P", b2)], w=[("rstat", hh)])
                sc.add("dve", lambda e, hh=hh: e.bn_aggr(rmv[:, hh, :], rstat[:, hh, :]), r=[("rstat", hh)], w=[("rmv", hh)])
            sc.add("act", lambda e: e.activation(out=rmv[:, :, 1:2], in_=rmv[:, :, 1:2], func=AF.Sqrt, bias=EPS, scale=1.0), r=[("rmv", 0), ("rmv", 1)], w=[("rmv", 0), ("rmv", 1)])
            sc.add("dve", lambda e: e.reciprocal(rmv[:, :, 1:2], rmv[:, :, 1:2]), r=[("rmv", 0), ("rmv", 1)], w=[("rmv", 0), ("rmv", 1)])
            for hh in range(2):
                sc.add("dve", lambda e, hh=hh, sl=sl: e.tensor_scalar(retn[:, sl, hh * 128:(hh + 1) * 128], P[b2][:, hh * 128:(hh + 1) * 128], rmv[:, hh, 0:1], rmv[:, hh, 1:2],
                                                                     op0=ALU.subtract, op1=ALU.mult), r=[("P", b2), ("rmv", hh)], w=[("retn", sl)])

            def dst_r(i0, cnt, pv, pk2, n=n, hp=hp):
                for hh in range(2):
                    h = hp * 2 + hh
                    sc.add("dve", lambda e, hh=hh, h=h: e.scalar_tensor_tensor(cc[:, h, n * 128:(n + 1) * 128], pv[:, hh * 128:(hh + 1) * 128], cs("gnB", 1, h),
                                                                               gT[:, hh, n * 128:(n + 1) * 128], op0=ALU.mult, op1=ALU.mult),
                           r=[pk2, "CS", ("gT", hh, n // 4)], w=[("cc", h, n // 4)])
            transposes_to(dst_r, lambda i, sl=sl: retn[:, sl, i * 128:(i + 1) * 128], 2, BF16, lambda i, sl=sl: [("retn", sl), "identb"], None)
        stst = Sv(44, 45, F32).rearrange("p (h e) -> p h e", h=2)
        for hh in range(2):
            h = hp * 2 + hh
            sc.add("dve", lambda e, hh=hh, h=h: e.tensor_scalar(stst[:, hh, :], Aacc[:, hh, :], cs("gfin", 1, h), None, op0=ALU.mult), r=["Aacc", "CS"], w=[("stst", hh)])
        sc.add("sp", lambda e, hp=hp: e.dma_start(out=stp_o[hp * 2:hp * 2 + 2].rearrange("h d e -> d h e"), in_=stst), r=[("stst", 0), ("stst", 1)], dma="out_stp")

    sc.barrier()
    def dst_qTs(i0, cnt, pv, pkey):
        sc.add("dve", lambda e: e.tensor_copy(qTs[:, i0:i0 + cnt, :], pv.rearrange("p (c t) -> p c t", c=cnt)), r=[pkey], w=[("qTs", i0 // 4)])
    transposes_to(dst_qTs, lambda i: qs_s[0:TS, i * 128:(i + 1) * 128], 8, F32, lambda i: [("qs_s", i // 2), "CS"], None, np_in=TS, np_out=128)
    Qsel = Sv(16, 24, F32).rearrange("p (h s c) -> p h s c", h=8, s=TS)
    sc.add("dve", lambda e: e.tensor_tensor(Qsel, vw(qTs[:, 0, 0:1], [[TS, 8], [1, TS], [0, TS]]),
                                            vw(cs("eyeq", 256), [[0, 8], [TS, TS], [1, TS]]), op=ALU.mult),
           r=[("qTs", 0), ("qTs", 1), "CS"], w=["Qsel"])
    qk = Mv(2192, 8, npart=TS)
    qkj = Sv(24, 28, F32)
    sc.add("dve", lambda e: e.tensor_tensor(qkj[0:TS, :], qs_s, ks_s[0:TS, :], op=ALU.mult), r=[("qs_s", i) for i in range(4)] + [("ks_s", i) for i in range(4)], w=["qkj"])
    sc.add("dve", lambda e: e.tensor_reduce(qk, qkj[0:TS, :].rearrange("p (h d) -> p h d", h=8), axis=AX.X, op=ALU.add), r=["qkj"], w=["qk"])
    rs_tok = Sv(28, 32, F32)
    sc.barrier()
    pcross = 5; pc2 = 6
    cracc = Sv(24, 28, F32)[0:TS, :]
    sc.add("dve", lambda e: e.memset(cracc, 0.0), w=["cracc"])
    Ksel = Sv(32, 36, F32)[0:TS, :]
    for s_ in range(TS):
        sl = s_ % 2
        sc.add("sp", lambda e, s_=s_, sl=sl: e.dma_start(out=stsl[:, sl], in_=st_in[s_].rearrange("h d e -> d h e")), w=[("stsl", sl)], dma=("stin", sl))
        sc.add("dve", lambda e, s_=s_, sl=sl: e.tensor_scalar(Ksel, ks_s[0:TS, :], cs("eye16", 1, s_, TS), None, op0=ALU.mult),
               r=[("ks_s", i) for i in range(4)] + ["CS"], w=["Ksel"])
        for h in range(8):
            pb = pcross if h < 4 else pc2
            sc.add("pe", lambda e, s_=s_, sl=sl, h=h, pb=pb: e.matmul(P[pb][0:TS, (h % 4) * 128:(h % 4 + 1) * 128], Qsel[:, h, s_, :], stsl[:, sl, h, :],
                                                                     start=True, stop=True), r=["Qsel", ("stsl", sl)], w=[("P", pb)])
        for half in range(2):
            pb = (pcross, pc2)[half]
            sc.add("dve", lambda e, half=half, pb=pb: e.tensor_tensor(cracc[:, half * 512:(half + 1) * 512], cracc[:, half * 512:(half + 1) * 512], P[pb][0:TS, :], op=ALU.add),
                   r=[("P", pb), "cracc"], w=["cracc"])
        bo1 = 0; bo2 = 1
        for h in range(8):
            bo = bo1 if h < 4 else bo2
            sc.add("pe", lambda e, sl=sl, h=h, bo=bo: e.matmul(P[bo][:, (h % 4) * 128:(h % 4 + 1) * 128], Ksel[:, h * 128:(h + 1) * 128], vs_s[0:TS, h * 128:(h + 1) * 128],
                                                              start=True, stop=True), r=["Ksel"] + [("vs_s", i) for i in range(4)], w=[("P", bo)])
        for h in range(8):
            bo = bo1 if h < 4 else bo2
            sc.add("dve", lambda e, sl=sl, h=h, bo=bo: e.scalar_tensor_tensor(stsl[:, sl, h, :], stsl[:, sl, h, :], GAM[h], P[bo][:, (h % 4) * 128:(h % 4 + 1) * 128],
                                                                             op0=ALU.mult, op1=ALU.add), r=[("P", bo), ("stsl", sl)], w=[("stsl", sl)])
        sc.add("sp", lambda e, s_=s_, sl=sl: e.dma_start(out=sts_o[s_].rearrange("h d e -> d h e"), in_=stsl[:, sl]), r=[("stsl", sl)], dma=("stout", sl))
    sc.add("dve", lambda e: e.tensor_tensor(rs_tok[0:TS, :].rearrange("p (h e) -> p h e", h=8), vs_s[0:TS, :].rearrange("p (h e) -> p h e", h=8),
                                            vw(qk[:, 0:1], [[1, 8], [0, 128]]), op=ALU.mult), r=["qk"] + [("vs_s", i) for i in range(4)], w=["rs_tok"])
    for h in range(8):
        sc.add("dve", lambda e, h=h: e.scalar_tensor_tensor(rs_tok[0:TS, h * 128:(h + 1) * 128], cracc[:, h * 128:(h + 1) * 128], GAM[h],
                                                            rs_tok[0:TS, h * 128:(h + 1) * 128], op0=ALU.mult, op1=ALU.add), r=["cracc", "rs_tok"], w=["rs_tok"])
    rst_s = Mv(2200, 48, npart=TS).rearrange("p (h c) -> p h c", h=8)
    rmv_s = Mv(2248, 16, npart=TS).rearrange("p (h c) -> p h c", h=8)
    for h in range(8):
        sc.add("dve", lambda e, h=h: e.bn_stats(rst_s[:, h, :], rs_tok[0:TS, h * 128:(h + 1) * 128]), r=["rs_tok"], w=["rst_s"])
        sc.add("dve", lambda e, h=h: e.bn_aggr(rmv_s[:, h, :], rst_s[:, h, :]), r=["rst_s"], w=["rmv_s"])
    sc.add("act", lambda e: e.activation(out=rmv_s[:, :, 1:2], in_=rmv_s[:, :, 1:2], func=AF.Sqrt, bias=EPS, scale=1.0), r=["rmv_s"], w=["rmv_s"])
    sc.add("dve", lambda e: e.reciprocal(rmv_s[:, :, 1:2], rmv_s[:, :, 1:2]), r=["rmv_s"], w=["rmv_s"])
    for h in range(8):
        sc.add("dve", lambda e, h=h: e.tensor_scalar(rs_tok[0:TS, h * 128:(h + 1) * 128], rs_tok[0:TS, h * 128:(h + 1) * 128], rmv_s[:, h, 0:1], rmv_s[:, h, 1:2],
                                                     op0=ALU.subtract, op1=ALU.mult), r=["rmv_s", "rs_tok"], w=["rs_tok"])

    def dst_rs(i0, cnt, pv, pkey):
        for i in range(cnt):
            h = i0 + i
            sc.add("dve", lambda e, h=h, i=i: e.scalar_tensor_tensor(ccs[:, h, :], pv[:, i * TS:(i + 1) * TS], cs("gnB", 1, h), gTs[:, h, :], op0=ALU.mult, op1=ALU.mult),
                   r=[pkey, "CS", ("gTs", h)], w=[("ccs", h)])
    transposes_to(dst_rs, lambda i: rs_tok[0:TS, i * 128:(i + 1) * 128], 8, F32, lambda i: ["rs_tok", "CS"], None, np_in=TS, np_out=128)

    gemm_b(w_out, 1024, 8, 0, 2048, xk_cc, epi_res, xs=xs_ccs, epis=epis_res)
    sc.barrier()
    if stop_after == "mixB":
        return finish(nc, es, sc, H, Hs, y_cur, y_smp, S, Sv, P, CS, transposes_to, load_transpose, rmsnorm_fm, debug=True)

    rmsnorm_fm(H, Hk, X, Xk, 1, 512, 2)
    rmsnorm_fm(Hs, Hsk, Xs, Xsk, 1, TS, 1)
    Q = Sv(0, 32).rearrange("p (k t) -> p k t", k=16)
    qtok_s = Sv(48, 56, F32)[0:TS, :]

    def epi_cq(f, th, ps, pkey):
        eng = evac_eng()
        sc.add(eng, copy_op(eng, Q[:, f, th * 512:(th + 1) * 512], ps), r=[pkey], w=[("Q", f, th)])

    def epis_cq(s, ps, pkey):
        sc.add("act", lambda e: e.copy(qtok_s[:, s * 256:(s + 1) * 256], ps), r=[pkey], w=[("qtok_s", s // 2)])

    gemm_b(w_cq, 0, 16, 0, 2048, xk_X, epi_cq, xs=xs_Xs, epis=epis_cq, sform="a")
    sc.barrier()
    if stop_after == "xq":
        return finish(nc, es, sc, H, Hs, y_cur, y_smp, S, Sv, P, CS, transposes_to, load_transpose, rmsnorm_fm, debug=True)
    Xf = X[:].rearrange("p k t -> p (k t)")
    mH = Xf[:, 0:8192].bitcast(F32).rearrange("p (k t) -> p k t", k=16)
    mnT = Xf[:, 8192:12288].rearrange("p (k t) -> p k t", k=16)
    mkT = Xf[:, 0:4096].rearrange("p (k t) -> p k t", k=16)
    mvb = Xf[:, 4096:8192].rearrange("p (m f) -> p m f", m=2)
    pT = Xf[:, 12288:14336].rearrange("p (m t) -> p m t", m=2)
    mkst = Xf[:, 14336:16384].bitcast(F32).rearrange("p (s f) -> p s f", s=4)
    pbuf = Mv(0, 512).rearrange("p (s f) -> p s f", s=2)
    pbb = Mv(512, 256, BF16).rearrange("p (s f) -> p s f", s=2)

    def dst_mem(i0, cnt, tt, pv, pkey):
        eng = evac_eng()
        sc.add(eng, copy_op(eng, mH[:, i0:i0 + cnt, tt * 128:(tt + 1) * 128], pv.rearrange("p (c t) -> p c t", c=cnt)), r=[pkey], w=[("mH", k) for k in range(i0, i0 + cnt)])
    load_transpose(mem, 2, 128, dst_mem, None, stage_kb=(32, 48))
    rmsnorm_fm(mH, lambda k, th: ("mH", k), mnT, lambda k, th: ("mnT", k), 2, 256, 1)
    sc.barrier()
    mtog = {"n": 0}

    def xt_mn(k, tt):
        return mnT[:, k, tt * 128:(tt + 1) * 128], [("mnT", k)]

    def epi_mk(s, tt, ps, pkey, which="k"):
        mtog["n"] += 1
        sl = mtog["n"] % 4
        sc.add("act", lambda e: e.copy(mkst[:, sl, :], ps), r=[pkey], w=[("mkst", sl)])
        dsto = (mk_o if which == "k" else mv_o)[tt * 128:(tt + 1) * 128, s * 256:(s + 1) * 256]
        sc.add("sp", lambda e: e.dma_start(out=dsto, in_=mkst[:, sl, :]), r=[("mkst", sl)], dma=("mko", sl))
        if which == "v":
            sc.add("dve", lambda e: e.tensor_copy(mvb[:, tt, s * 256:(s + 1) * 256], mkst[:, sl, :]), r=[("mkst", sl)], w=[("mvb", tt, s)])
        else:
            sc.add("dve", lambda e: e.tensor_copy(pbb[:, tt, :], mkst[:, sl, :]), r=[("mkst", sl)], w=[("pbb", tt)])

            def dst_k(i0, cnt, pv, pk2):
                sc.add("act", lambda e: e.copy(mkT[:, s * 2:s * 2 + 2, tt * 128:(tt + 1) * 128], pv.rearrange("p (c t) -> p c t", c=2)), r=[pk2], w=[("mkT", s, tt)])
            transposes_to(dst_k, lambda i: pbb[:, tt, i * 128:(i + 1) * 128], 2, BF16, lambda i: [("pbb", tt), "identb"], None)

    gemm_a(w_ck, 0, 16, 0, 2048, 2, xt_mn, epi_mk)
    gemm_a(w_cv, 0, 16, 0, 2048, 2, xt_mn, lambda s, tt, ps, pkey: epi_mk(s, tt, ps, pkey, "v"))

    if stop_after == "xkv":
        sc.barrier()
        return finish(nc, es, sc, H, Hs, y_cur, y_smp, S, Sv, P, CS, transposes_to, load_transpose, rmsnorm_fm, debug=True)
    SCL = 512.0 ** -0.5
    amx = Mv(2368, 4)
    for h in range(4):
        for tt in range(NT):
            b = miscbank()
            for dc in range(4):
                kq = 4 * h + dc
                sc.add("pe", lambda e, kq=kq, dc=dc: e.matmul(P[b][:, 0:256], Q[:, kq, tt * 128:(tt + 1) * 128], mkT[:, kq, :], start=(dc == 0), stop=(dc == 3)),
                       r=[("Q", kq, tt // 4)] + [("mkT", kq // 2, m_) for m_ in range(2)], w=[("P", b)])
            sl = tt % 2
            sc.add("dve", lambda e: e.tensor_reduce(amx[:, 0:1], P[b][:, 0:256], axis=AX.X, op=ALU.max), r=[("P", b)], w=["amx0"])
            sc.add("dve", lambda e: e.tensor_scalar(amx[:, 1:2], amx[:, 0:1], -SCL, None, op0=ALU.mult), r=["amx0"], w=["amx1"])
            sc.add("act", lambda e, sl=sl: e.activation(out=pbuf[:, sl, :], in_=P[b][:, 0:256], func=AF.Exp, bias=amx[:, 1:2], scale=SCL, accum_out=amx[:, 2:3]),
                   r=[("P", b), "amx1"], w=[("pbuf", sl), "amx2"])
            sc.add("dve", lambda e: e.reciprocal(amx[:, 3:4], amx[:, 2:3]), r=["amx2"], w=["amx3"])
            sc.add("dve", lambda e, sl=sl: e.tensor_scalar(pbb[:, sl, :], pbuf[:, sl, :], amx[:, 3:4], None, op0=ALU.mult), r=[("pbuf", sl), "amx3"], w=[("pbb", sl)])

            def dst_p(i0, cnt, pv, pk2, tt=tt):
                sc.add("act", lambda e: e.copy(pT[:, :, tt * 128:(tt + 1) * 128], pv.rearrange("p (m t) -> p m t", m=2)), r=[pk2], w=[("pT", tt // 4)])
            transposes_to(dst_p, lambda i, sl=sl: pbb[:, sl, i * 128:(i + 1) * 128], 2, BF16, lambda i, sl=sl: [("pbb", sl), "identb"], None)
        for dc in range(4):
            kq = 4 * h + dc
            for th in range(2):
                b = mainbank()
                for mh in range(2):
                    sc.add("pe", lambda e, mh=mh: e.matmul(P[b][:], mvb[:, mh, kq * 128:(kq + 1) * 128], pT[:, mh, th * 512:(th + 1) * 512], start=(mh == 0), stop=(mh == 1)),
                           r=[("mvb", mh, kq // 2), ("pT", th)], w=[("P", b)])
                eng = evac_eng()
                sc.add(eng, copy_op(eng, Q[:, kq, th * 512:(th + 1) * 512], P[b][:]), r=[("P", b)], w=[("Q", kq, th)])

    if stop_after == "xpa":
        sc.barrier()
        return finish(nc, es, sc, H, Hs, y_cur, y_smp, S, Sv, P, CS, transposes_to, load_transpose, rmsnorm_fm, debug=True)
    KV = Sv(32, 48).rearrange("p (s m f) -> p s m f", s=2, m=2)
    scs = Mv(2048, 128).rearrange("p (m c) -> p m c", m=2)
    junk5 = Mv(768, 512)
    qm = Mv(1280, 512, npart=TS)
    sc.add("dve", lambda e: e.memset(scs, 0.0), w=["scs"])
    for s_ in range(TS):
        sl = s_ % 2
        sc.add("pool", lambda e, s_=s_, sl=sl: e.dma_start(out=KV[:, sl], in_=ck[s_].rearrange("(m p) f -> p m f", p=128)), w=[("KV", sl)], dma=("KV", sl))
        for h in range(4):
            sc.add("dve", lambda e, s_=s_, h=h: e.tensor_scalar(qm, qtok_s[:, h * 512:(h + 1) * 512], cs("eye16", 1, s_, TS), None, op0=ALU.mult),
                   r=[("qtok_s", i) for i in range(4)] + ["CS"], w=["qm"])
            b = miscbank()
            sc.add("pe", lambda e, b=b: e.matmul(P[b][:], ones32[:], qm, start=True, stop=True), r=["qm", "ones32"], w=[("P", b)])
            for mh in range(2):
                sc.add("dve", lambda e, s_=s_, sl=sl, h=h, mh=mh, b=b: e.scalar_tensor_tensor(junk5, KV[:, sl, mh, h * 512:(h + 1) * 512], 1.0, P[b][:], op0=ALU.mult, op1=ALU.mult,
                                                                                           accum_out=scs[:, mh, s_ * 4 + h:s_ * 4 + h + 1]),
                       r=[("KV", sl), ("P", b), "scs"], w=["junk5", "scs"])
    smT = Mv(1792, 256, npart=64)
    b = miscbank()
    for mh in range(2):
        sc.add("pe", lambda e, mh=mh: e.transpose(P[b][0:64, mh * 128:(mh + 1) * 128], scs[:, mh, :], ident), r=["scs", "CS"], w=[("P", b)])
    amx_s = Mv(2376, 4, npart=64)
    sc.add("dve", lambda e: e.tensor_reduce(amx_s[:, 0:1], P[b][0:64, 0:256], axis=AX.X, op=ALU.max), r=[("P", b)], w=["amxs"])
    sc.add("dve", lambda e: e.tensor_scalar(amx_s[:, 1:2], amx_s[:, 0:1], -SCL, None, op0=ALU.mult), r=["amxs"], w=["amxs"])
    sc.add("act", lambda e: e.activation(out=smT, in_=P[b][0:64, 0:256], func=AF.Exp, bias=amx_s[:, 1:2], scale=SCL, accum_out=amx_s[:, 2:3]), r=[("P", b), "amxs"], w=["smT", "amxs"])
    sc.add("dve", lambda e: e.reciprocal(amx_s[:, 3:4], amx_s[:, 2:3]), r=["amxs"], w=["amxs"])
    sc.add("dve", lambda e: e.tensor_scalar(smT, smT, amx_s[:, 3:4], None, op0=ALU.mult), r=["smT", "amxs"], w=["smT"])
    pTs = Mv(2176, 64, BF16).rearrange("p (m c) -> p m c", m=2)
    b = miscbank()
    for mh in range(2):
        sc.add("pe", lambda e, mh=mh: e.transpose(P[b][:, mh * 64:(mh + 1) * 64], smT[:, mh * 128:(mh + 1) * 128], CS[0:64, CO["ident"]:CO["ident"] + 64]), r=["smT", "CS"], w=[("P", b)])
    sc.add("dve", lambda e: e.tensor_copy(Mv(2176, 64, BF16), P[b][:, 0:128]), r=[("P", b)], w=["pTs"])
    if stop_after == "xsk":
        sc.barrier()
        return finish(nc, es, sc, H, Hs, y_cur, y_smp, S, Sv, P, CS, transposes_to, load_transpose, rmsnorm_fm, debug=True)
    po = miscbank()
    oTs = Mv(2240, 128, BF16).rearrange("p (c s) -> p c s", c=16)
    for s_ in range(TS):
        sl = s_ % 2
        sc.add("pool", lambda e, s_=s_, sl=sl: e.dma_start(out=KV[:, sl], in_=cv[s_].rearrange("(m p) f -> p m f", p=128)), w=[("KV", sl)], dma=("KV", sl))
        for c in range(16):
            hh = c // 4
            for mh in range(2):
                sc.add("pe", lambda e, s_=s_, sl=sl, c=c, hh=hh, mh=mh: e.matmul(P[po][:, c * TS + s_:c * TS + s_ + 1], KV[:, sl, mh, c * 128:(c + 1) * 128], pTs[:, mh, s_ * 4 + hh:s_ * 4 + hh + 1],
                                                                                 start=(mh == 0), stop=(mh == 1)), r=[("KV", sl), "pTs"], w=[("P", po)])
    sc.add("dve", lambda e: e.tensor_copy(Mv(2240, 128, BF16), P[po][:, 0:256]), r=[("P", po)], w=["oTs"])

    def xk_Q(k, th):
        return Q[:, k, th * 512:(th + 1) * 512], [("Q", k, th)]

    def xs_oTs(k):
        return oTs[:, k, :], ["oTs"]

    gemm_b(w_co, 0, 16, 0, 2048, xk_Q, epi_res, xs=xs_oTs, epis=epis_res)
    sc.barrier()
    if stop_after == "xattn":
        return finish(nc, es, sc, H, Hs, y_cur, y_smp, S, Sv, P, CS, transposes_to, load_transpose, rmsnorm_fm, debug=True)

    rmsnorm_fm(H, Hk, X, Xk, 3, 512, 2)
    rmsnorm_fm(Hs, Hsk, Xs, Xsk, 3, TS, 1)
    sc.barrier()
    hid = Sv(0, 32).rearrange("p (k t) -> p k t", k=16)
    rl = Sv(32, 36, F32).rearrange("p (s f) -> p s f", s=2)
    hids = Mv(0, 128, BF16).rearrange("p (c s) -> p c s", c=16)
    rls = Mv(128, 16)
    ftog = {"n": 0}
    for g in range(4):
        def epi_f1(f, th, ps, pkey):
            ftog["n"] += 1
            sl = ftog["n"] % 2
            sc.add("act", lambda e: e.activation(out=rl[:, sl, :], in_=ps, func=AF.Relu), r=[pkey], w=[("rl", sl)])
            sc.add("pool", lambda e: e.tensor_tensor(hid[:, f, th * 512:(th + 1) * 512], rl[:, sl, :], rl[:, sl, :], op=ALU.mult), r=[("rl", sl)], w=[("hid", f, th)])

        def epis_f1(f, ps, pkey):
            sc.add("act", lambda e: e.activation(out=rls, in_=ps, func=AF.Relu), r=[pkey], w=["rls"])
            sc.add("dve", lambda e: e.tensor_tensor(hids[:, f, :], rls, rls, op=ALU.mult), r=["rls"], w=[("hids", f)])

        gemm_b(w_ff1, 0, 16, g * 2048, 2048, xk_X, epi_f1, xs=xs_Xs, epis=epis_f1)

        def xk_hid(k, th):
            return hid[:, k, th * 512:(th + 1) * 512], [("hid", k, th)]

        def xs_hids(k):
            return hids[:, k, :], [("hids", k)]

        gemm_b(w_ff2, g * 2048, 16, 0, 2048, xk_hid, epi_res, xs=xs_hids, epis=epis_res)
    sc.barrier()
    return finish(nc, es, sc, H, Hs, y_cur, y_smp, S, Sv, P, CS, transposes_to, load_transpose, rmsnorm_fm, debug=False)


def finish(nc, es, sc, H, Hs, y_cur, y_smp, S, Sv, P, CS, transposes_to, load_transpose, rmsnorm_fm, debug):
    X_dummy = None
    if not debug:
        sq = Sv(0, 32).rearrange("p (k t) -> p k t", k=16)
        rmsnorm_fm_out(sc, H, sq, 512, 2, 4, "H", Sv, P, CS)
        sqs = Sv(32, 33).rearrange("p (k t) -> p k t", k=16)
        rmsnorm_fm_out(sc, Hs, sqs, TS, 1, 4, "Hs", Sv, P, CS)
    yst = Sv(36, 52, F32).rearrange("p (s f) -> p s f", s=2)
    cnt = {"n": 0}
    for tt in range(NT + 1):
        smp = (tt == NT)
        sl = tt % 2
        npo = TS if smp else 128

        def dst_y(i0, c, pv, pkey, sl=sl, npo=npo):
            cnt["n"] += 1
            eng = "act" if cnt["n"] % 2 else "dve"
            o = yst[0:npo, sl, i0 * 128:(i0 + c) * 128]
            if eng == "act":
                sc.add("act", lambda e: e.copy(o, pv), r=[pkey], w=[("yst", sl)])
            else:
                sc.add("dve", lambda e: e.tensor_copy(o, pv), r=[pkey], w=[("yst", sl)])
        if smp:
            src_of = lambda i: Hs[:, i, :]
            keys = lambda i: [("Hs", i), "CS"]
        else:
            src_of = lambda i, tt=tt: H[:, i, tt * 128:(tt + 1) * 128]
            keys = lambda i, tt=tt: [("H", i, tt // 4), "CS"]
        transposes_to(dst_y, src_of, 16, F32, keys, None, np_in=128, np_out=npo)
        dsto = y_smp[:, :] if smp else y_cur[tt * 128:(tt + 1) * 128, :]
        sc.add("sp", lambda e, dsto=dsto, sl=sl, npo=npo: e.dma_start(out=dsto, in_=yst[0:npo, sl, :]), r=[("yst", sl)], dma=("yout", sl))
    sc.emit(nc, es)
    return nc, es


def rmsnorm_fm_out(sc, Hbuf, sq, ncols, nhalf, gi, hname, Sv, P, CS):
    rst = Sv(52, 56, F32).rearrange("p (h t) -> p h t", h=2)
    for th in range(nhalf):
        cols = slice(th * ncols, (th + 1) * ncols)
        hkey = (lambda k: (hname, k, th)) if hname == "H" else (lambda k: (hname, k))
        b = 5 + th
        for k in range(16):
            sc.add("act", lambda e, o=sq[:, k, cols], i=Hbuf[:, k, cols]: e.activation(out=o, in_=i, func=AF.Square), r=[hkey(k)], w=[("sq", hname, k, th)])
            sc.add("pe", lambda e, o=P[b][:, 0:ncols], r_=sq[:, k, cols], st=(k == 0), sp=(k == 15): e.matmul(o, ONESB[0][:], r_, start=st, stop=sp),
                   r=[("sq", hname, k, th), "onesb"], w=[("P", b)])
        rv = rst[:, th, 0:ncols]
        sc.add("act", lambda e, o=rv, i=P[b][:, 0:ncols]: e.activation(out=o, in_=i, func=AF.Sqrt, bias=EPS, scale=1.0 / D), r=[("P", b)], w=[("rstf", th)])
        sc.add("dve", lambda e, o=rv: e.reciprocal(o, o), r=[("rstf", th)], w=[("rstf", th)])
        for k in range(16):
            g = CS[:, CO["gpk"] + gi * 16 + k:CO["gpk"] + gi * 16 + k + 1]
            sc.add("dve", lambda e, o=Hbuf[:, k, cols], g=g, rv=rv: e.scalar_tensor_tensor(o, o, g, rv, op0=ALU.mult, op1=ALU.mult),
                   r=[hkey(k), ("rstf", th), "CS"], w=[hkey(k)])


ONESB = [None]


def _consts(hf):
    c = np.zeros((128, NCST), np.float32)

    def put(name, arr):
        arr = np.asarray(arr, np.float32)
        c[:arr.shape[0], CO[name]:CO[name] + arr.shape[1]] = arr

    put("ident", np.eye(128))
    j = np.arange(128)
    m = (j[:, None] <= j[None, :]).astype(np.float32)
    put("maskT", np.concatenate([m, m], axis=1))
    half = 64
    freqs = (10000.0 ** (-np.arange(half, dtype=np.float32) / half)).astype(np.float32)

    def rope(pos):
        ang = pos.astype(np.float32)[:, None] * freqs[None, :]
        return np.cos(ang).astype(np.float32), np.sin(ang).astype(np.float32)

    pos_c = (hf * T + np.arange(T)).astype(np.float32)
    cc_, ss_ = rope(pos_c)
    put("cosc", cc_.reshape(8, 128, 64).transpose(1, 0, 2).reshape(128, 512))
    put("sinc", ss_.reshape(8, 128, 64).transpose(1, 0, 2).reshape(128, 512))
    cp, sp_ = rope(np.arange(T).astype(np.float32))
    ropep = np.concatenate([cp.reshape(8, 128, 64).transpose(1, 0, 2).reshape(128, 512),
                            sp_.reshape(8, 128, 64).transpose(1, 0, 2).reshape(128, 512)], axis=1).astype(np.float32)
    c16, s16 = rope(np.full((128,), 16384.0, np.float32))
    put("coss", c16)
    put("sins", s16)
    g = np.array(GAM, np.float64)
    t = np.arange(T, dtype=np.float64)
    qs = np.exp(np.log(g)[None, :] * t[:, None])
    ks = np.exp(-np.log(g)[None, :] * t[:, None]) * (128.0 ** -0.5)
    put("qs", qs.reshape(8, 128, 8).transpose(1, 0, 2).reshape(128, 64))
    put("ks", ks.reshape(8, 128, 8).transpose(1, 0, 2).reshape(128, 64))
    put("gfin", np.tile((g ** 1023)[None, :], (128, 1)))
    put("gini", np.tile((g ** 1024)[None, :], (128, 1)))
    put("eye16", np.eye(16))
    put("eyeq", np.tile(np.eye(16).reshape(1, 256), (128, 1)))
    return c, ropep


_CACHE = {}


def kernel(x_prompt, x_sample, mem_prompt, cache_mem_k, cache_mem_v, state_ret,
           norm1_g, w_in, sgu_norm_g, sgu_w_s, sgu_b, ret_gn_g, w_out, norm2_g,
           mem_norm_g, w_cq, w_ck, w_cv, w_co, norm3_g, w_ff1, w_ff2, final_norm_g):
    f = lambda a: np.ascontiguousarray(np.asarray(a, dtype=np.float32))
    x_prompt, x_sample, mem_prompt = f(x_prompt), f(x_sample), f(mem_prompt)
    cache_mem_k, cache_mem_v, state_ret = f(cache_mem_k), f(cache_mem_v), f(state_ret)
    if "nc" not in _CACHE:
        _CACHE["nc"] = build()
    nc, _es = _CACHE["nc"]
    gains = [f(norm1_g)[0], f(norm2_g)[0], f(mem_norm_g)[0], f(norm3_g)[0], f(final_norm_g)]
    gpk = np.stack([gn.reshape(16, 128).T for gn in gains], axis=1).reshape(128, 80)
    gA = f(sgu_norm_g)[0].reshape(8, 128).T
    gnB = f(ret_gn_g)[0].reshape(8, 128).T
    ws = f(sgu_w_s)[0]
    sbias = f(sgu_b)[0]
    ws00 = np.tile(ws[:, 0, 0][None, :], (128, 1))
    b0 = np.tile(sbias[:, 0][None, :], (128, 1))
    wsT = np.ascontiguousarray(ws.transpose(2, 0, 1).reshape(128, 1024))
    b_bc = np.ascontiguousarray(np.tile(sbias.reshape(1, 1024), (128, 1)))
    zeros_prev = np.zeros((T, D), np.float32)
    shared = dict(w_in=f(w_in)[0], w_out=f(w_out)[0], w_cq=f(w_cq)[0], w_ck=f(w_ck)[0], w_cv=f(w_cv)[0], w_co=f(w_co)[0],
                  w_ff1=f(w_ff1)[0], w_ff2=f(w_ff2)[0], wsT=wsT, b_bc=b_bc)
    in_maps = []
    for c in range(8):
        b, hf = c // 2, c % 2
        cst, ropep = _consts(hf)
        for name, arr in (("gpk", gpk), ("gA", gA), ("gnB", gnB), ("ws00", ws00), ("b0", b0)):
            cst[:, CO[name]:CO[name] + arr.shape[1]] = arr
        m = dict(shared)
        m.update(
            x_cur=np.ascontiguousarray(x_prompt[b, hf * T:(hf + 1) * T]),
            x_prev=np.ascontiguousarray(x_prompt[b, 0:T]) if hf == 1 else zeros_prev,
            x_smp=np.ascontiguousarray(x_sample[c * TS:(c + 1) * TS, 0]),
            mem=np.ascontiguousarray(mem_prompt[b]),
            ck=np.ascontiguousarray(cache_mem_k[0, c * TS:(c + 1) * TS].reshape(TS, 256, D)),
            cv=np.ascontiguousarray(cache_mem_v[0, c * TS:(c + 1) * TS].reshape(TS, 256, D)),
            st_in=np.ascontiguousarray(state_ret[0, c * TS:(c + 1) * TS]),
            cst=cst, ropep=ropep,
        )
        in_maps.append(m)
    if _CACHE.get("test_cores"):
        n = _CACHE["test_cores"]
        res = run_bass_kernel_spmd(nc, in_maps[:n], core_ids=list(range(n)))
        return res.results
    res = run_bass_kernel_spmd(nc, in_maps, core_ids=list(range(8)))
    R = res.results
    y_prompt = np.zeros((4, 2048, D), np.float32)
    y_sample = np.zeros((128, 1, D), np.float32)
    mk = np.zeros((1, 4, 256, 4, 512), np.float32)
    mv = np.zeros((1, 4, 256, 4, 512), np.float32)
    sp = np.zeros((1, 4, 8, 128, 128), np.float32)
    ss = np.zeros((1, 128, 8, 128, 128), np.float32)
    cvs = np.zeros((1, 128, 1, 8, 128), np.float32)
    for c in range(8):
        b, hf = c // 2, c % 2
        r = R[c]
        y_prompt[b, hf * T:(hf + 1) * T] = r["y_cur"]
        y_sample[c * TS:(c + 1) * TS, 0] = r["y_smp"]
        ss[0, c * TS:(c + 1) * TS] = r["sts_o"]
        cvs[0, c * TS:(c + 1) * TS, 0] = r["cvs_o"].reshape(TS, 8, 128)
        if hf == 0:
            mk[0, b] = r["mk_o"].reshape(256, 4, 512)
            mv[0, b] = r["mv_o"].reshape(256, 4, 512)
        else:
            sp[0, b] = r["stp_o"]
    return (y_prompt, y_sample, mk, mv, sp, ss, cvs)
```

```python
import contextlib
import numpy as np
import concourse.bass as bass
import concourse.mybir as mybir
from concourse.bass_utils import run_bass_kernel_spmd

F32 = mybir.dt.float32
BF16 = mybir.dt.bfloat16
AF = mybir.ActivationFunctionType
ALU = mybir.AluOpType
AX = mybir.AxisListType

D = 2048
T = 1024
NT = 8
TS = 16
EPS = 1e-6
GAM = [1.0 - 2.0 ** (-5 - h) for h in range(8)]
ENGS = ("pe", "act", "dve", "pool", "sp")

CO = {}
_c = 0
for _n, _w in (("ident", 128), ("maskT", 256), ("gpk", 80), ("gA", 8), ("gnB", 8),
               ("ws00", 8), ("b0", 8), ("cosc", 512), ("sinc", 512),
               ("coss", 64), ("sins", 64), ("qs", 64), ("ks", 64), ("gfin", 8), ("gini", 8),
               ("eye16", 16), ("eyeq", 256)):
    CO[_n] = _c
    _c += _w
NCST = _c


class Op:
    __slots__ = ("eng", "fn", "deps", "signal", "sigval", "dma", "dval", "idx")


class _Rec:
    def __init__(self):
        self.call = None

    def __getattr__(self, name):
        def f(*a, **k):
            self.call = (name, a, k)
            return None
        return f


class Sched:
    def __init__(self):
        self.ops = []
        self.by_eng = {e: [] for e in ENGS}
        self.lastw = {}
        self.readers = {}
        self.dcount = {}
        self.last_real = {e: None for e in ENGS}
        self.dma_since = []

    def add(self, eng, fn, r=(), w=(), dma=None):
        op = Op()
        if fn is not None:
            rec = _Rec()
            fn(rec)
            assert rec.call is not None
            call = rec.call
            fn = lambda e, call=call: getattr(e, call[0])(*call[1], **call[2])
        op.eng, op.fn, op.dma, op.signal, op.sigval, op.dval = eng, fn, dma, False, 0, 0
        op.idx = len(self.ops)
        deps = {}

        def adddep(d):
            if d is None or d is op:
                return
            if d.dma is None and dma is None and d.eng == "pe" and eng == "pe":
                return
            key = ("d", id(d)) if d.dma is not None else ("e", d.eng)
            o = deps.get(key)
            if o is None or o.idx < d.idx:
                deps[key] = d

        for k in list(r) + list(w):
            adddep(self.lastw.get(k))
        for k in w:
            rd = self.readers.get(k)
            if rd:
                for d in rd.values():
                    adddep(d)
        op.deps = list(deps.values())
        for d in op.deps:
            d.signal = True
        for k in w:
            self.lastw[k] = op
            self.readers[k] = {}
        for k in r:
            rk = ("d", op.idx) if dma is not None else ("e", eng)
            self.readers.setdefault(k, {})[rk] = op
        if dma is not None:
            self.dcount[dma] = self.dcount.get(dma, 0) + 1
            op.dval = 16 * self.dcount[dma]
            self.dma_since.append(op)
        self.ops.append(op)
        self.by_eng[eng].append(op)
        if fn is not None:
            self.last_real[eng] = op
        return op

    def barrier(self):
        lasts = [self.last_real[e] for e in ENGS if self.last_real[e] is not None]
        dmas = list(self.dma_since)
        self.dma_since = []
        for e in ENGS:
            op = Op()
            op.eng, op.fn, op.dma, op.signal, op.sigval, op.dval = e, None, None, False, 0, 0
            op.idx = len(self.ops)
            deps = []
            for d in lasts:
                if d.eng != e or d.dma is not None:
                    deps.append(d)
            chan = {}
            for d in dmas:
                if d.dma not in chan or chan[d.dma].idx < d.idx:
                    chan[d.dma] = d
            deps += [d for d in chan.values() if d not in deps]
            op.deps = deps
            for d in deps:
                d.signal = True
            self.ops.append(op)
            self.by_eng[e].append(op)

    def emit(self, nc, es):
        esem = {e: es.enter_context(nc.semaphore("s_" + e)) for e in ENGS}
        dsem = {ch: es.enter_context(nc.semaphore("d_%s" % str(ch))) for ch in self.dcount}
        for e in ENGS:
            c = 0
            for op in self.by_eng[e]:
                if op.signal and op.dma is None and op.fn is not None:
                    c += 1
                    op.sigval = c
        block = es.enter_context(nc.Block())

        def run(ename, eng):
            waited = {}
            for op in self.by_eng[ename]:
                for d in op.deps:
                    if d.dma is not None:
                        sem, val = dsem[d.dma], d.dval
                    else:
                        sem, val = esem[d.eng], d.sigval
                    if waited.get(id(sem), 0) < val:
                        eng.wait_ge(sem, val)
                        waited[id(sem)] = val
                if op.fn is None:
                    continue
                ins = op.fn(eng)
                if op.dma is not None:
                    ins.then_inc(dsem[op.dma], 16)
                elif op.signal:
                    ins.then_inc(esem[ename], 1)
            if ename == "sp":
                for ch, n in self.dcount.items():
                    eng.wait_ge(dsem[ch], 16 * n)

        block.tensor(lambda t: run("pe", t))
        block.scalar(lambda t: run("act", t))
        block.vector(lambda t: run("dve", t))
        block.gpsimd(lambda t: run("pool", t))
        block.sync(lambda t: run("sp", t))


class Defer:
    def __init__(self, depth):
        self.q = []
        self.depth = depth

    def push(self, thunk):
        self.q.append(thunk)
        while len(self.q) > self.depth:
            self.q.pop(0)()

    def flush(self):
        while self.q:
            self.q.pop(0)()


def vw(base, dims, npart=None):
    p = base.ap[0]
    return bass.AP(base.tensor, base.offset, [[p[0], npart if npart else p[1]]] + [list(d) for d in dims])


def build(stop_after=None):
    nc = bass.Bass("TRN2", target_bir_lowering=False)
    es = contextlib.ExitStack()
    sc = Sched()

    def din(name, shape):
        return nc.dram_tensor(name, list(shape), F32, kind="ExternalInput").ap()

    def dout(name, shape):
        return nc.dram_tensor(name, list(shape), F32, kind="ExternalOutput").ap()

    x_cur = din("x_cur", (T, D)); x_prev = din("x_prev", (T, D)); x_smp = din("x_smp", (TS, D))
    mem = din("mem", (256, D)); ck = din("ck", (TS, 256, D)); cv = din("cv", (TS, 256, D))
    st_in = din("st_in", (TS, 8, 128, 128)); cst_d = din("cst", (128, NCST)); wsT_d = din("wsT", (128, 1024)); ropep_d = din("ropep", (128, 1024)); bbc_d = din("b_bc", (128, 1024))
    w_in = din("w_in", (D, 6144)); w_out = din("w_out", (D, D)); w_cq = din("w_cq", (D, D))
    w_ck = din("w_ck", (D, D)); w_cv = din("w_cv", (D, D)); w_co = din("w_co", (D, D))
    w_ff1 = din("w_ff1", (D, 8192)); w_ff2 = din("w_ff2", (8192, D))
    y_cur = dout("y_cur", (T, D)); y_smp = dout("y_smp", (TS, D)); mk_o = dout("mk_o", (256, D)); mv_o = dout("mv_o", (256, D))
    stp_o = dout("stp_o", (8, 128, 128)); sts_o = dout("sts_o", (TS, 8, 128, 128)); cvs_o = dout("cvs_o", (TS, 1024))

    def sb(name, shape, dt):
        return es.enter_context(nc.sbuf_tensor(name, list(shape), dt))

    H = sb("H", (128, 16, T), F32)
    X = sb("X", (128, 16, T), BF16)
    S = sb("S", (128, 28672), BF16)
    W = sb("W", (128, 3, 4096), BF16)
    CS = sb("CS", (128, NCST), F32)
    wsTb = sb("wsTb", (128, 8, 128), BF16)
    identb = sb("identb", (128, 128), BF16)
    onesb = sb("onesb", (128, 128), BF16)
    ones32 = sb("ones32", (16, 128), F32)
    Hs = sb("Hs", (128, 16, TS), F32)
    Xs = sb("Xs", (128, 16, TS), BF16)
    ccs = sb("ccs", (128, 8, TS), BF16)
    smx = sb("smx", (128, 64), F32)
    M = sb("M", (128, 3328), F32)
    rstb = sb("rstb", (128, 512), F32)
    ONESB[0] = onesb

    def Mv(o, n, dt=F32, npart=128):
        a = M[0:npart, o:o + n]
        return a.bitcast(BF16) if dt == BF16 else a
    P = [es.enter_context(nc.psum_tensor("P%d" % i, [128, 512], F32)) for i in range(8)]

    def cs(name, w, c0=0, npart=128):
        o = CO[name] + c0
        return CS[0:npart, o:o + w]

    ident = cs("ident", 128)

    def Sv(kb0, kb1, dt=BF16):
        a = S[:, kb0 * 512:kb1 * 512]
        return a.bitcast(F32) if dt == F32 else a

    rr = {"main": 0, "misc": 0, "smp": 0}

    def mainbank():
        b = rr["main"] % 4
        rr["main"] += 1
        return b

    def miscbank():
        b = 5 + rr["misc"] % 2
        rr["misc"] += 1
        return b

    def smpreg():
        r_ = rr["smp"] % 2
        rr["smp"] += 1
        return (4, 7)[r_]

    tog = {"n": 0}

    def evac_eng():
        tog["n"] += 1
        return "act" if tog["n"] % 2 else "dve"

    def copy_op(eng, out, in_):
        if eng == "act":
            return lambda e: e.copy(out, in_)
        return lambda e: e.tensor_copy(out, in_)

    wplan = []
    wstate = {"issued": 0, "used": 0}

    def wdeclare(dram, r0, kc, c0, fc):
        wplan.append((dram, r0, kc, c0, fc))

    def wissue_upto(n):
        while wstate["issued"] < min(n, len(wplan)):
            i = wstate["issued"]
            dram, r0, kc, c0, fc = wplan[i]
            slot = i % 3
            dst = W[:, slot, :].rearrange("p (k f) -> p k f", k=kc)
            src = dram[r0:r0 + kc * 128, c0:c0 + fc].rearrange("(k p) f -> p k f", p=128)
            sc.add("pool", lambda e, dst=dst, src=src: e.dma_start(out=dst, in_=src), w=[("W", slot)], dma=("W", slot))
            wstate["issued"] += 1

    def wnext(dram, r0, kc, c0, fc):
        i = wstate["used"]
        assert wplan[i][1:] == (r0, kc, c0, fc) and wplan[i][0].tensor.name == dram.tensor.name, (i, wplan[i][1:], (r0, kc, c0, fc))
        wissue_upto(i + 3)
        wstate["used"] += 1
        slot = i % 3
        return W[:, slot, :].rearrange("p (k f) -> p k f", k=kc), ("W", slot)

    def gemm_b(dram, r0, kc, c0, ncols, xk, epi, xs=None, epis=None, sform="b"):
        fc = 4096 // kc
        for s in range(ncols // fc):
            wv, wkey = wnext(dram, r0, kc, c0 + s * fc, fc)
            for j in range(fc // 128):
                f = s * (fc // 128) + j
                banks = [mainbank(), mainbank()]
                sreg = smpreg() if (xs is not None and sform == "b") else None
                for k in range(kc):
                    lhsT = wv[:, k, j * 128:(j + 1) * 128]
                    for th in range(2):
                        xa, xkeys = xk(k, th)
                        sc.add("pe", lambda e, o=P[banks[th]][:], l=lhsT, r_=xa, st=(k == 0), sp=(k == kc - 1):
                               e.matmul(o, l, r_, start=st, stop=sp), r=[wkey] + xkeys, w=[("P", banks[th])])
                    if sreg is not None:
                        xa, xkeys = xs(k)
                        sc.add("pe", lambda e, o=P[sreg][:, 0:16], l=lhsT, r_=xa, st=(k == 0), sp=(k == kc - 1):
                               e.matmul(o, l, r_, start=st, stop=sp), r=[wkey] + xkeys, w=[("P", sreg)])
                for th in range(2):
                    epi(f, th, P[banks[th]][:], ("P", banks[th]))
                if sreg is not None:
                    epis(f, P[sreg][:, 0:16], ("P", sreg))
            if xs is not None and sform == "a":
                b = miscbank()
                for k in range(kc):
                    xa, xkeys = xs(k)
                    sc.add("pe", lambda e, o=P[b][0:TS, 0:fc], l=xa, r_=wv[:, k, :], st=(k == 0), sp=(k == kc - 1):
                           e.matmul(o, l, r_, start=st, stop=sp), r=[wkey] + xkeys, w=[("P", b)])
                epis(s, P[b][0:TS, 0:fc], ("P", b))

    def gemm_a(dram, r0, kc, c0, ncols, ntiles, xt, epi, xs=None, epis=None):
        fc = 4096 // kc
        for s in range(ncols // fc):
            wv, wkey = wnext(dram, r0, kc, c0 + s * fc, fc)
            for tt in range(ntiles):
                b = mainbank()
                for k in range(kc):
                    xa, xkeys = xt(k, tt)
                    sc.add("pe", lambda e, o=P[b][:, 0:fc], l=xa, r_=wv[:, k, :], st=(k == 0), sp=(k == kc - 1):
                           e.matmul(o, l, r_, start=st, stop=sp), r=[wkey] + xkeys, w=[("P", b)])
                epi(s, tt, P[b][:, 0:fc], ("P", b))
            if xs is not None:
                b = miscbank()
                for k in range(kc):
                    xa, xkeys = xs(k)
                    sc.add("pe", lambda e, o=P[b][0:TS, 0:fc], l=xa, r_=wv[:, k, :], st=(k == 0), sp=(k == kc - 1):
                           e.matmul(o, l, r_, start=st, stop=sp), r=[wkey] + xkeys, w=[("P", b)])
                epis(s, P[b][0:TS, 0:fc], ("P", b))

    def transposes_to(dst_of, src_of, n, dt, srckeys, dstkeys, np_in=128, np_out=128, evac=None):
        grp = 4
        for i0 in range(0, n, grp):
            cnt = min(grp, n - i0)
            b = miscbank()
            if dt == F32:
                pv = P[b][0:np_out, :]
                idn = CS[0:np_in, CO["ident"]:CO["ident"] + np_in]
            else:
                pv = P[b][:].bitcast(BF16)[0:np_out, 0:512]
                idn = identb[0:np_in, 0:np_in]
            for i in range(cnt):
                sc.add("pe", lambda e, o=pv[:, i * np_in:(i + 1) * np_in], s_=src_of(i0 + i), idn=idn: e.transpose(o, s_, idn),
                       r=srckeys(i0 + i), w=[("P", b)])
            dst_of(i0, cnt, pv[:, 0:cnt * np_in], ("P", b))

    sc.add("sp", lambda e: e.dma_start(out=CS[:], in_=cst_d[:, :]), w=["CS"], dma="cst")
    wst32 = Sv(16, 20, F32)
    sc.add("sp", lambda e: e.dma_start(out=wst32, in_=wsT_d[:, :]), w=["wst32"], dma="cst2")
    sc.add("dve", lambda e: e.tensor_copy(identb[:], ident), r=["CS"], w=["identb"])
    sc.add("dve", lambda e: e.memset(onesb[:], 1.0), w=["onesb"])
    sc.add("dve", lambda e: e.memset(ones32[:], 1.0), w=["ones32"])
    sc.add("dve", lambda e: e.tensor_tensor(wsTb[:], wst32.rearrange("p (h i) -> p h i", h=8),
                                            vw(cs("maskT", 128), [[0, 8], [1, 128]]), op=ALU.mult), r=["CS", "wst32"], w=["wsTb"])

    sc.barrier()
    ropep = Sv(32, 36, F32)
    sc.add("sp", lambda e: e.dma_start(out=ropep, in_=ropep_d[:, :]), w=["ropep"], dma="cst3")
    for hp in range(4):
        wdeclare(w_in, 0, 16, 3072 + hp * 256, 256)
        wdeclare(w_in, 0, 16, 4096 + hp * 256, 256)
    for s in range(4):
        wdeclare(w_in, 0, 16, 1024 + s * 256, 256)
    for s in range(4):
        wdeclare(w_in, 0, 16, s * 256, 256)
    for s in range(4):
        wdeclare(w_out, 0, 8, s * 512, 512)
    for hp in range(4):
        for base in (2048, 3072, 4096, 5120):
            wdeclare(w_in, 0, 16, base + hp * 256, 256)
    for s in range(4):
        wdeclare(w_out, 1024, 8, s * 512, 512)
    for wd in (w_cq, w_ck, w_cv, w_co):
        for s in range(8):
            wdeclare(wd, 0, 16, s * 256, 256)
    for g in range(4):
        for s in range(8):
            wdeclare(w_ff1, 0, 16, g * 2048 + s * 256, 256)
        for s in range(8):
            wdeclare(w_ff2, g * 2048, 16, s * 256, 256)
    wissue_upto(2)

    def load_transpose(src, ntok_tiles, rows_per_tile, dst, dkey, stage_kb=(0, 16), gain=None, ssq=None):
        st = Sv(stage_kb[0], stage_kb[1], F32).rearrange("p (s f) -> p s f", s=2)
        for tt in range(ntok_tiles):
            slot = tt % 2
            stv = st[0:rows_per_tile, slot, :]
            sc.add("sp", lambda e, o=stv, i=src[tt * rows_per_tile:(tt + 1) * rows_per_tile, :]: e.dma_start(out=o, in_=i),
                   w=[("xst", slot)], dma=("xst", slot))
            if ssq is not None:
                ssq(tt, stv, ("xst", slot))

            def dst_of(i0, cnt, pv, pkey, tt=tt):
                dst(i0, cnt, tt, pv, pkey)
            transposes_to(dst_of, lambda i, stv=stv: stv[:, i * 128:(i + 1) * 128], 16, F32,
                          lambda i, slot=slot: [("xst", slot), "CS"], None, np_in=rows_per_tile, np_out=128)

    def rmsnorm_fm(Hbuf, hkey, Xbuf, xkey, gi, ncols, nhalf, out32=False):
        for th in range(nhalf):
            cols = slice(th * ncols, (th + 1) * ncols)
            b = miscbank()
            for k in range(16):
                sc.add("act", lambda e, o=Xbuf[:, k, cols], i=Hbuf[:, k, cols]: e.activation(out=o, in_=i, func=AF.Square),
                       r=[hkey(k, th)], w=[xkey(k, th)])
                sc.add("pe", lambda e, o=P[b][:, 0:ncols], r_=Xbuf[:, k, cols], st=(k == 0), sp=(k == 15):
                       e.matmul(o, onesb[:], r_, start=st, stop=sp), r=[xkey(k, th), "onesb"], w=[("P", b)])
            rv = rstb[:, 0:ncols]
            sc.add("act", lambda e, o=rv, i=P[b][:, 0:ncols]: e.activation(out=o, in_=i, func=AF.Sqrt, bias=EPS, scale=1.0 / D),
                   r=[("P", b)], w=[("rst", 0)])
            sc.add("dve", lambda e, o=rv: e.reciprocal(o, o), r=[("rst", 0)], w=[("rst", 0)])
            for k in range(16):
                g = cs("gpk", 1, gi * 16 + k)
                if out32:
                    sc.add("dve", lambda e, o=Hbuf[:, k, cols], g=g, rv=rv: e.scalar_tensor_tensor(o, o, g, rv, op0=ALU.mult, op1=ALU.mult),
                           r=[hkey(k, th), ("rst", 0), "CS"], w=[hkey(k, th)])
                else:
                    sc.add("dve", lambda e, o=Xbuf[:, k, cols], i=Hbuf[:, k, cols], g=g, rv=rv:
                           e.scalar_tensor_tensor(o, i, g, rv, op0=ALU.mult, op1=ALU.mult),
                           r=[hkey(k, th), ("rst", 0), "CS"], w=[xkey(k, th)])

    Hk = lambda k, th: ("H", k, th)
    Xk = lambda k, th: ("X", k, th)
    Hsk = lambda k, th: ("Hs", k)
    Xsk = lambda k, th: ("Xs", k)

    def xk_X(k, th):
        return X[:, k, th * 512:(th + 1) * 512], [("X", k, th)]

    def xt_X(k, tt):
        return X[:, k, tt * 128:(tt + 1) * 128], [("X", k, tt // 4)]

    def xs_Xs(k):
        return Xs[:, k, :], [("Xs", k)]

    def rope_scale(dst_bf, ps, pkey, cosA, sinA, scaleA, nparts, tmpv, tmpkey, outkeys, imm_scale=None, extra=()):
        xs_ = tmpv[0:nparts, 0, :]
        t1 = tmpv[0:nparts, 1, 0:128]
        t2 = tmpv[0:nparts, 2, 0:128]
        if scaleA is not None:
            sc.add("dve", lambda e: e.tensor_tensor(xs_.rearrange("p (h d) -> p h d", h=2), ps.rearrange("p (h d) -> p h d", h=2),
                                                    vw(scaleA, [[1, 2], [0, 128]]), op=ALU.mult), r=[pkey, "CS"], w=[tmpkey])
        else:
            sc.add("act", lambda e: e.activation(out=xs_, in_=ps, func=AF.Copy, scale=(imm_scale or 1.0)), r=[pkey], w=[tmpkey])
        xv = xs_.rearrange("p (h t d) -> p h t d", h=2, t=2)
        dv = dst_bf.rearrange("p (h t d) -> p h t d", h=2, t=2)
        cb = vw(cosA, [[0, 2], [1, 64]])
        sb_ = vw(sinA, [[0, 2], [1, 64]])
        t1v = t1.rearrange("p (h d) -> p h d", h=2)
        t2v = t2.rearrange("p (h d) -> p h d", h=2)
        rk = [tmpkey, "CS"] + list(extra)
        sc.add("dve", lambda e: e.tensor_tensor(t1v, xv[:, :, 0, :], cb, op=ALU.mult), r=rk, w=[tmpkey + ("a",)])
        sc.add("dve", lambda e: e.tensor_tensor(t2v, xv[:, :, 1, :], sb_, op=ALU.mult), r=rk, w=[tmpkey + ("b",)])
        sc.add("dve", lambda e: e.tensor_tensor(dv[:, :, 0, :], t1v, t2v, op=ALU.subtract), r=[tmpkey + ("a",), tmpkey + ("b",)], w=outkeys)
        t3 = tmpv[0:nparts, 1, 128:256].rearrange("p (h d) -> p h d", h=2)
        t4 = tmpv[0:nparts, 2, 128:256].rearrange("p (h d) -> p h d", h=2)
        sc.add("dve", lambda e: e.tensor_tensor(t3, xv[:, :, 0, :], sb_, op=ALU.mult), r=rk, w=[tmpkey + ("c",)])
        sc.add("dve", lambda e: e.tensor_tensor(t4, xv[:, :, 1, :], cb, op=ALU.mult), r=rk, w=[tmpkey + ("d",)])
        sc.add("dve", lambda e: e.tensor_tensor(dv[:, :, 1, :], t3, t4, op=ALU.add), r=[tmpkey + ("c",), tmpkey + ("d",)], w=outkeys)

    ropetmp = [Mv(0, 768).rearrange("p (a f) -> p a f", a=3), Mv(2560, 768).rearrange("p (a f) -> p a f", a=3)]
    rtog = {"n": 0}

    def ropeslot():
        rtog["n"] += 1
        i_ = rtog["n"] % 2
        return ropetmp[i_], ("rtmp", i_)

    Ainit = Sv(52, 56, F32).rearrange("p (h e) -> p h e", h=8)
    kp_tok = Sv(16, 20).rearrange("p (t f) -> p t f", t=8)
    vp_tok = Sv(20, 24).rearrange("p (t f) -> p t f", t=8)
    rstd_p = smx[:, 0:8]
    ssq_p = smx[:, 8:16]
    junkp = Sv(24, 32, F32)

    def ssq_prev(tt, stv, skey):
        sc.add("act", lambda e: e.activation(out=junkp, in_=stv, func=AF.Square, accum_out=ssq_p[:, tt:tt + 1]), r=[skey], w=["junkp", ("ssqp", tt)])

    def dst_prev(i0, cnt, tt, pv, pkey):
        eng = evac_eng()
        for i in range(cnt):
            k = i0 + i
            if eng == "act":
                sc.add("act", lambda e, o=X[:, k, tt * 128:(tt + 1) * 128], i_=pv[:, i * 128:(i + 1) * 128], g=cs("gpk", 1, k):
                       e.activation(out=o, in_=i_, func=AF.Copy, scale=g), r=[pkey, "CS"], w=[("X", k, tt // 4)])
            else:
                sc.add("dve", lambda e, o=X[:, k, tt * 128:(tt + 1) * 128], i_=pv[:, i * 128:(i + 1) * 128], g=cs("gpk", 1, k):
                       e.tensor_scalar(o, i_, g, None, op0=ALU.mult), r=[pkey, "CS"], w=[("X", k, tt // 4)])

    sc.add("dve", lambda e: e.memset(ssq_p, 0.0), w=[("ssqp", t_) for t_ in range(8)])
    load_transpose(x_prev, NT, 128, dst_prev, None, ssq=ssq_prev)
    sc.add("act", lambda e: e.activation(out=rstd_p, in_=ssq_p, func=AF.Sqrt, bias=EPS, scale=1.0 / D), r=[("ssqp", t_) for t_ in range(8)], w=["rstdp"])
    sc.add("dve", lambda e: e.reciprocal(rstd_p, rstd_p), r=["rstdp"], w=["rstdp"])

    for hp in range(4):
        def epi_kp2(s, tt, ps, pkey, hp=hp):
            tv, tk = ropeslot()
            scl = smx[:, 16 + 2 * (tt % 2):18 + 2 * (tt % 2)]
            sc.add("dve", lambda e: e.tensor_scalar(scl, cs("ks", 2, tt * 8 + hp * 2), rstd_p[:, tt:tt + 1], None, op0=ALU.mult),
                   r=["CS", "rstdp"], w=[tk])
            rope_scale(kp_tok[:, tt, :], ps, pkey, ropep[:, tt * 64:(tt + 1) * 64], ropep[:, 512 + tt * 64:512 + (tt + 1) * 64], scl, 128, tv, tk, [("kp", tt)], extra=["ropep"])

        def epi_vp(s, tt, ps, pkey):
            sc.add("act", lambda e: e.activation(out=vp_tok[:, tt, :], in_=ps, func=AF.Copy, scale=rstd_p[:, tt:tt + 1]), r=[pkey, "rstdp"], w=[("vp", tt)])

        gemm_a(w_in, 0, 16, 3072 + hp * 256, 256, NT, xt_X, epi_kp2)
        gemm_a(w_in, 0, 16, 4096 + hp * 256, 256, NT, xt_X, epi_vp)
        b = miscbank()
        for hh in range(2):
            for tt in range(NT):
                sc.add("pe", lambda e, o=P[b][:, hh * 128:(hh + 1) * 128], l=kp_tok[:, tt, hh * 128:(hh + 1) * 128], r_=vp_tok[:, tt, hh * 128:(hh + 1) * 128],
                       st=(tt == 0), sp=(tt == NT - 1): e.matmul(o, l, r_, start=st, stop=sp), r=[("kp", tt), ("vp", tt)], w=[("P", b)])
        for hh in range(2):
            h = hp * 2 + hh
            sc.add("dve", lambda e, o=Ainit[:, h, :], i=P[b][:, hh * 128:(hh + 1) * 128], g=cs("gini", 1, h):
                   e.tensor_scalar(o, i, g, None, op0=ALU.mult), r=[("P", b), "CS"], w=[("Ainit", h)])
    sc.barrier()
    if stop_after == "prefix":
        return finish(nc, es, sc, H, Hs, y_cur, y_smp, S, Sv, P, CS, transposes_to, load_transpose, rmsnorm_fm, debug=True)

    def dst_cur(i0, cnt, tt, pv, pkey):
        eng = evac_eng()
        o = H[:, i0:i0 + cnt, tt * 128:(tt + 1) * 128]
        i_ = pv.rearrange("p (c t) -> p c t", c=cnt)
        sc.add(eng, copy_op(eng, o, i_), r=[pkey], w=[("H", k, tt // 4) for k in range(i0, i0 + cnt)])

    load_transpose(x_cur, NT, 128, dst_cur, None)

    def dst_smp(i0, cnt, tt, pv, pkey):
        o = Hs[:, i0:i0 + cnt, :]
        i_ = pv.rearrange("p (c t) -> p c t", c=cnt)
        sc.add("dve", copy_op("dve", o, i_), r=[pkey], w=[("Hs", k) for k in range(i0, i0 + cnt)])

    load_transpose(x_smp, 1, TS, dst_smp, None)
    rmsnorm_fm(H, Hk, X, Xk, 0, 512, 2)
    rmsnorm_fm(Hs, Hsk, Xs, Xsk, 0, TS, 1)
    sc.barrier()
    if stop_after == "norm1":
        return finish(nc, es, sc, H, Hs, y_cur, y_smp, S, Sv, P, CS, transposes_to, load_transpose, rmsnorm_fm, debug=True)

    cc = Sv(0, 16).rearrange("p (j t) -> p j t", j=8)
    vrows = Sv(16, 32).rearrange("p (t f) -> p t f", t=8)
    g32 = Sv(36, 38, F32).rearrange("p (s f) -> p s f", s=2)
    ug = Sv(38, 42, F32).rearrange("p (s f) -> p s f", s=2)
    t32 = Sv(42, 44, F32)
    vs32 = Sv(32, 36, F32)
    bst = Mv(768, 192).rearrange("p (t s c) -> p t s c", t=8, s=4)
    bsts = Mv(1216, 24, npart=TS).rearrange("p (s c) -> p s c", s=4)
    ugs = Mv(960, 128).rearrange("p (h t) -> p h t", h=8)
    vgs = Mv(1088, 128).rearrange("p (h t) -> p h t", h=8)
    b_bc = Sv(48, 52, F32)
    sc.add("sp", lambda e: e.dma_start(out=b_bc, in_=bbc_d[:, :]), w=["b_bc"], dma="cst4")

    gtog = {"n": 0}

    def epi_vA(s, tt, ps, pkey):
        gtog["n"] += 1
        sl = gtog["n"] % 2
        sc.add("act", lambda e: e.activation(out=g32[:, sl, :], in_=ps, func=AF.Gelu_apprx_tanh), r=[pkey], w=[("g32", sl)])
        sc.add("dve", lambda e: e.bn_stats(bst[:, tt, s, :], g32[:, sl, :]), r=[("g32", sl)], w=[("bst", tt)])
        sc.add("dve", lambda e: e.tensor_copy(vrows[:, tt, s * 256:(s + 1) * 256], g32[:, sl, :]), r=[("g32", sl)], w=[("vrows", tt)])

    def epis_vA(s, ps, pkey):
        sc.add("act", lambda e: e.activation(out=vs32[0:TS, s * 256:(s + 1) * 256], in_=ps, func=AF.Gelu_apprx_tanh), r=[pkey], w=["vs32"])
        sc.add("dve", lambda e: e.bn_stats(bsts[:, s, :], vs32[0:TS, s * 256:(s + 1) * 256]), r=["vs32"], w=["bsts"])

    gemm_a(w_in, 0, 16, 1024, 1024, NT, xt_X, epi_vA, xs=xs_Xs, epis=epis_vA)

    def ln_finish(stats, np_, data_in, data_out, keys_r, keys_w):
        mv = smx[0:np_, 0:2]
        sc.add("dve", lambda e: e.bn_aggr(mv, stats), r=keys_r, w=["mvA"])
        sc.add("act", lambda e: e.activation(out=mv[:, 1:2], in_=mv[:, 1:2], func=AF.Sqrt, bias=EPS, scale=1.0), r=["mvA"], w=["mvA"])
        sc.add("dve", lambda e: e.reciprocal(mv[:, 1:2], mv[:, 1:2]), r=["mvA"], w=["mvA"])
        sc.add("dve", lambda e: e.tensor_scalar(data_out, data_in, mv[:, 0:1], mv[:, 1:2], op0=ALU.subtract, op1=ALU.mult),
               r=["mvA"] + keys_r, w=keys_w)

    for tt in range(NT):
        ln_finish(bst[:, tt, :, :].rearrange("p s c -> p (s c)"), 128, vrows[:, tt, :], vrows[:, tt, :], [("bst", tt), ("vrows", tt)], [("vrows", tt)])
    ln_finish(bsts.rearrange("p s c -> p (s c)"), TS, vs32[0:TS, :], vs32[0:TS, :], ["bsts", "vs32"], ["vs32"])
    def dst_vgs(i0, cnt, pv, pkey):
        for i in range(cnt):
            h = i0 + i
            sc.add("dve", lambda e, o=vgs[:, h, :], i_=pv[:, i * TS:(i + 1) * TS], g=cs("gA", 1, h): e.tensor_scalar(o, i_, g, None, op0=ALU.mult),
                   r=[pkey, "CS"], w=[("vgs", h)])
    transposes_to(dst_vgs, lambda i: vs32[0:TS, i * 128:(i + 1) * 128], 8, F32, lambda i: ["vs32", "CS"], None, np_in=TS, np_out=128)
    cvst = Sv(44, 48, F32)
    def dst_cvs(i0, cnt, pv, pkey):
        sc.add("act", lambda e: e.copy(cvst[0:TS, i0 * 128:(i0 + cnt) * 128], pv), r=[pkey], w=["cvst"])
    transposes_to(dst_cvs, lambda i: vgs[:, i, :], 8, F32, lambda i: [("vgs", i), "CS"], None, np_in=128, np_out=TS)
    sc.add("sp", lambda e: e.dma_start(out=cvs_o[:, :], in_=cvst[0:TS, :]), r=["cvst"], dma="out_cvs")

    utog = {"n": 0}

    def epi_u(f, th, ps, pkey):
        utog["n"] += 1
        sl = utog["n"] % 2
        sc.add("act", lambda e: e.activation(out=ug[:, sl, :], in_=ps, func=AF.Gelu_apprx_tanh), r=[pkey], w=[("ug", sl)])
        b = miscbank()
        for n in range(4):
            tt = th * 4 + n
            sc.add("pe", lambda e, o=P[b][:, n * 128:(n + 1) * 128], l=vrows[:, tt, f * 128:(f + 1) * 128], r_=wsTb[:, f, :]:
                   e.matmul(o, l, r_, start=True, stop=True), r=[("vrows", tt), "wsTb"], w=[("P", b)])
        sc.add("dve", lambda e: e.scalar_tensor_tensor(t32.rearrange("p (n i) -> p n i", n=4), P[b][:].rearrange("p (n i) -> p n i", n=4), cs("gA", 1, f),
                                                       vw(b_bc[:, f * 128:(f + 1) * 128], [[0, 4], [1, 128]]), op0=ALU.mult, op1=ALU.add),
               r=[("P", b), "CS", "b_bc"], w=["t32"])
        sc.add("dve", lambda e: e.tensor_tensor(cc[:, f, th * 512:(th + 1) * 512], t32, ug[:, sl, :], op=ALU.mult), r=["t32", ("ug", sl)], w=[("cc", f, th)])

    def epis_u(f, ps, pkey):
        sc.add("act", lambda e: e.activation(out=ugs[:, f, :], in_=ps, func=AF.Gelu_apprx_tanh), r=[pkey], w=[("ugs", f)])
        tmp = smx[:, 32:48]
        sc.add("dve", lambda e: e.tensor_scalar(tmp, vgs[:, f, :], cs("ws00", 1, f), cs("b0", 1, f), op0=ALU.mult, op1=ALU.add),
               r=[("vgs", f), "CS"], w=["tmps"])
        sc.add("dve", lambda e: e.tensor_tensor(ccs[:, f, :], tmp, ugs[:, f, :], op=ALU.mult), r=["tmps", ("ugs", f)], w=[("ccs", f)])

    gemm_b(w_in, 0, 16, 0, 1024, xk_X, epi_u, xs=xs_Xs, epis=epis_u)

    def xk_cc(k, th):
        return cc[:, k, th * 512:(th + 1) * 512], [("cc", k, th)]

    def xs_ccs(k):
        return ccs[:, k, :], [("ccs", k)]

    def epi_res(f, th, ps, pkey):
        eng = "dve"
        o = H[:, f, th * 512:(th + 1) * 512]
        sc.add(eng, lambda e: e.tensor_tensor(o, o, ps, op=ALU.add), r=[pkey, ("H", f, th)], w=[("H", f, th)])

    def epis_res(f, ps, pkey):
        o = Hs[:, f, :]
        sc.add("dve", lambda e: e.tensor_tensor(o, o, ps, op=ALU.add), r=[pkey, ("Hs", f)], w=[("Hs", f)])

    gemm_b(w_out, 0, 8, 0, 2048, xk_cc, epi_res, xs=xs_ccs, epis=epis_res)
    sc.barrier()
    if stop_after == "mixA":
        return finish(nc, es, sc, H, Hs, y_cur, y_smp, S, Sv, P, CS, transposes_to, load_transpose, rmsnorm_fm, debug=True)

    qT = Sv(16, 20).rearrange("p (h t) -> p h t", h=2)
    kT = Sv(20, 24).rearrange("p (h t) -> p h t", h=2)
    ktok = Sv(24, 28).rearrange("p (t f) -> p t f", t=8)
    vtok = Sv(28, 32).rearrange("p (t f) -> p t f", t=8)
    gT = Sv(32, 36).rearrange("p (h t) -> p h t", h=2)
    ks_s = Sv(36, 40, F32)
    vs_s = Sv(40, 44, F32)
    stsl = Sv(44, 56, F32).rearrange("p (s h e) -> p s h e", s=3, h=8)
    qtok_t = Mv(768, 256, BF16).rearrange("p (s f) -> p s f", s=2)
    qs_s = Sv(48, 52, F32)[0:TS, :]
    qTs = Mv(1920, 128).rearrange("p (h t) -> p h t", h=8)
    gTs = Mv(2048, 128).rearrange("p (h t) -> p h t", h=8)
    Aacc = Mv(1536, 256).rearrange("p (h e) -> p h e", h=2)
    Abf = Mv(1792, 128, BF16).rearrange("p (h e) -> p h e", h=2)
    scm = Mv(1024, 256, BF16).rearrange("p (s f) -> p s f", s=2)
    retn = Mv(1280, 256, BF16).rearrange("p (s f) -> p s f", s=2)
    rstat = Mv(2176, 12).rearrange("p (h c) -> p h c", h=2)
    rmv = Mv(2188, 4).rearrange("p (h c) -> p h c", h=2)
    qtt = {"n": 0}
    dfq = Defer(1)

    for hp in range(4):
        def epi_q(s, tt, ps, pkey, hp=hp, which="q"):
            tv, tk = ropeslot()
            qtt["n"] += 1
            sl = qtt["n"] % 2
            if which == "q":
                dstb, dkey, scl = qtok_t[:, sl, :], ("qtokt", sl), cs("qs", 2, tt * 8 + hp * 2)
            else:
                dstb, dkey, scl = ktok[:, tt, :], ("ktok", tt), cs("ks", 2, tt * 8 + hp * 2)
            rope_scale(dstb, ps, pkey, cs("cosc", 64, tt * 64), cs("sinc", 64, tt * 64), scl, 128, tv, tk, [dkey])
            dT = qT if which == "q" else kT
            nm = "qT" if which == "q" else "kT"

            def dst_t(i0, cnt, pv, pk2):
                sc.add("act", lambda e: e.copy(dT[:, :, tt * 128:(tt + 1) * 128], pv.rearrange("p (h t) -> p h t", h=2)), r=[pk2], w=[(nm, tt)])
            dfq.push(lambda: transposes_to(dst_t, lambda i: dstb[:, i * 128:(i + 1) * 128], 2, BF16, lambda i: [dkey, "identb"], None))

        def epis_q(s, ps, pkey, hp=hp, which="q"):
            tv, tk = ropeslot()
            dsts = (qs_s if which == "q" else ks_s)[0:TS, hp * 256:(hp + 1) * 256]
            rope_scale(dsts, ps, pkey, cs("coss", 64, 0, TS), cs("sins", 64, 0, TS), None, TS, tv, tk, [("qs_s" if which == "q" else "ks_s", hp)],
                       imm_scale=(1.0 if which == "q" else 128.0 ** -0.5))

        gemm_a(w_in, 0, 16, 2048 + hp * 256, 256, NT, xt_X, epi_q, xs=xs_Xs, epis=epis_q)
        dfq.flush()
        gemm_a(w_in, 0, 16, 3072 + hp * 256, 256, NT, xt_X, lambda s, tt, ps, pkey, hp=hp: epi_q(s, tt, ps, pkey, hp, "k"),
               xs=xs_Xs, epis=lambda s, ps, pkey, hp=hp: epis_q(s, ps, pkey, hp, "k"))
        dfq.flush()

        def epi_v(s, tt, ps, pkey):
            eng = evac_eng()
            sc.add(eng, copy_op(eng, vtok[:, tt, :], ps), r=[pkey], w=[("vtok", tt)])

        def epis_v(s, ps, pkey, hp=hp):
            sc.add("act", lambda e: e.copy(vs_s[0:TS, hp * 256:(hp + 1) * 256], ps), r=[pkey], w=[("vs_s", hp)])

        gemm_a(w_in, 0, 16, 4096 + hp * 256, 256, NT, xt_X, epi_v, xs=xs_Xs, epis=epis_v)

        def epi_g(f, th, ps, pkey):
            sc.add("act", lambda e: e.activation(out=gT[:, f, th * 512:(th + 1) * 512], in_=ps, func=AF.Silu), r=[pkey], w=[("gT", f, th)])

        def epis_g(f, ps, pkey, hp=hp):
            sc.add("act", lambda e: e.activation(out=gTs[:, hp * 2 + f, :], in_=ps, func=AF.Silu), r=[pkey], w=[("gTs", hp * 2 + f)])

        gemm_b(w_in, 0, 16, 5120 + hp * 256, 256, xk_X, epi_g, xs=xs_Xs, epis=epis_g)

        for hh in range(2):
            h = hp * 2 + hh
            sc.add("dve", lambda e, hh=hh, h=h: e.tensor_copy(Aacc[:, hh, :], Ainit[:, h, :]), r=[("Ainit", h)], w=["Aacc"])
        sc.add("act", lambda e: e.copy(Mv(1792, 128, BF16), Mv(1536, 256)), r=["Aacc"], w=["Abf"])
        deferred = [None]
        for n in range(NT):
            b1 = miscbank()
            for hh in range(2):
                sc.add("pe", lambda e, hh=hh: e.matmul(P[b1][:, hh * 128:(hh + 1) * 128], kT[:, hh, n * 128:(n + 1) * 128], qT[:, hh, n * 128:(n + 1) * 128],
                                                      start=True, stop=True), r=[("kT", n), ("qT", n)], w=[("P", b1)])
            sl = n % 2
            sc.add("dve", lambda e, sl=sl: e.tensor_tensor(scm[:, sl, :], P[b1][:, 0:256], cs("maskT", 256), op=ALU.mult), r=[("P", b1), "CS"], w=[("scm", sl)])
            b2 = mainbank()
            for hh in range(2):
                sc.add("pe", lambda e, hh=hh, sl=sl: e.matmul(P[b2][:, hh * 128:(hh + 1) * 128], scm[:, sl, hh * 128:(hh + 1) * 128], vtok[:, n, hh * 128:(hh + 1) * 128],
                                                             start=True, stop=False), r=[("scm", sl), ("vtok", n)], w=[("P", b2)])
                sc.add("pe", lambda e, hh=hh: e.matmul(P[b2][:, hh * 128:(hh + 1) * 128], qT[:, hh, n * 128:(n + 1) * 128], Abf[:, hh, :],
                                                      start=False, stop=True), r=[("qT", n), "Abf"], w=[("P", b2)])
            b3 = miscbank()
            for hh in range(2):
                sc.add("pe", lambda e, hh=hh: e.matmul(P[b3][:, hh * 128:(hh + 1) * 128], ktok[:, n, hh * 128:(hh + 1) * 128], vtok[:, n, hh * 128:(hh + 1) * 128],
                                                      start=True, stop=True), r=[("ktok", n), ("vtok", n)], w=[("P", b3)])
            sc.add("dve", lambda e: e.tensor_tensor(Mv(1536, 256), Mv(1536, 256), P[b3][:, 0:256], op=ALU.add),
                   r=[("P", b3), "Aacc"], w=["Aacc"])
            sc.add("act", lambda e: e.copy(Mv(1792, 128, BF16), Mv(1536, 256)), r=["Aacc"], w=["Abf"])
            for hh in range(2):
                sc.add("dve", lambda e, hh=hh: e.bn_stats(rstat[:, hh, :], P[b2][:, hh * 128:(hh + 1) * 128]), r=[("P", b2)], w=[("rstat", hh)])
                sc.add("dve", lambda e, hh=hh: e.bn_aggr(rmv[:, hh, :], rstat[:, hh, :]), r=[("rstat", hh)], w=[("rmv", hh)])
            sc.add("act", lambda e: e.activation(out=rmv[:, :, 1:2], in_=rmv[:, :, 1:2], func=AF.Sqrt, bias=EPS, scale=1.0), r=[("rmv", 0), ("rmv", 1)], w=[("rmv", 0), ("rmv", 1)])
            sc.add("dve", lambda e: e.reciprocal(rmv[:, :, 1:2], rmv[:, :, 1:2]), r=[("rmv", 0), ("rmv", 1)], w=[("rmv", 0), ("rmv", 1)])
            sc.add("dve", lambda e: e.scalar_tensor_tensor(rmv[:, :, 0:1], rmv[:, :, 0:1], -1.0, rmv[:, :, 1:2], op0=ALU.mult, op1=ALU.mult),
                   r=[("rmv", 0), ("rmv", 1)], w=[("rmv", 0), ("rmv", 1)])
            for hh in range(2):
                sc.add("act", lambda e, hh=hh, sl=sl: e.activation(out=retn[:, sl, hh * 128:(hh + 1) * 128], in_=P[b2][:, hh * 128:(hh + 1) * 128], func=AF.Identity,
                                                                  bias=rmv[:, hh, 0:1], scale=rmv[:, hh, 1:2]), r=[("P", b2), ("rmv", hh)], w=[("retn", sl)])

            def dst_r(i0, cnt, pv, pk2, n=n, hp=hp):
                for hh in range(2):
                    h = hp * 2 + hh
                    sc.add("dve", lambda e, hh=hh, h=h: e.scalar_tensor_tensor(cc[:, h, n * 128:(n + 1) * 128], pv[:, hh * 128:(hh + 1) * 128], cs("gnB", 1, h),
                                                                               gT[:, hh, n * 128:(n + 1) * 128], op0=ALU.mult, op1=ALU.mult),
                           r=[pk2, "CS", ("gT", hh, n // 4)], w=[("cc", h, n // 4)])
            if deferred[0] is not None:
                deferred[0]()
            deferred[0] = (lambda dst_r=dst_r, sl=sl: transposes_to(dst_r, lambda i: retn[:, sl, i * 128:(i + 1) * 128], 2, BF16, lambda i: [("retn", sl), "identb"], None))
        deferred[0]()
        stst = Sv(44, 45, F32).rearrange("p (h e) -> p h e", h=2)
        for hh in range(2):
            h = hp * 2 + hh
            sc.add("dve", lambda e, hh=hh, h=h: e.tensor_scalar(stst[:, hh, :], Aacc[:, hh, :], cs("gfin", 1, h), None, op0=ALU.mult), r=["Aacc", "CS"], w=[("stst", hh)])
        sc.add("sp", lambda e, hp=hp: e.dma_start(out=stp_o[hp * 2:hp * 2 + 2].rearrange("h d e -> d h e"), in_=stst), r=[("stst", 0), ("stst", 1)], dma="out_stp")

    sc.barrier()
    def dst_qTs(i0, cnt, pv, pkey):
        sc.add("dve", lambda e: e.tensor_copy(qTs[:, i0:i0 + cnt, :], pv.rearrange("p (c t) -> p c t", c=cnt)), r=[pkey], w=[("qTs", i0 // 4)])
    transposes_to(dst_qTs, lambda i: qs_s[0:TS, i * 128:(i + 1) * 128], 8, F32, lambda i: [("qs_s", i // 2), "CS"], None, np_in=TS, np_out=128)
    Qsel = Sv(16, 24, F32).rearrange("p (h s c) -> p h s c", h=8, s=TS)
    sc.add("dve", lambda e: e.tensor_tensor(Qsel, vw(qTs[:, 0, 0:1], [[TS, 8], [1, TS], [0, TS]]),
                                            vw(cs("eyeq", 256), [[0, 8], [TS, TS], [1, TS]]), op=ALU.mult),
           r=[("qTs", 0), ("qTs", 1), "CS"], w=["Qsel"])
    qk = Mv(2192, 8, npart=TS)
    qkj = Sv(24, 28, F32)
    sc.add("dve", lambda e: e.tensor_tensor(qkj[0:TS, :], qs_s, ks_s[0:TS, :], op=ALU.mult), r=[("qs_s", i) for i in range(4)] + [("ks_s", i) for i in range(4)], w=["qkj"])
    sc.add("dve", lambda e: e.tensor_reduce(qk, qkj[0:TS, :].rearrange("p (h d) -> p h d", h=8), axis=AX.X, op=ALU.add), r=["qkj"], w=["qk"])
    rs_tok = Sv(28, 32, F32)
    sc.barrier()
    pcross = 5; pc2 = 6
    cracc = Sv(24, 28, F32)[0:TS, :]
    sc.add("dve", lambda e: e.memset(cracc, 0.0), w=["cracc"])
    Ksel = Sv(32, 36, F32)[0:TS, :]
    for s_ in range(TS):
        sl = s_ % 3
        sc.add("sp", lambda e, s_=s_, sl=sl: e.dma_start(out=stsl[:, sl], in_=st_in[s_].rearrange("h d e -> d h e")), w=[("stsl", sl)], dma=("stin", sl))
        sc.add("dve", lambda e, s_=s_, sl=sl: e.tensor_scalar(Ksel, ks_s[0:TS, :], cs("eye16", 1, s_, TS), None, op0=ALU.mult),
               r=[("ks_s", i) for i in range(4)] + ["CS"], w=["Ksel"])
        for h in range(8):
            pb = pcross if h < 4 else pc2
            sc.add("pe", lambda e, s_=s_, sl=sl, h=h, pb=pb: e.matmul(P[pb][0:TS, (h % 4) * 128:(h % 4 + 1) * 128], Qsel[:, h, s_, :], stsl[:, sl, h, :],
                                                                     start=True, stop=True), r=["Qsel", ("stsl", sl)], w=[("P", pb)])
        for half in range(2):
            pb = (pcross, pc2)[half]
            sc.add("dve", lambda e, half=half, pb=pb: e.tensor_tensor(cracc[:, half * 512:(half + 1) * 512], cracc[:, half * 512:(half + 1) * 512], P[pb][0:TS, :], op=ALU.add),
                   r=[("P", pb), "cracc"], w=["cracc"])
        bo1 = 0; bo2 = 1
        for h in range(8):
            bo = bo1 if h < 4 else bo2
            sc.add("pe", lambda e, sl=sl, h=h, bo=bo: e.matmul(P[bo][:, (h % 4) * 128:(h % 4 + 1) * 128], Ksel[:, h * 128:(h + 1) * 128], vs_s[0:TS, h * 128:(h + 1) * 128],
                                                              start=True, stop=True), r=["Ksel"] + [("vs_s", i) for i in range(4)], w=[("P", bo)])
        for h in range(8):
            bo = bo1 if h < 4 else bo2
            sc.add("dve", lambda e, sl=sl, h=h, bo=bo: e.scalar_tensor_tensor(stsl[:, sl, h, :], stsl[:, sl, h, :], GAM[h], P[bo][:, (h % 4) * 128:(h % 4 + 1) * 128],
                                                                             op0=ALU.mult, op1=ALU.add), r=[("P", bo), ("stsl", sl)], w=[("stsl", sl)])
        sc.add("sp", lambda e, s_=s_, sl=sl: e.dma_start(out=sts_o[s_].rearrange("h d e -> d h e"), in_=stsl[:, sl]), r=[("stsl", sl)], dma=("stout", sl))
    sc.add("dve", lambda e: e.tensor_tensor(rs_tok[0:TS, :].rearrange("p (h e) -> p h e", h=8), vs_s[0:TS, :].rearrange("p (h e) -> p h e", h=8),
                                            vw(qk[:, 0:1], [[1, 8], [0, 128]]), op=ALU.mult), r=["qk"] + [("vs_s", i) for i in range(4)], w=["rs_tok"])
    for h in range(8):
        sc.add("dve", lambda e, h=h: e.scalar_tensor_tensor(rs_tok[0:TS, h * 128:(h + 1) * 128], cracc[:, h * 128:(h + 1) * 128], GAM[h],
                                                            rs_tok[0:TS, h * 128:(h + 1) * 128], op0=ALU.mult, op1=ALU.add), r=["cracc", "rs_tok"], w=["rs_tok"])
    rst_s = Mv(2200, 48, npart=TS).rearrange("p (h c) -> p h c", h=8)
    rmv_s = Mv(2248, 16, npart=TS).rearrange("p (h c) -> p h c", h=8)
    for h in range(8):
        sc.add("dve", lambda e, h=h: e.bn_stats(rst_s[:, h, :], rs_tok[0:TS, h * 128:(h + 1) * 128]), r=["rs_tok"], w=["rst_s"])
        sc.add("dve", lambda e, h=h: e.bn_aggr(rmv_s[:, h, :], rst_s[:, h, :]), r=["rst_s"], w=["rmv_s"])
    sc.add("act", lambda e: e.activation(out=rmv_s[:, :, 1:2], in_=rmv_s[:, :, 1:2], func=AF.Sqrt, bias=EPS, scale=1.0), r=["rmv_s"], w=["rmv_s"])
    sc.add("dve", lambda e: e.reciprocal(rmv_s[:, :, 1:2], rmv_s[:, :, 1:2]), r=["rmv_s"], w=["rmv_s"])
    for h in range(8):
        sc.add("dve", lambda e, h=h: e.tensor_scalar(rs_tok[0:TS, h * 128:(h + 1) * 128], rs_tok[0:TS, h * 128:(h + 1) * 128], rmv_s[:, h, 0:1], rmv_s[:, h, 1:2],
                                                     op0=ALU.subtract, op1=ALU.mult), r=["rmv_s", "rs_tok"], w=["rs_tok"])

    def dst_rs(i0, cnt, pv, pkey):
        for i in range(cnt):
            h = i0 + i
            sc.add("dve", lambda e, h=h, i=i: e.scalar_tensor_tensor(ccs[:, h, :], pv[:, i * TS:(i + 1) * TS], cs("gnB", 1, h), gTs[:, h, :], op0=ALU.mult, op1=ALU.mult),
                   r=[pkey, "CS", ("gTs", h)], w=[("ccs", h)])
    transposes_to(dst_rs, lambda i: rs_tok[0:TS, i * 128:(i + 1) * 128], 8, F32, lambda i: ["rs_tok", "CS"], None, np_in=TS, np_out=128)

    gemm_b(w_out, 1024, 8, 0, 2048, xk_cc, epi_res, xs=xs_ccs, epis=epis_res)
    sc.barrier()
    if stop_after == "mixB":
        return finish(nc, es, sc, H, Hs, y_cur, y_smp, S, Sv, P, CS, transposes_to, load_transpose, rmsnorm_fm, debug=True)

    rmsnorm_fm(H, Hk, X, Xk, 1, 512, 2)
    rmsnorm_fm(Hs, Hsk, Xs, Xsk, 1, TS, 1)
    Q = Sv(0, 32).rearrange("p (k t) -> p k t", k=16)
    qtok_s = Sv(48, 56, F32)[0:TS, :]

    def epi_cq(f, th, ps, pkey):
        eng = evac_eng()
        sc.add(eng, copy_op(eng, Q[:, f, th * 512:(th + 1) * 512], ps), r=[pkey], w=[("Q", f, th)])

    def epis_cq(s, ps, pkey):
        sc.add("act", lambda e: e.copy(qtok_s[:, s * 256:(s + 1) * 256], ps), r=[pkey], w=[("qtok_s", s // 2)])

    gemm_b(w_cq, 0, 16, 0, 2048, xk_X, epi_cq, xs=xs_Xs, epis=epis_cq, sform="a")
    sc.barrier()
    if stop_after == "xq":
        return finish(nc, es, sc, H, Hs, y_cur, y_smp, S, Sv, P, CS, transposes_to, load_transpose, rmsnorm_fm, debug=True)
    Xf = X[:].rearrange("p k t -> p (k t)")
    mH = Xf[:, 0:8192].bitcast(F32).rearrange("p (k t) -> p k t", k=16)
    mnT = Xf[:, 8192:12288].rearrange("p (k t) -> p k t", k=16)
    mkT = Xf[:, 0:4096].rearrange("p (k t) -> p k t", k=16)
    mvb = Xf[:, 4096:8192].rearrange("p (m f) -> p m f", m=2)
    pT = Xf[:, 12288:14336].rearrange("p (m t) -> p m t", m=2)
    mkst = Xf[:, 14336:16384].bitcast(F32).rearrange("p (s f) -> p s f", s=4)
    pbuf = Mv(0, 512).rearrange("p (s f) -> p s f", s=2)
    pbb = Mv(512, 256, BF16).rearrange("p (s f) -> p s f", s=2)

    def dst_mem(i0, cnt, tt, pv, pkey):
        eng = evac_eng()
        sc.add(eng, copy_op(eng, mH[:, i0:i0 + cnt, tt * 128:(tt + 1) * 128], pv.rearrange("p (c t) -> p c t", c=cnt)), r=[pkey], w=[("mH", k) for k in range(i0, i0 + cnt)])
    load_transpose(mem, 2, 128, dst_mem, None, stage_kb=(32, 48))
    rmsnorm_fm(mH, lambda k, th: ("mH", k), mnT, lambda k, th: ("mnT", k), 2, 256, 1)
    sc.barrier()
    mtog = {"n": 0}
    dfm = Defer(1)

    def xt_mn(k, tt):
        return mnT[:, k, tt * 128:(tt + 1) * 128], [("mnT", k)]

    def epi_mk(s, tt, ps, pkey, which="k"):
        tick()
        mtog["n"] += 1
        sl = mtog["n"] % 4
        sc.add("act", lambda e: e.copy(mkst[:, sl, :], ps), r=[pkey], w=[("mkst", sl)])
        dsto = (mk_o if which == "k" else mv_o)[tt * 128:(tt + 1) * 128, s * 256:(s + 1) * 256]
        sc.add("sp", lambda e: e.dma_start(out=dsto, in_=mkst[:, sl, :]), r=[("mkst", sl)], dma=("mko", sl))
        if which == "v":
            sc.add("dve", lambda e: e.tensor_copy(mvb[:, tt, s * 256:(s + 1) * 256], mkst[:, sl, :]), r=[("mkst", sl)], w=[("mvb", tt, s)])
        else:
            sc.add("dve", lambda e: e.tensor_copy(pbb[:, tt, :], mkst[:, sl, :]), r=[("mkst", sl)], w=[("pbb", tt)])

            def dst_k(i0, cnt, pv, pk2):
                sc.add("act", lambda e: e.copy(mkT[:, s * 2:s * 2 + 2, tt * 128:(tt + 1) * 128], pv.rearrange("p (c t) -> p c t", c=2)), r=[pk2], w=[("mkT", s, tt)])
            dfm.push(lambda: transposes_to(dst_k, lambda i: pbb[:, tt, i * 128:(i + 1) * 128], 2, BF16, lambda i: [("pbb", tt), "identb"], None))

    from collections import deque
    pend = deque()
    tkc = {"n": 0}

    def tick(every=3):
        tkc["n"] += 1
        if tkc["n"] % every == 0 and pend:
            pend.popleft()()

    SCL = 512.0 ** -0.5
    KV = Sv(32, 48).rearrange("p (s f) -> p s f", s=4)
    KVf = Sv(32, 48, F32).rearrange("p (s f) -> p s f", s=2)
    scs = Mv(2048, 128).rearrange("p (m c) -> p m c", m=2)
    junk5 = Mv(768, 512)
    qm = Mv(1280, 512, npart=TS)
    smT = Mv(1792, 256, npart=64)
    amx_s = Mv(2376, 4, npart=64)
    pTs = Mv(2176, 64, BF16).rearrange("p (m c) -> p m c", m=2)
    oTs = Mv(2240, 128, BF16).rearrange("p (c s) -> p c s", c=16)
    sc.add("dve", lambda e: e.memset(scs, 0.0), w=["scs"])
    qb = {"n": 0}


    def kpass(s_):
        for mh in range(2):
            sc.add("sp", lambda e: e.dma_start(out=KVf[:, mh, :], in_=ck[s_, mh * 128:(mh + 1) * 128, :]), w=[("KV", 2 * mh), ("KV", 2 * mh + 1)], dma=("KVf", mh))
        for h in range(4):
            sc.add("dve", lambda e: e.tensor_scalar(qm, qtok_s[:, h * 512:(h + 1) * 512], cs("eye16", 1, s_, TS), None, op0=ALU.mult),
                   r=[("qtok_s", i) for i in range(4)] + ["CS"], w=["qm"])
            qb["n"] += 1
            b_ = (4, 7)[qb["n"] % 2]
            sc.add("pe", lambda e: e.matmul(P[b_][:], ones32[:], qm, start=True, stop=True), r=["qm", "ones32"], w=[("P", b_)])
            for mh in range(2):
                sc.add("dve", lambda e: e.scalar_tensor_tensor(junk5, KVf[:, mh, h * 512:(h + 1) * 512], 1.0, P[b_][:], op0=ALU.mult, op1=ALU.mult,
                                                               accum_out=scs[:, mh, s_ * 4 + h:s_ * 4 + h + 1]),
                       r=[("KV", 2 * mh), ("KV", 2 * mh + 1), ("P", b_), "scs"], w=["junk5", "scs"])

    def ssoftmax():
        b_ = miscbank()
        for mh in range(2):
            sc.add("pe", lambda e: e.transpose(P[b_][0:64, mh * 128:(mh + 1) * 128], scs[:, mh, :], ident), r=["scs", "CS"], w=[("P", b_)])
        sc.add("dve", lambda e: e.tensor_reduce(amx_s[:, 0:1], P[b_][0:64, 0:256], axis=AX.X, op=ALU.max), r=[("P", b_)], w=["amxs"])
        sc.add("dve", lambda e: e.tensor_scalar(amx_s[:, 1:2], amx_s[:, 0:1], -SCL, None, op0=ALU.mult), r=["amxs"], w=["amxs"])
        sc.add("act", lambda e: e.activation(out=smT, in_=P[b_][0:64, 0:256], func=AF.Exp, bias=amx_s[:, 1:2], scale=SCL, accum_out=amx_s[:, 2:3]),
               r=[("P", b_), "amxs"], w=["smT", "amxs"])
        sc.add("dve", lambda e: e.reciprocal(amx_s[:, 3:4], amx_s[:, 2:3]), r=["amxs"], w=["amxs"])
        sc.add("dve", lambda e: e.tensor_scalar(smT, smT, amx_s[:, 3:4], None, op0=ALU.mult), r=["smT", "amxs"], w=["smT"])
        b2_ = miscbank()
        for mh in range(2):
            sc.add("pe", lambda e: e.transpose(P[b2_][:, mh * 64:(mh + 1) * 64], smT[:, mh * 128:(mh + 1) * 128], CS[0:64, CO["ident"]:CO["ident"] + 64]),
                   r=["smT", "CS"], w=[("P", b2_)])
        sc.add("dve", lambda e: e.tensor_copy(Mv(2176, 64, BF16), P[b2_][:, 0:128]), r=[("P", b2_)], w=["pTs"])

    def vpass(s_):
        for mh in range(2):
            sl = (2 * s_ + mh) % 4
            sc.add("pool", lambda e: e.dma_start(out=KV[:, sl, :], in_=cv[s_, mh * 128:(mh + 1) * 128, :]), w=[("KV", sl)], dma=("KV", sl))
        for c in range(16):
            hh = c // 4
            for mh in range(2):
                sl = (2 * s_ + mh) % 4
                sc.add("pe", lambda e: e.matmul(P[4][:, c * TS + s_:c * TS + s_ + 1], KV[:, sl, c * 128:(c + 1) * 128], pTs[:, mh, s_ * 4 + hh:s_ * 4 + hh + 1],
                                                start=(mh == 0), stop=(mh == 1)), r=[("KV", sl), "pTs"], w=[("P", 4)])

    for s_ in range(TS):
        pend.append(lambda s_=s_: kpass(s_))
    pend.append(ssoftmax)
    for s_ in range(TS):
        pend.append(lambda s_=s_: vpass(s_))

    gemm_a(w_ck, 0, 16, 0, 2048, 2, xt_mn, epi_mk)
    dfm.flush()
    gemm_a(w_cv, 0, 16, 0, 2048, 2, xt_mn, lambda s, tt, ps, pkey: epi_mk(s, tt, ps, pkey, "v"))

    amx = Mv(2368, 4)
    dfp = Defer(1)
    for h in range(4):
        for tt in range(NT):
            b = miscbank()
            for dc in range(4):
                kq = 4 * h + dc
                sc.add("pe", lambda e: e.matmul(P[b][:, 0:256], Q[:, kq, tt * 128:(tt + 1) * 128], mkT[:, kq, :], start=(dc == 0), stop=(dc == 3)),
                       r=[("Q", kq, tt // 4)] + [("mkT", kq // 2, m_) for m_ in range(2)], w=[("P", b)])
            sl = tt % 2
            sc.add("dve", lambda e: e.tensor_reduce(amx[:, 0:1], P[b][:, 0:256], axis=AX.X, op=ALU.max), r=[("P", b)], w=["amx0"])
            sc.add("dve", lambda e: e.tensor_scalar(amx[:, 1:2], amx[:, 0:1], -SCL, None, op0=ALU.mult), r=["amx0"], w=["amx1"])
            sc.add("act", lambda e: e.activation(out=pbuf[:, sl, :], in_=P[b][:, 0:256], func=AF.Exp, bias=amx[:, 1:2], scale=SCL, accum_out=amx[:, 2:3]),
                   r=[("P", b), "amx1"], w=[("pbuf", sl), "amx2"])
            sc.add("dve", lambda e: e.reciprocal(amx[:, 3:4], amx[:, 2:3]), r=["amx2"], w=["amx3"])
            sc.add("dve", lambda e: e.tensor_scalar(pbb[:, sl, :], pbuf[:, sl, :], amx[:, 3:4], None, op0=ALU.mult), r=[("pbuf", sl), "amx3"], w=[("pbb", sl)])

            def dst_p(i0, cnt, pv, pk2, tt=tt):
                sc.add("act", lambda e: e.copy(pT[:, :, tt * 128:(tt + 1) * 128], pv.rearrange("p (m t) -> p m t", m=2)), r=[pk2], w=[("pT", tt // 4)])
            dfp.push(lambda dst_p=dst_p, sl=sl: transposes_to(dst_p, lambda i: pbb[:, sl, i * 128:(i + 1) * 128], 2, BF16, lambda i: [("pbb", sl), "identb"], None))
            tick()
        dfp.flush()
        for dc in range(4):
            kq = 4 * h + dc
            for th in range(2):
                b = mainbank()
                for mh in range(2):
                    sc.add("pe", lambda e: e.matmul(P[b][:], mvb[:, mh, kq * 128:(kq + 1) * 128], pT[:, mh, th * 512:(th + 1) * 512], start=(mh == 0), stop=(mh == 1)),
                           r=[("mvb", mh, kq // 2), ("pT", th)], w=[("P", b)])
                eng = evac_eng()
                sc.add(eng, copy_op(eng, Q[:, kq, th * 512:(th + 1) * 512], P[b][:]), r=[("P", b)], w=[("Q", kq, th)])
                tick()
    while pend:
        pend.popleft()()
    sc.add("dve", lambda e: e.tensor_copy(Mv(2240, 128, BF16), P[4][:, 0:256]), r=[("P", 4)], w=["oTs"])

    def xk_Q(k, th):
        return Q[:, k, th * 512:(th + 1) * 512], [("Q", k, th)]

    def xs_oTs(k):
        return oTs[:, k, :], ["oTs"]

    gemm_b(w_co, 0, 16, 0, 2048, xk_Q, epi_res, xs=xs_oTs, epis=epis_res)
    sc.barrier()
    if stop_after == "xattn":
        return finish(nc, es, sc, H, Hs, y_cur, y_smp, S, Sv, P, CS, transposes_to, load_transpose, rmsnorm_fm, debug=True)

    rmsnorm_fm(H, Hk, X, Xk, 3, 512, 2)
    rmsnorm_fm(Hs, Hsk, Xs, Xsk, 3, TS, 1)
    sc.barrier()
    hid = Sv(0, 32).rearrange("p (k t) -> p k t", k=16)
    rl = Sv(32, 36, F32).rearrange("p (s f) -> p s f", s=2)
    hids = Mv(0, 128, BF16).rearrange("p (c s) -> p c s", c=16)
    rls = Mv(128, 16)
    ftog = {"n": 0}
    for g in range(4):
        def epi_f1(f, th, ps, pkey):
            ftog["n"] += 1
            sl = ftog["n"] % 2
            sc.add("act", lambda e: e.activation(out=rl[:, sl, :], in_=ps, func=AF.Relu), r=[pkey], w=[("rl", sl)])
            sc.add("pool", lambda e: e.tensor_tensor(hid[:, f, th * 512:(th + 1) * 512], rl[:, sl, :], rl[:, sl, :], op=ALU.mult), r=[("rl", sl)], w=[("hid", f, th)])

        def epis_f1(f, ps, pkey):
            sc.add("act", lambda e: e.activation(out=rls, in_=ps, func=AF.Relu), r=[pkey], w=["rls"])
            sc.add("dve", lambda e: e.tensor_tensor(hids[:, f, :], rls, rls, op=ALU.mult), r=["rls"], w=[("hids", f)])

        gemm_b(w_ff1, 0, 16, g * 2048, 2048, xk_X, epi_f1, xs=xs_Xs, epis=epis_f1)

        def xk_hid(k, th):
            return hid[:, k, th * 512:(th + 1) * 512], [("hid", k, th)]

        def xs_hids(k):
            return hids[:, k, :], [("hids", k)]

        gemm_b(w_ff2, g * 2048, 16, 0, 2048, xk_hid, epi_res, xs=xs_hids, epis=epis_res)
    sc.barrier()
    return finish(nc, es, sc, H, Hs, y_cur, y_smp, S, Sv, P, CS, transposes_to, load_transpose, rmsnorm_fm, debug=False)


def finish(nc, es, sc, H, Hs, y_cur, y_smp, S, Sv, P, CS, transposes_to, load_transpose, rmsnorm_fm, debug):
    X_dummy = None
    if not debug:
        sq = Sv(0, 32).rearrange("p (k t) -> p k t", k=16)
        rmsnorm_fm_out(sc, H, sq, 512, 2, 4, "H", Sv, P, CS)
        sqs = Sv(32, 33).rearrange("p (k t) -> p k t", k=16)
        rmsnorm_fm_out(sc, Hs, sqs, TS, 1, 4, "Hs", Sv, P, CS)
    yst = Sv(36, 52, F32).rearrange("p (s f) -> p s f", s=2)
    cnt = {"n": 0}
    for tt in range(NT + 1):
        smp = (tt == NT)
        sl = tt % 2
        npo = TS if smp else 128

        def dst_y(i0, c, pv, pkey, sl=sl, npo=npo):
            cnt["n"] += 1
            eng = "act" if cnt["n"] % 2 else "dve"
            o = yst[0:npo, sl, i0 * 128:(i0 + c) * 128]
            if eng == "act":
                sc.add("act", lambda e: e.copy(o, pv), r=[pkey], w=[("yst", sl)])
            else:
                sc.add("dve", lambda e: e.tensor_copy(o, pv), r=[pkey], w=[("yst", sl)])
        if smp:
            src_of = lambda i: Hs[:, i, :]
            keys = lambda i: [("Hs", i), "CS"]
        else:
            src_of = lambda i, tt=tt: H[:, i, tt * 128:(tt + 1) * 128]
            keys = lambda i, tt=tt: [("H", i, tt // 4), "CS"]
        transposes_to(dst_y, src_of, 16, F32, keys, None, np_in=128, np_out=npo)
        dsto = y_smp[:, :] if smp else y_cur[tt * 128:(tt + 1) * 128, :]
        sc.add("sp", lambda e, dsto=dsto, sl=sl, npo=npo: e.dma_start(out=dsto, in_=yst[0:npo, sl, :]), r=[("yst", sl)], dma=("yout", sl))
    sc.emit(nc, es)
    return nc, es


def rmsnorm_fm_out(sc, Hbuf, sq, ncols, nhalf, gi, hname, Sv, P, CS):
    rst = Sv(52, 56, F32).rearrange("p (h t) -> p h t", h=2)
    for th in range(nhalf):
        cols = slice(th * ncols, (th + 1) * ncols)
        hkey = (lambda k: (hname, k, th)) if hname == "H" else (lambda k: (hname, k))
        b = 5 + th
        for k in range(16):
            sc.add("act", lambda e, o=sq[:, k, cols], i=Hbuf[:, k, cols]: e.activation(out=o, in_=i, func=AF.Square), r=[hkey(k)], w=[("sq", hname, k, th)])
            sc.add("pe", lambda e, o=P[b][:, 0:ncols], r_=sq[:, k, cols], st=(k == 0), sp=(k == 15): e.matmul(o, ONESB[0][:], r_, start=st, stop=sp),
                   r=[("sq", hname, k, th), "onesb"], w=[("P", b)])
        rv = rst[:, th, 0:ncols]
        sc.add("act", lambda e, o=rv, i=P[b][:, 0:ncols]: e.activation(out=o, in_=i, func=AF.Sqrt, bias=EPS, scale=1.0 / D), r=[("P", b)], w=[("rstf", th)])
        sc.add("dve", lambda e, o=rv: e.reciprocal(o, o), r=[("rstf", th)], w=[("rstf", th)])
        for k in range(16):
            g = CS[:, CO["gpk"] + gi * 16 + k:CO["gpk"] + gi * 16 + k + 1]
            sc.add("dve", lambda e, o=Hbuf[:, k, cols], g=g, rv=rv: e.scalar_tensor_tensor(o, o, g, rv, op0=ALU.mult, op1=ALU.mult),
                   r=[hkey(k), ("rstf", th), "CS"], w=[hkey(k)])


ONESB = [None]


def _consts(hf):
    c = np.zeros((128, NCST), np.float32)

    def put(name, arr):
        arr = np.asarray(arr, np.float32)
        c[:arr.shape[0], CO[name]:CO[name] + arr.shape[1]] = arr

    put("ident", np.eye(128))
    j = np.arange(128)
    m = (j[:, None] <= j[None, :]).astype(np.float32)
    put("maskT", np.concatenate([m, m], axis=1))
    half = 64
    freqs = (10000.0 ** (-np.arange(half, dtype=np.float32) / half)).astype(np.float32)

    def rope(pos):
        ang = pos.astype(np.float32)[:, None] * freqs[None, :]
        return np.cos(ang).astype(np.float32), np.sin(ang).astype(np.float32)

    pos_c = (hf * T + np.arange(T)).astype(np.float32)
    cc_, ss_ = rope(pos_c)
    put("cosc", cc_.reshape(8, 128, 64).transpose(1, 0, 2).reshape(128, 512))
    put("sinc", ss_.reshape(8, 128, 64).transpose(1, 0, 2).reshape(128, 512))
    cp, sp_ = rope(np.arange(T).astype(np.float32))
    ropep = np.concatenate([cp.reshape(8, 128, 64).transpose(1, 0, 2).reshape(128, 512),
                            sp_.reshape(8, 128, 64).transpose(1, 0, 2).reshape(128, 512)], axis=1).astype(np.float32)
    c16, s16 = rope(np.full((128,), 16384.0, np.float32))
    put("coss", c16)
    put("sins", s16)
    g = np.array(GAM, np.float64)
    t = np.arange(T, dtype=np.float64)
    qs = np.exp(np.log(g)[None, :] * t[:, None])
    ks = np.exp(-np.log(g)[None, :] * t[:, None]) * (128.0 ** -0.5)
    put("qs", qs.reshape(8, 128, 8).transpose(1, 0, 2).reshape(128, 64))
    put("ks", ks.reshape(8, 128, 8).transpose(1, 0, 2).reshape(128, 64))
    put("gfin", np.tile((g ** 1023)[None, :], (128, 1)))
    put("gini", np.tile((g ** 1024)[None, :], (128, 1)))
    put("eye16", np.eye(16))
    put("eyeq", np.tile(np.eye(16).reshape(1, 256), (128, 1)))
    return c, ropep


_CACHE = {}


def kernel(x_prompt, x_sample, mem_prompt, cache_mem_k, cache_mem_v, state_ret,
           norm1_g, w_in, sgu_norm_g, sgu_w_s, sgu_b, ret_gn_g, w_out, norm2_g,
           mem_norm_g, w_cq, w_ck, w_cv, w_co, norm3_g, w_ff1, w_ff2, final_norm_g):
    f = lambda a: np.ascontiguousarray(np.asarray(a, dtype=np.float32))
    x_prompt, x_sample, mem_prompt = f(x_prompt), f(x_sample), f(mem_prompt)
    cache_mem_k, cache_mem_v, state_ret = f(cache_mem_k), f(cache_mem_v), f(state_ret)
    if "nc" not in _CACHE:
        _CACHE["nc"] = build()
    nc, _es = _CACHE["nc"]
    gains = [f(norm1_g)[0], f(norm2_g)[0], f(mem_norm_g)[0], f(norm3_g)[0], f(final_norm_g)]
    gpk = np.stack([gn.reshape(16, 128).T for gn in gains], axis=1).reshape(128, 80)
    gA = f(sgu_norm_g)[0].reshape(8, 128).T
    gnB = f(ret_gn_g)[0].reshape(8, 128).T
    ws = f(sgu_w_s)[0]
    sbias = f(sgu_b)[0]
    ws00 = np.tile(ws[:, 0, 0][None, :], (128, 1))
    b0 = np.tile(sbias[:, 0][None, :], (128, 1))
    wsT = np.ascontiguousarray(ws.transpose(2, 0, 1).reshape(128, 1024))
    b_bc = np.ascontiguousarray(np.tile(sbias.reshape(1, 1024), (128, 1)))
    zeros_prev = np.zeros((T, D), np.float32)
    shared = dict(w_in=f(w_in)[0], w_out=f(w_out)[0], w_cq=f(w_cq)[0], w_ck=f(w_ck)[0], w_cv=f(w_cv)[0], w_co=f(w_co)[0],
                  w_ff1=f(w_ff1)[0], w_ff2=f(w_ff2)[0], wsT=wsT, b_bc=b_bc)
    in_maps = []
    for c in range(8):
        b, hf = c // 2, c % 2
        cst, ropep = _consts(hf)
        for name, arr in (("gpk", gpk), ("gA", gA), ("gnB", gnB), ("ws00", ws00), ("b0", b0)):
            cst[:, CO[name]:CO[name] + arr.shape[1]] = arr
        m = dict(shared)
        m.update(
            x_cur=np.ascontiguousarray(x_prompt[b, hf * T:(hf + 1) * T]),
            x_prev=np.ascontiguousarray(x_prompt[b, 0:T]) if hf == 1 else zeros_prev,
            x_smp=np.ascontiguousarray(x_sample[c * TS:(c + 1) * TS, 0]),
            mem=np.ascontiguousarray(mem_prompt[b]),
            ck=np.ascontiguousarray(cache_mem_k[0, c * TS:(c + 1) * TS].reshape(TS, 256, D)),
            cv=np.ascontiguousarray(cache_mem_v[0, c * TS:(c + 1) * TS].reshape(TS, 256, D)),
            st_in=np.ascontiguousarray(state_ret[0, c * TS:(c + 1) * TS]),
            cst=cst, ropep=ropep,
        )
        in_maps.append(m)
    if _CACHE.get("test_cores"):
        n = _CACHE["test_cores"]
        res = run_bass_kernel_spmd(nc, in_maps[:n], core_ids=list(range(n)), trace=bool(_CACHE.get("trace")))
        _CACHE["exec_ns"] = res.exec_time_ns
        return res.results
    res = run_bass_kernel_spmd(nc, in_maps, core_ids=list(range(8)))
    R = res.results
    y_prompt = np.zeros((4, 2048, D), np.float32)
    y_sample = np.zeros((128, 1, D), np.float32)
    mk = np.zeros((1, 4, 256, 4, 512), np.float32)
    mv = np.zeros((1, 4, 256, 4, 512), np.float32)
    sp = np.zeros((1, 4, 8, 128, 128), np.float32)
    ss = np.zeros((1, 128, 8, 128, 128), np.float32)
    cvs = np.zeros((1, 128, 1, 8, 128), np.float32)
    for c in range(8):
        b, hf = c // 2, c % 2
        r = R[c]
        y_prompt[b, hf * T:(hf + 1) * T] = r["y_cur"]
        y_sample[c * TS:(c + 1) * TS, 0] = r["y_smp"]
        ss[0, c * TS:(c + 1) * TS] = r["sts_o"]
        cvs[0, c * TS:(c + 1) * TS, 0] = r["cvs_o"].reshape(TS, 8, 128)
        if hf == 0:
            mk[0, b] = r["mk_o"].reshape(256, 4, 512)
            mv[0, b] = r["mv_o"].reshape(256, 4, 512)
        else:
            sp[0, b] = r["stp_o"]
    return (y_prompt, y_sample, mk, mv, sp, ss, cvs)
```

```python
import contextlib
import numpy as np
import concourse.bass as bass
import concourse.mybir as mybir
from concourse.bass_utils import run_bass_kernel_spmd

F32 = mybir.dt.float32
BF16 = mybir.dt.bfloat16
AF = mybir.ActivationFunctionType
ALU = mybir.AluOpType
AX = mybir.AxisListType

D = 2048
T = 1024
NT = 8
TS = 16
EPS = 1e-6
GAM = [1.0 - 2.0 ** (-5 - h) for h in range(8)]
ENGS = ("pe", "act", "dve", "pool", "sp")

CO = {}
_c = 0
for _n, _w in (("ident", 128), ("maskT", 256), ("gpk", 80), ("gA", 8), ("gnB", 8),
               ("ws00", 8), ("b0", 8), ("cosc", 512), ("sinc", 512),
               ("coss", 64), ("sins", 64), ("qs", 64), ("ks", 64), ("gfin", 8), ("gini", 8),
               ("eye16", 16), ("eyeq", 256)):
    CO[_n] = _c
    _c += _w
NCST = _c


class Op:
    __slots__ = ("eng", "fn", "deps", "signal", "sigval", "dma", "dval", "idx")


class _Rec:
    def __init__(self):
        self.call = None

    def __getattr__(self, name):
        def f(*a, **k):
            self.call = (name, a, k)
            return None
        return f


class Sched:
    def __init__(self):
        self.ops = []
        self.by_eng = {e: [] for e in ENGS}
        self.lastw = {}
        self.readers = {}
        self.dcount = {}
        self.last_real = {e: None for e in ENGS}
        self.dma_since = []

    def add(self, eng, fn, r=(), w=(), dma=None):
        op = Op()
        if fn is not None:
            rec = _Rec()
            fn(rec)
            assert rec.call is not None
            call = rec.call
            fn = lambda e, call=call: getattr(e, call[0])(*call[1], **call[2])
        op.eng, op.fn, op.dma, op.signal, op.sigval, op.dval = eng, fn, dma, False, 0, 0
        op.idx = len(self.ops)
        deps = {}

        def adddep(d):
            if d is None or d is op:
                return
            if d.dma is None and dma is None and d.eng == "pe" and eng == "pe":
                return
            key = ("d", id(d)) if d.dma is not None else ("e", d.eng)
            o = deps.get(key)
            if o is None or o.idx < d.idx:
                deps[key] = d

        for k in list(r) + list(w):
            adddep(self.lastw.get(k))
        for k in w:
            rd = self.readers.get(k)
            if rd:
                for d in rd.values():
                    adddep(d)
        op.deps = list(deps.values())
        for d in op.deps:
            d.signal = True
        for k in w:
            self.lastw[k] = op
            self.readers[k] = {}
        for k in r:
            rk = ("d", op.idx) if dma is not None else ("e", eng)
            self.readers.setdefault(k, {})[rk] = op
        if dma is not None:
            self.dcount[dma] = self.dcount.get(dma, 0) + 1
            op.dval = 16 * self.dcount[dma]
            self.dma_since.append(op)
        self.ops.append(op)
        self.by_eng[eng].append(op)
        if fn is not None:
            self.last_real[eng] = op
        return op

    def barrier(self):
        lasts = [self.last_real[e] for e in ENGS if self.last_real[e] is not None]
        dmas = list(self.dma_since)
        self.dma_since = []
        for e in ENGS:
            op = Op()
            op.eng, op.fn, op.dma, op.signal, op.sigval, op.dval = e, None, None, False, 0, 0
            op.idx = len(self.ops)
            deps = []
            for d in lasts:
                if d.eng != e or d.dma is not None:
                    deps.append(d)
            chan = {}
            for d in dmas:
                if d.dma not in chan or chan[d.dma].idx < d.idx:
                    chan[d.dma] = d
            deps += [d for d in chan.values() if d not in deps]
            op.deps = deps
            for d in deps:
                d.signal = True
            self.ops.append(op)
            self.by_eng[e].append(op)

    def emit(self, nc, es):
        esem = {e: es.enter_context(nc.semaphore("s_" + e)) for e in ENGS}
        dsem = {ch: es.enter_context(nc.semaphore("d_%s" % str(ch))) for ch in self.dcount}
        for e in ENGS:
            c = 0
            for op in self.by_eng[e]:
                if op.signal and op.dma is None and op.fn is not None:
                    c += 1
                    op.sigval = c
        block = es.enter_context(nc.Block())

        def run(ename, eng):
            waited = {}
            for op in self.by_eng[ename]:
                for d in op.deps:
                    if d.dma is not None:
                        sem, val = dsem[d.dma], d.dval
                    else:
                        sem, val = esem[d.eng], d.sigval
                    if waited.get(id(sem), 0) < val:
                        eng.wait_ge(sem, val)
                        waited[id(sem)] = val
                if op.fn is None:
                    continue
                ins = op.fn(eng)
                if op.dma is not None:
                    ins.then_inc(dsem[op.dma], 16)
                elif op.signal:
                    ins.then_inc(esem[ename], 1)
            if ename == "sp":
                for ch, n in self.dcount.items():
                    eng.wait_ge(dsem[ch], 16 * n)

        block.tensor(lambda t: run("pe", t))
        block.scalar(lambda t: run("act", t))
        block.vector(lambda t: run("dve", t))
        block.gpsimd(lambda t: run("pool", t))
        block.sync(lambda t: run("sp", t))


class Defer:
    def __init__(self, depth):
        self.q = []
        self.depth = depth

    def push(self, thunk):
        self.q.append(thunk)
        while len(self.q) > self.depth:
            self.q.pop(0)()

    def flush(self):
        while self.q:
            self.q.pop(0)()


def vw(base, dims, npart=None):
    p = base.ap[0]
    return bass.AP(base.tensor, base.offset, [[p[0], npart if npart else p[1]]] + [list(d) for d in dims])


def build(stop_after=None):
    nc = bass.Bass("TRN2", target_bir_lowering=False)
    es = contextlib.ExitStack()
    sc = Sched()

    def din(name, shape):
        return nc.dram_tensor(name, list(shape), F32, kind="ExternalInput").ap()

    def dout(name, shape):
        return nc.dram_tensor(name, list(shape), F32, kind="ExternalOutput").ap()

    x_cur = din("x_cur", (T, D)); x_prev = din("x_prev", (T, D)); x_smp = din("x_smp", (TS, D))
    mem = din("mem", (256, D)); ck = din("ck", (TS, 256, D)); cv = din("cv", (TS, 256, D))
    st_in = din("st_in", (TS, 8, 128, 128)); cst_d = din("cst", (128, NCST)); wsT_d = din("wsT", (128, 1024)); ropep_d = din("ropep", (128, 1024)); bbc_d = din("b_bc", (128, 1024))
    w_in = din("w_in", (D, 6144)); w_out = din("w_out", (D, D)); w_cq = din("w_cq", (D, D))
    w_ck = din("w_ck", (D, D)); w_cv = din("w_cv", (D, D)); w_co = din("w_co", (D, D))
    w_ff1 = din("w_ff1", (D, 8192)); w_ff2 = din("w_ff2", (8192, D))
    y_cur = dout("y_cur", (T, D)); y_smp = dout("y_smp", (TS, D)); mk_o = dout("mk_o", (256, D)); mv_o = dout("mv_o", (256, D))
    stp_o = dout("stp_o", (8, 128, 128)); sts_o = dout("sts_o", (TS, 8, 128, 128)); cvs_o = dout("cvs_o", (TS, 1024))

    def sb(name, shape, dt):
        return es.enter_context(nc.sbuf_tensor(name, list(shape), dt))

    H = sb("H", (128, 16, T), F32)
    X = sb("X", (128, 16, T), BF16)
    S = sb("S", (128, 28672), BF16)
    W = sb("W", (128, 3, 4096), BF16)
    CS = sb("CS", (128, NCST), F32)
    wsTb = sb("wsTb", (128, 8, 128), BF16)
    identb = sb("identb", (128, 128), BF16)
    onesb = sb("onesb", (128, 128), BF16)
    ones32 = sb("ones32", (16, 128), F32)
    Hs = sb("Hs", (128, 16, TS), F32)
    Xs = sb("Xs", (128, 16, TS), BF16)
    ccs = sb("ccs", (128, 8, TS), BF16)
    smx = sb("smx", (128, 64), F32)
    M = sb("M", (128, 3328), F32)
    rstb = sb("rstb", (128, 512), F32)
    ONESB[0] = onesb

    def Mv(o, n, dt=F32, npart=128):
        a = M[0:npart, o:o + n]
        return a.bitcast(BF16) if dt == BF16 else a
    P = [es.enter_context(nc.psum_tensor("P%d" % i, [128, 512], F32)) for i in range(8)]

    def cs(name, w, c0=0, npart=128):
        o = CO[name] + c0
        return CS[0:npart, o:o + w]

    ident = cs("ident", 128)

    def Sv(kb0, kb1, dt=BF16):
        a = S[:, kb0 * 512:kb1 * 512]
        return a.bitcast(F32) if dt == F32 else a

    rr = {"main": 0, "misc": 0, "smp": 0}

    def mainbank():
        b = rr["main"] % 4
        rr["main"] += 1
        return b

    def miscbank():
        b = 5 + rr["misc"] % 2
        rr["misc"] += 1
        return b

    def smpreg():
        r_ = rr["smp"] % 2
        rr["smp"] += 1
        return (4, 7)[r_]

    tog = {"n": 0}

    def evac_eng():
        tog["n"] += 1
        return "act" if tog["n"] % 2 else "dve"

    def copy_op(eng, out, in_):
        if eng == "act":
            return lambda e: e.copy(out, in_)
        return lambda e: e.tensor_copy(out, in_)

    wplan = []
    wstate = {"issued": 0, "used": 0}

    def wdeclare(dram, r0, kc, c0, fc):
        wplan.append((dram, r0, kc, c0, fc))

    def wissue_upto(n):
        while wstate["issued"] < min(n, len(wplan)):
            i = wstate["issued"]
            dram, r0, kc, c0, fc = wplan[i]
            slot = i % 3
            dst = W[:, slot, :].rearrange("p (k f) -> p k f", k=kc)
            src = dram[r0:r0 + kc * 128, c0:c0 + fc].rearrange("(k p) f -> p k f", p=128)
            sc.add("pool", lambda e, dst=dst, src=src: e.dma_start(out=dst, in_=src), w=[("W", slot)], dma=("W", slot))
            wstate["issued"] += 1

    def wnext(dram, r0, kc, c0, fc):
        i = wstate["used"]
        assert wplan[i][1:] == (r0, kc, c0, fc) and wplan[i][0].tensor.name == dram.tensor.name, (i, wplan[i][1:], (r0, kc, c0, fc))
        wissue_upto(i + 3)
        wstate["used"] += 1
        slot = i % 3
        return W[:, slot, :].rearrange("p (k f) -> p k f", k=kc), ("W", slot)

    def gemm_b(dram, r0, kc, c0, ncols, xk, epi, xs=None, epis=None, sform="b"):
        fc = 4096 // kc
        for s in range(ncols // fc):
            wv, wkey = wnext(dram, r0, kc, c0 + s * fc, fc)
            for j in range(fc // 128):
                f = s * (fc // 128) + j
                banks = [mainbank(), mainbank()]
                sreg = smpreg() if (xs is not None and sform == "b") else None
                for k in range(kc):
                    lhsT = wv[:, k, j * 128:(j + 1) * 128]
                    for th in range(2):
                        xa, xkeys = xk(k, th)
                        sc.add("pe", lambda e, o=P[banks[th]][:], l=lhsT, r_=xa, st=(k == 0), sp=(k == kc - 1):
                               e.matmul(o, l, r_, start=st, stop=sp), r=[wkey] + xkeys, w=[("P", banks[th])])
                    if sreg is not None:
                        xa, xkeys = xs(k)
                        sc.add("pe", lambda e, o=P[sreg][:, 0:16], l=lhsT, r_=xa, st=(k == 0), sp=(k == kc - 1):
                               e.matmul(o, l, r_, start=st, stop=sp), r=[wkey] + xkeys, w=[("P", sreg)])
                for th in range(2):
                    epi(f, th, P[banks[th]][:], ("P", banks[th]))
                if sreg is not None:
                    epis(f, P[sreg][:, 0:16], ("P", sreg))
            if xs is not None and sform == "a":
                b = miscbank()
                for k in range(kc):
                    xa, xkeys = xs(k)
                    sc.add("pe", lambda e, o=P[b][0:TS, 0:fc], l=xa, r_=wv[:, k, :], st=(k == 0), sp=(k == kc - 1):
                           e.matmul(o, l, r_, start=st, stop=sp), r=[wkey] + xkeys, w=[("P", b)])
                epis(s, P[b][0:TS, 0:fc], ("P", b))

    def gemm_a(dram, r0, kc, c0, ncols, ntiles, xt, epi, xs=None, epis=None):
        fc = 4096 // kc
        for s in range(ncols // fc):
            wv, wkey = wnext(dram, r0, kc, c0 + s * fc, fc)
            for tt in range(ntiles):
                b = mainbank()
                for k in range(kc):
                    xa, xkeys = xt(k, tt)
                    sc.add("pe", lambda e, o=P[b][:, 0:fc], l=xa, r_=wv[:, k, :], st=(k == 0), sp=(k == kc - 1):
                           e.matmul(o, l, r_, start=st, stop=sp), r=[wkey] + xkeys, w=[("P", b)])
                epi(s, tt, P[b][:, 0:fc], ("P", b))
            if xs is not None:
                b = miscbank()
                for k in range(kc):
                    xa, xkeys = xs(k)
                    sc.add("pe", lambda e, o=P[b][0:TS, 0:fc], l=xa, r_=wv[:, k, :], st=(k == 0), sp=(k == kc - 1):
                           e.matmul(o, l, r_, start=st, stop=sp), r=[wkey] + xkeys, w=[("P", b)])
                epis(s, P[b][0:TS, 0:fc], ("P", b))

    def gemm_s(dram, r0, kc, c0, ncols, xs, epis):
        fc = 4096 // kc
        for s in range(ncols // fc):
            wv, wkey = wnext(dram, r0, kc, c0 + s * fc, fc)
            for j in range(fc // 128):
                f = s * (fc // 128) + j
                sreg = smpreg()
                for k in range(kc):
                    xa, xkeys = xs(k)
                    sc.add("pe", lambda e: e.matmul(P[sreg][:, 0:16], wv[:, k, j * 128:(j + 1) * 128], xa, start=(k == 0), stop=(k == kc - 1)),
                           r=[wkey] + xkeys, w=[("P", sreg)])
                epis(f, P[sreg][:, 0:16], ("P", sreg))

    def transposes_to(dst_of, src_of, n, dt, srckeys, dstkeys, np_in=128, np_out=128, evac=None):
        grp = 4
        for i0 in range(0, n, grp):
            cnt = min(grp, n - i0)
            b = miscbank()
            if dt == F32:
                pv = P[b][0:np_out, :]
                idn = CS[0:np_in, CO["ident"]:CO["ident"] + np_in]
            else:
                pv = P[b][:].bitcast(BF16)[0:np_out, 0:512]
                idn = identb[0:np_in, 0:np_in]
            for i in range(cnt):
                sc.add("pe", lambda e, o=pv[:, i * np_in:(i + 1) * np_in], s_=src_of(i0 + i), idn=idn: e.transpose(o, s_, idn),
                       r=srckeys(i0 + i), w=[("P", b)])
            dst_of(i0, cnt, pv[:, 0:cnt * np_in], ("P", b))

    sc.add("sp", lambda e: e.dma_start(out=CS[:], in_=cst_d[:, :]), w=["CS"], dma="cst")
    wst32 = Sv(16, 20, F32)
    sc.add("sp", lambda e: e.dma_start(out=wst32, in_=wsT_d[:, :]), w=["wst32"], dma="cst2")
    sc.add("dve", lambda e: e.tensor_copy(identb[:], ident), r=["CS"], w=["identb"])
    sc.add("dve", lambda e: e.memset(onesb[:], 1.0), w=["onesb"])
    sc.add("dve", lambda e: e.memset(ones32[:], 1.0), w=["ones32"])
    sc.add("dve", lambda e: e.tensor_tensor(wsTb[:], wst32.rearrange("p (h i) -> p h i", h=8),
                                            vw(cs("maskT", 128), [[0, 8], [1, 128]]), op=ALU.mult), r=["CS", "wst32"], w=["wsTb"])

    sc.barrier()
    ropep = Sv(32, 36, F32)
    sc.add("sp", lambda e: e.dma_start(out=ropep, in_=ropep_d[:, :]), w=["ropep"], dma="cst3")
    for hp in range(4):
        wdeclare(w_in, 0, 16, 3072 + hp * 256, 256)
        wdeclare(w_in, 0, 16, 4096 + hp * 256, 256)
    for s in range(4):
        wdeclare(w_in, 0, 16, 1024 + s * 256, 256)
    for s in range(4):
        wdeclare(w_in, 0, 16, s * 256, 256)
    for s in range(4):
        wdeclare(w_out, 0, 8, s * 512, 512)
    for hp in range(4):
        for base in (2048, 3072, 4096, 5120):
            wdeclare(w_in, 0, 16, base + hp * 256, 256)
    for s in range(4):
        wdeclare(w_out, 1024, 8, s * 512, 512)
    for s in range(4):
        wdeclare(w_out, 1024, 8, s * 512, 512)
    for wd in (w_cq, w_ck, w_cv, w_co):
        for s in range(8):
            wdeclare(wd, 0, 16, s * 256, 256)
    for g in range(4):
        for s in range(8):
            wdeclare(w_ff1, 0, 16, g * 2048 + s * 256, 256)
        for s in range(8):
            wdeclare(w_ff2, g * 2048, 16, s * 256, 256)
    wissue_upto(2)

    def load_transpose(src, ntok_tiles, rows_per_tile, dst, dkey, stage_kb=(0, 16), gain=None, ssq=None):
        st = Sv(stage_kb[0], stage_kb[1], F32).rearrange("p (s f) -> p s f", s=2)
        for tt in range(ntok_tiles):
            slot = tt % 2
            stv = st[0:rows_per_tile, slot, :]
            sc.add("sp", lambda e, o=stv, i=src[tt * rows_per_tile:(tt + 1) * rows_per_tile, :]: e.dma_start(out=o, in_=i),
                   w=[("xst", slot)], dma=("xst", slot))
            if ssq is not None:
                ssq(tt, stv, ("xst", slot))

            def dst_of(i0, cnt, pv, pkey, tt=tt):
                dst(i0, cnt, tt, pv, pkey)
            transposes_to(dst_of, lambda i, stv=stv: stv[:, i * 128:(i + 1) * 128], 16, F32,
                          lambda i, slot=slot: [("xst", slot), "CS"], None, np_in=rows_per_tile, np_out=128)

    def rmsnorm_fm(Hbuf, hkey, Xbuf, xkey, gi, ncols, nhalf, out32=False):
        for th in range(nhalf):
            cols = slice(th * ncols, (th + 1) * ncols)
            b = miscbank()
            for k in range(16):
                sc.add("act", lambda e, o=Xbuf[:, k, cols], i=Hbuf[:, k, cols]: e.activation(out=o, in_=i, func=AF.Square),
                       r=[hkey(k, th)], w=[xkey(k, th)])
                sc.add("pe", lambda e, o=P[b][:, 0:ncols], r_=Xbuf[:, k, cols], st=(k == 0), sp=(k == 15):
                       e.matmul(o, onesb[:], r_, start=st, stop=sp), r=[xkey(k, th), "onesb"], w=[("P", b)])
            rv = rstb[:, 0:ncols]
            sc.add("act", lambda e, o=rv, i=P[b][:, 0:ncols]: e.activation(out=o, in_=i, func=AF.Sqrt, bias=EPS, scale=1.0 / D),
                   r=[("P", b)], w=[("rst", 0)])
            sc.add("dve", lambda e, o=rv: e.reciprocal(o, o), r=[("rst", 0)], w=[("rst", 0)])
            for k in range(16):
                g = cs("gpk", 1, gi * 16 + k)
                if out32:
                    sc.add("dve", lambda e, o=Hbuf[:, k, cols], g=g, rv=rv: e.scalar_tensor_tensor(o, o, g, rv, op0=ALU.mult, op1=ALU.mult),
                           r=[hkey(k, th), ("rst", 0), "CS"], w=[hkey(k, th)])
                else:
                    sc.add("dve", lambda e, o=Xbuf[:, k, cols], i=Hbuf[:, k, cols], g=g, rv=rv:
                           e.scalar_tensor_tensor(o, i, g, rv, op0=ALU.mult, op1=ALU.mult),
                           r=[hkey(k, th), ("rst", 0), "CS"], w=[xkey(k, th)])

    Hk = lambda k, th: ("H", k, th)
    Xk = lambda k, th: ("X", k, th)
    Hsk = lambda k, th: ("Hs", k)
    Xsk = lambda k, th: ("Xs", k)

    def xk_X(k, th):
        return X[:, k, th * 512:(th + 1) * 512], [("X", k, th)]

    def xt_X(k, tt):
        return X[:, k, tt * 128:(tt + 1) * 128], [("X", k, tt // 4)]

    def xs_Xs(k):
        return Xs[:, k, :], [("Xs", k)]

    def rope_scale(dst_bf, ps, pkey, cosA, sinA, scaleA, nparts, tmpv, tmpkey, outkeys, imm_scale=None, extra=()):
        xs_ = tmpv[0:nparts, 0, :]
        t1 = tmpv[0:nparts, 1, 0:128]
        t2 = tmpv[0:nparts, 2, 0:128]
        if scaleA is not None:
            sc.add("dve", lambda e: e.tensor_tensor(xs_.rearrange("p (h d) -> p h d", h=2), ps.rearrange("p (h d) -> p h d", h=2),
                                                    vw(scaleA, [[1, 2], [0, 128]]), op=ALU.mult), r=[pkey, "CS"], w=[tmpkey])
        else:
            sc.add("act", lambda e: e.activation(out=xs_, in_=ps, func=AF.Copy, scale=(imm_scale or 1.0)), r=[pkey], w=[tmpkey])
        xv = xs_.rearrange("p (h t d) -> p h t d", h=2, t=2)
        dv = dst_bf.rearrange("p (h t d) -> p h t d", h=2, t=2)
        cb = vw(cosA, [[0, 2], [1, 64]])
        sb_ = vw(sinA, [[0, 2], [1, 64]])
        t1v = t1.rearrange("p (h d) -> p h d", h=2)
        t2v = t2.rearrange("p (h d) -> p h d", h=2)
        rk = [tmpkey, "CS"] + list(extra)
        sc.add("dve", lambda e: e.tensor_tensor(t1v, xv[:, :, 0, :], cb, op=ALU.mult), r=rk, w=[tmpkey + ("a",)])
        sc.add("dve", lambda e: e.tensor_tensor(t2v, xv[:, :, 1, :], sb_, op=ALU.mult), r=rk, w=[tmpkey + ("b",)])
        sc.add("dve", lambda e: e.tensor_tensor(dv[:, :, 0, :], t1v, t2v, op=ALU.subtract), r=[tmpkey + ("a",), tmpkey + ("b",)], w=outkeys)
        t3 = tmpv[0:nparts, 1, 128:256].rearrange("p (h d) -> p h d", h=2)
        t4 = tmpv[0:nparts, 2, 128:256].rearrange("p (h d) -> p h d", h=2)
        sc.add("dve", lambda e: e.tensor_tensor(t3, xv[:, :, 0, :], sb_, op=ALU.mult), r=rk, w=[tmpkey + ("c",)])
        sc.add("dve", lambda e: e.tensor_tensor(t4, xv[:, :, 1, :], cb, op=ALU.mult), r=rk, w=[tmpkey + ("d",)])
        sc.add("dve", lambda e: e.tensor_tensor(dv[:, :, 1, :], t3, t4, op=ALU.add), r=[tmpkey + ("c",), tmpkey + ("d",)], w=outkeys)

    ropetmp = [Mv(0, 768).rearrange("p (a f) -> p a f", a=3), Mv(2560, 768).rearrange("p (a f) -> p a f", a=3)]
    rtog = {"n": 0}

    def ropeslot():
        rtog["n"] += 1
        i_ = rtog["n"] % 2
        return ropetmp[i_], ("rtmp", i_)

    Ainit = Sv(52, 56, F32).rearrange("p (h e) -> p h e", h=8)
    kp_tok = Sv(16, 20).rearrange("p (t f) -> p t f", t=8)
    vp_tok = Sv(20, 24).rearrange("p (t f) -> p t f", t=8)
    rstd_p = smx[:, 0:8]
    ssq_p = smx[:, 8:16]
    junkp = Sv(24, 32, F32)

    def ssq_prev(tt, stv, skey):
        sc.add("act", lambda e: e.activation(out=junkp, in_=stv, func=AF.Square, accum_out=ssq_p[:, tt:tt + 1]), r=[skey], w=["junkp", ("ssqp", tt)])

    def dst_prev(i0, cnt, tt, pv, pkey):
        eng = evac_eng()
        for i in range(cnt):
            k = i0 + i
            if eng == "act":
                sc.add("act", lambda e, o=X[:, k, tt * 128:(tt + 1) * 128], i_=pv[:, i * 128:(i + 1) * 128], g=cs("gpk", 1, k):
                       e.activation(out=o, in_=i_, func=AF.Copy, scale=g), r=[pkey, "CS"], w=[("X", k, tt // 4)])
            else:
                sc.add("dve", lambda e, o=X[:, k, tt * 128:(tt + 1) * 128], i_=pv[:, i * 128:(i + 1) * 128], g=cs("gpk", 1, k):
                       e.tensor_scalar(o, i_, g, None, op0=ALU.mult), r=[pkey, "CS"], w=[("X", k, tt // 4)])

    sc.add("dve", lambda e: e.memset(ssq_p, 0.0), w=[("ssqp", t_) for t_ in range(8)])
    load_transpose(x_prev, NT, 128, dst_prev, None, ssq=ssq_prev)
    sc.add("act", lambda e: e.activation(out=rstd_p, in_=ssq_p, func=AF.Sqrt, bias=EPS, scale=1.0 / D), r=[("ssqp", t_) for t_ in range(8)], w=["rstdp"])
    sc.add("dve", lambda e: e.reciprocal(rstd_p, rstd_p), r=["rstdp"], w=["rstdp"])

    for hp in range(4):
        def epi_kp2(s, tt, ps, pkey, hp=hp):
            tv, tk = ropeslot()
            scl = smx[:, 16 + 2 * (tt % 2):18 + 2 * (tt % 2)]
            sc.add("dve", lambda e: e.tensor_scalar(scl, cs("ks", 2, tt * 8 + hp * 2), rstd_p[:, tt:tt + 1], None, op0=ALU.mult),
                   r=["CS", "rstdp"], w=[tk])
            rope_scale(kp_tok[:, tt, :], ps, pkey, ropep[:, tt * 64:(tt + 1) * 64], ropep[:, 512 + tt * 64:512 + (tt + 1) * 64], scl, 128, tv, tk, [("kp", tt)], extra=["ropep"])

        def epi_vp(s, tt, ps, pkey):
            sc.add("act", lambda e: e.activation(out=vp_tok[:, tt, :], in_=ps, func=AF.Copy, scale=rstd_p[:, tt:tt + 1]), r=[pkey, "rstdp"], w=[("vp", tt)])

        gemm_a(w_in, 0, 16, 3072 + hp * 256, 256, NT, xt_X, epi_kp2)
        gemm_a(w_in, 0, 16, 4096 + hp * 256, 256, NT, xt_X, epi_vp)
        b = miscbank()
        for hh in range(2):
            for tt in range(NT):
                sc.add("pe", lambda e, o=P[b][:, hh * 128:(hh + 1) * 128], l=kp_tok[:, tt, hh * 128:(hh + 1) * 128], r_=vp_tok[:, tt, hh * 128:(hh + 1) * 128],
                       st=(tt == 0), sp=(tt == NT - 1): e.matmul(o, l, r_, start=st, stop=sp), r=[("kp", tt), ("vp", tt)], w=[("P", b)])
        for hh in range(2):
            h = hp * 2 + hh
            sc.add("dve", lambda e, o=Ainit[:, h, :], i=P[b][:, hh * 128:(hh + 1) * 128], g=cs("gini", 1, h):
                   e.tensor_scalar(o, i, g, None, op0=ALU.mult), r=[("P", b), "CS"], w=[("Ainit", h)])
    sc.barrier()
    if stop_after == "prefix":
        return finish(nc, es, sc, H, Hs, y_cur, y_smp, S, Sv, P, CS, transposes_to, load_transpose, rmsnorm_fm, debug=True)

    def dst_cur(i0, cnt, tt, pv, pkey):
        eng = evac_eng()
        o = H[:, i0:i0 + cnt, tt * 128:(tt + 1) * 128]
        i_ = pv.rearrange("p (c t) -> p c t", c=cnt)
        sc.add(eng, copy_op(eng, o, i_), r=[pkey], w=[("H", k, tt // 4) for k in range(i0, i0 + cnt)])

    load_transpose(x_cur, NT, 128, dst_cur, None)

    def dst_smp(i0, cnt, tt, pv, pkey):
        o = Hs[:, i0:i0 + cnt, :]
        i_ = pv.rearrange("p (c t) -> p c t", c=cnt)
        sc.add("dve", copy_op("dve", o, i_), r=[pkey], w=[("Hs", k) for k in range(i0, i0 + cnt)])

    load_transpose(x_smp, 1, TS, dst_smp, None)
    rmsnorm_fm(H, Hk, X, Xk, 0, 512, 2)
    rmsnorm_fm(Hs, Hsk, Xs, Xsk, 0, TS, 1)
    sc.barrier()
    if stop_after == "norm1":
        return finish(nc, es, sc, H, Hs, y_cur, y_smp, S, Sv, P, CS, transposes_to, load_transpose, rmsnorm_fm, debug=True)

    cc = Sv(0, 16).rearrange("p (j t) -> p j t", j=8)
    vrows = Sv(16, 32).rearrange("p (t f) -> p t f", t=8)
    g32 = Sv(36, 38, F32).rearrange("p (s f) -> p s f", s=2)
    ug = Sv(38, 42, F32).rearrange("p (s f) -> p s f", s=2)
    t32 = Sv(42, 44, F32)
    vs32 = Sv(32, 36, F32)
    bst = Mv(768, 192).rearrange("p (t s c) -> p t s c", t=8, s=4)
    bsts = Mv(1216, 24, npart=TS).rearrange("p (s c) -> p s c", s=4)
    ugs = Mv(960, 128).rearrange("p (h t) -> p h t", h=8)
    vgs = Mv(1088, 128).rearrange("p (h t) -> p h t", h=8)
    b_bc = Sv(48, 52, F32)
    sc.add("sp", lambda e: e.dma_start(out=b_bc, in_=bbc_d[:, :]), w=["b_bc"], dma="cst4")

    gtog = {"n": 0}

    def epi_vA(s, tt, ps, pkey):
        gtog["n"] += 1
        sl = gtog["n"] % 2
        sc.add("act", lambda e: e.activation(out=g32[:, sl, :], in_=ps, func=AF.Gelu_apprx_tanh), r=[pkey], w=[("g32", sl)])
        sc.add("dve", lambda e: e.bn_stats(bst[:, tt, s, :], g32[:, sl, :]), r=[("g32", sl)], w=[("bst", tt)])
        sc.add("dve", lambda e: e.tensor_copy(vrows[:, tt, s * 256:(s + 1) * 256], g32[:, sl, :]), r=[("g32", sl)], w=[("vrows", tt)])

    def epis_vA(s, ps, pkey):
        sc.add("act", lambda e: e.activation(out=vs32[0:TS, s * 256:(s + 1) * 256], in_=ps, func=AF.Gelu_apprx_tanh), r=[pkey], w=["vs32"])
        sc.add("dve", lambda e: e.bn_stats(bsts[:, s, :], vs32[0:TS, s * 256:(s + 1) * 256]), r=["vs32"], w=["bsts"])

    gemm_a(w_in, 0, 16, 1024, 1024, NT, xt_X, epi_vA, xs=xs_Xs, epis=epis_vA)

    def ln_finish(stats, np_, data_in, data_out, keys_r, keys_w):
        mv = smx[0:np_, 0:2]
        sc.add("dve", lambda e: e.bn_aggr(mv, stats), r=keys_r, w=["mvA"])
        sc.add("act", lambda e: e.activation(out=mv[:, 1:2], in_=mv[:, 1:2], func=AF.Sqrt, bias=EPS, scale=1.0), r=["mvA"], w=["mvA"])
        sc.add("dve", lambda e: e.reciprocal(mv[:, 1:2], mv[:, 1:2]), r=["mvA"], w=["mvA"])
        sc.add("dve", lambda e: e.tensor_scalar(data_out, data_in, mv[:, 0:1], mv[:, 1:2], op0=ALU.subtract, op1=ALU.mult),
               r=["mvA"] + keys_r, w=keys_w)

    for tt in range(NT):
        ln_finish(bst[:, tt, :, :].rearrange("p s c -> p (s c)"), 128, vrows[:, tt, :], vrows[:, tt, :], [("bst", tt), ("vrows", tt)], [("vrows", tt)])
    ln_finish(bsts.rearrange("p s c -> p (s c)"), TS, vs32[0:TS, :], vs32[0:TS, :], ["bsts", "vs32"], ["vs32"])
    def dst_vgs(i0, cnt, pv, pkey):
        for i in range(cnt):
            h = i0 + i
            sc.add("dve", lambda e, o=vgs[:, h, :], i_=pv[:, i * TS:(i + 1) * TS], g=cs("gA", 1, h): e.tensor_scalar(o, i_, g, None, op0=ALU.mult),
                   r=[pkey, "CS"], w=[("vgs", h)])
    transposes_to(dst_vgs, lambda i: vs32[0:TS, i * 128:(i + 1) * 128], 8, F32, lambda i: ["vs32", "CS"], None, np_in=TS, np_out=128)
    cvst = Sv(44, 48, F32)
    def dst_cvs(i0, cnt, pv, pkey):
        sc.add("act", lambda e: e.copy(cvst[0:TS, i0 * 128:(i0 + cnt) * 128], pv), r=[pkey], w=["cvst"])
    transposes_to(dst_cvs, lambda i: vgs[:, i, :], 8, F32, lambda i: [("vgs", i), "CS"], None, np_in=128, np_out=TS)
    sc.add("sp", lambda e: e.dma_start(out=cvs_o[:, :], in_=cvst[0:TS, :]), r=["cvst"], dma="out_cvs")

    utog = {"n": 0}

    def epi_u(f, th, ps, pkey):
        utog["n"] += 1
        sl = utog["n"] % 2
        sc.add("act", lambda e: e.activation(out=ug[:, sl, :], in_=ps, func=AF.Gelu_apprx_tanh), r=[pkey], w=[("ug", sl)])
        b = miscbank()
        for n in range(4):
            tt = th * 4 + n
            sc.add("pe", lambda e, o=P[b][:, n * 128:(n + 1) * 128], l=vrows[:, tt, f * 128:(f + 1) * 128], r_=wsTb[:, f, :]:
                   e.matmul(o, l, r_, start=True, stop=True), r=[("vrows", tt), "wsTb"], w=[("P", b)])
        sc.add("dve", lambda e: e.scalar_tensor_tensor(t32.rearrange("p (n i) -> p n i", n=4), P[b][:].rearrange("p (n i) -> p n i", n=4), cs("gA", 1, f),
                                                       vw(b_bc[:, f * 128:(f + 1) * 128], [[0, 4], [1, 128]]), op0=ALU.mult, op1=ALU.add),
               r=[("P", b), "CS", "b_bc"], w=["t32"])
        sc.add("dve", lambda e: e.tensor_tensor(cc[:, f, th * 512:(th + 1) * 512], t32, ug[:, sl, :], op=ALU.mult), r=["t32", ("ug", sl)], w=[("cc", f, th)])

    def epis_u(f, ps, pkey):
        sc.add("act", lambda e: e.activation(out=ugs[:, f, :], in_=ps, func=AF.Gelu_apprx_tanh), r=[pkey], w=[("ugs", f)])
        tmp = smx[:, 32:48]
        sc.add("dve", lambda e: e.tensor_scalar(tmp, vgs[:, f, :], cs("ws00", 1, f), cs("b0", 1, f), op0=ALU.mult, op1=ALU.add),
               r=[("vgs", f), "CS"], w=["tmps"])
        sc.add("dve", lambda e: e.tensor_tensor(ccs[:, f, :], tmp, ugs[:, f, :], op=ALU.mult), r=["tmps", ("ugs", f)], w=[("ccs", f)])

    gemm_b(w_in, 0, 16, 0, 1024, xk_X, epi_u, xs=xs_Xs, epis=epis_u)

    def xk_cc(k, th):
        return cc[:, k, th * 512:(th + 1) * 512], [("cc", k, th)]

    def xs_ccs(k):
        return ccs[:, k, :], [("ccs", k)]

    def epi_res(f, th, ps, pkey):
        eng = "dve"
        o = H[:, f, th * 512:(th + 1) * 512]
        sc.add(eng, lambda e: e.tensor_tensor(o, o, ps, op=ALU.add), r=[pkey, ("H", f, th)], w=[("H", f, th)])

    def epis_res(f, ps, pkey):
        o = Hs[:, f, :]
        sc.add("dve", lambda e: e.tensor_tensor(o, o, ps, op=ALU.add), r=[pkey, ("Hs", f)], w=[("Hs", f)])

    gemm_b(w_out, 0, 8, 0, 2048, xk_cc, epi_res, xs=xs_ccs, epis=epis_res)
    sc.barrier()
    if stop_after == "mixA":
        return finish(nc, es, sc, H, Hs, y_cur, y_smp, S, Sv, P, CS, transposes_to, load_transpose, rmsnorm_fm, debug=True)

    qT = Sv(16, 20).rearrange("p (h t) -> p h t", h=2)
    kT = Sv(20, 24).rearrange("p (h t) -> p h t", h=2)
    ktok = Sv(24, 28).rearrange("p (t f) -> p t f", t=8)
    vtok = Sv(28, 32).rearrange("p (t f) -> p t f", t=8)
    gT = Sv(32, 36).rearrange("p (h t) -> p h t", h=2)
    ks_s = Sv(36, 40, F32)
    vs_s = Sv(40, 44, F32)
    stsl = Sv(44, 56, F32).rearrange("p (s h e) -> p s h e", s=3, h=8)
    qtok_t = Mv(768, 256, BF16).rearrange("p (s f) -> p s f", s=2)
    qs_s = Sv(48, 52, F32)[0:TS, :]
    qTs = Mv(1920, 128).rearrange("p (h t) -> p h t", h=8)
    gTs = Mv(2048, 128).rearrange("p (h t) -> p h t", h=8)
    Aacc = Mv(1536, 256).rearrange("p (h e) -> p h e", h=2)
    Abf = Mv(1792, 128, BF16).rearrange("p (h e) -> p h e", h=2)
    scm = Mv(1024, 256, BF16).rearrange("p (s f) -> p s f", s=2)
    retn = Mv(1280, 256, BF16).rearrange("p (s f) -> p s f", s=2)
    rstat = Mv(2176, 12).rearrange("p (h c) -> p h c", h=2)
    rmv = Mv(2188, 4).rearrange("p (h c) -> p h c", h=2)
    qtt = {"n": 0}
    dfq = Defer(1)

    for hp in range(4):
        def epi_q(s, tt, ps, pkey, hp=hp, which="q"):
            tv, tk = ropeslot()
            qtt["n"] += 1
            sl = qtt["n"] % 2
            if which == "q":
                dstb, dkey, scl = qtok_t[:, sl, :], ("qtokt", sl), cs("qs", 2, tt * 8 + hp * 2)
            else:
                dstb, dkey, scl = ktok[:, tt, :], ("ktok", tt), cs("ks", 2, tt * 8 + hp * 2)
            rope_scale(dstb, ps, pkey, cs("cosc", 64, tt * 64), cs("sinc", 64, tt * 64), scl, 128, tv, tk, [dkey])
            dT = qT if which == "q" else kT
            nm = "qT" if which == "q" else "kT"

            def dst_t(i0, cnt, pv, pk2):
                sc.add("act", lambda e: e.copy(dT[:, :, tt * 128:(tt + 1) * 128], pv.rearrange("p (h t) -> p h t", h=2)), r=[pk2], w=[(nm, tt)])
            dfq.push(lambda: transposes_to(dst_t, lambda i: dstb[:, i * 128:(i + 1) * 128], 2, BF16, lambda i: [dkey, "identb"], None))

        def epis_q(s, ps, pkey, hp=hp, which="q"):
            tv, tk = ropeslot()
            dsts = (qs_s if which == "q" else ks_s)[0:TS, hp * 256:(hp + 1) * 256]
            rope_scale(dsts, ps, pkey, cs("coss", 64, 0, TS), cs("sins", 64, 0, TS), None, TS, tv, tk, [("qs_s" if which == "q" else "ks_s", hp)],
                       imm_scale=(1.0 if which == "q" else 128.0 ** -0.5))

        gemm_a(w_in, 0, 16, 2048 + hp * 256, 256, NT, xt_X, epi_q, xs=xs_Xs, epis=epis_q)
        dfq.flush()
        gemm_a(w_in, 0, 16, 3072 + hp * 256, 256, NT, xt_X, lambda s, tt, ps, pkey, hp=hp: epi_q(s, tt, ps, pkey, hp, "k"),
               xs=xs_Xs, epis=lambda s, ps, pkey, hp=hp: epis_q(s, ps, pkey, hp, "k"))
        dfq.flush()

        def epi_v(s, tt, ps, pkey):
            eng = evac_eng()
            sc.add(eng, copy_op(eng, vtok[:, tt, :], ps), r=[pkey], w=[("vtok", tt)])

        def epis_v(s, ps, pkey, hp=hp):
            sc.add("act", lambda e: e.copy(vs_s[0:TS, hp * 256:(hp + 1) * 256], ps), r=[pkey], w=[("vs_s", hp)])

        gemm_a(w_in, 0, 16, 4096 + hp * 256, 256, NT, xt_X, epi_v, xs=xs_Xs, epis=epis_v)

        def epi_g(f, th, ps, pkey):
            sc.add("act", lambda e: e.activation(out=gT[:, f, th * 512:(th + 1) * 512], in_=ps, func=AF.Silu), r=[pkey], w=[("gT", f, th)])

        def epis_g(f, ps, pkey, hp=hp):
            sc.add("act", lambda e: e.activation(out=gTs[:, hp * 2 + f, :], in_=ps, func=AF.Silu), r=[pkey], w=[("gTs", hp * 2 + f)])

        gemm_b(w_in, 0, 16, 5120 + hp * 256, 256, xk_X, epi_g, xs=xs_Xs, epis=epis_g)

        for hh in range(2):
            h = hp * 2 + hh
            sc.add("dve", lambda e, hh=hh, h=h: e.tensor_copy(Aacc[:, hh, :], Ainit[:, h, :]), r=[("Ainit", h)], w=["Aacc"])
        sc.add("act", lambda e: e.copy(Mv(1792, 128, BF16), Mv(1536, 256)), r=["Aacc"], w=["Abf"])
        deferred = [None]
        for n in range(NT):
            b1 = miscbank()
            for hh in range(2):
                sc.add("pe", lambda e, hh=hh: e.matmul(P[b1][:, hh * 128:(hh + 1) * 128], kT[:, hh, n * 128:(n + 1) * 128], qT[:, hh, n * 128:(n + 1) * 128],
                                                      start=True, stop=True), r=[("kT", n), ("qT", n)], w=[("P", b1)])
            sl = n % 2
            sc.add("dve", lambda e, sl=sl: e.tensor_tensor(scm[:, sl, :], P[b1][:, 0:256], cs("maskT", 256), op=ALU.mult), r=[("P", b1), "CS"], w=[("scm", sl)])
            b2 = mainbank()
            for hh in range(2):
                sc.add("pe", lambda e, hh=hh, sl=sl: e.matmul(P[b2][:, hh * 128:(hh + 1) * 128], scm[:, sl, hh * 128:(hh + 1) * 128], vtok[:, n, hh * 128:(hh + 1) * 128],
                                                             start=True, stop=False), r=[("scm", sl), ("vtok", n)], w=[("P", b2)])
                sc.add("pe", lambda e, hh=hh: e.matmul(P[b2][:, hh * 128:(hh + 1) * 128], qT[:, hh, n * 128:(n + 1) * 128], Abf[:, hh, :],
                                                      start=False, stop=True), r=[("qT", n), "Abf"], w=[("P", b2)])
            b3 = miscbank()
            for hh in range(2):
                sc.add("pe", lambda e, hh=hh: e.matmul(P[b3][:, hh * 128:(hh + 1) * 128], ktok[:, n, hh * 128:(hh + 1) * 128], vtok[:, n, hh * 128:(hh + 1) * 128],
                                                      start=True, stop=True), r=[("ktok", n), ("vtok", n)], w=[("P", b3)])
            sc.add("dve", lambda e: e.tensor_tensor(Mv(1536, 256), Mv(1536, 256), P[b3][:, 0:256], op=ALU.add),
                   r=[("P", b3), "Aacc"], w=["Aacc"])
            sc.add("act", lambda e: e.copy(Mv(1792, 128, BF16), Mv(1536, 256)), r=["Aacc"], w=["Abf"])
            for hh in range(2):
                sc.add("dve", lambda e, hh=hh: e.bn_stats(rstat[:, hh, :], P[b2][:, hh * 128:(hh + 1) * 128]), r=[("P", b2)], w=[("rstat", hh)])
                sc.add("dve", lambda e, hh=hh: e.bn_aggr(rmv[:, hh, :], rstat[:, hh, :]), r=[("rstat", hh)], w=[("rmv", hh)])
            sc.add("act", lambda e: e.activation(out=rmv[:, :, 1:2], in_=rmv[:, :, 1:2], func=AF.Sqrt, bias=EPS, scale=1.0), r=[("rmv", 0), ("rmv", 1)], w=[("rmv", 0), ("rmv", 1)])
            sc.add("dve", lambda e: e.reciprocal(rmv[:, :, 1:2], rmv[:, :, 1:2]), r=[("rmv", 0), ("rmv", 1)], w=[("rmv", 0), ("rmv", 1)])
            sc.add("dve", lambda e: e.scalar_tensor_tensor(rmv[:, :, 0:1], rmv[:, :, 0:1], -1.0, rmv[:, :, 1:2], op0=ALU.mult, op1=ALU.mult),
                   r=[("rmv", 0), ("rmv", 1)], w=[("rmv", 0), ("rmv", 1)])
            for hh in range(2):
                sc.add("act", lambda e, hh=hh, sl=sl: e.activation(out=retn[:, sl, hh * 128:(hh + 1) * 128], in_=P[b2][:, hh * 128:(hh + 1) * 128], func=AF.Identity,
                                                                  bias=rmv[:, hh, 0:1], scale=rmv[:, hh, 1:2]), r=[("P", b2), ("rmv", hh)], w=[("retn", sl)])

            def dst_r(i0, cnt, pv, pk2, n=n, hp=hp):
                for hh in range(2):
                    h = hp * 2 + hh
                    sc.add("dve", lambda e, hh=hh, h=h: e.scalar_tensor_tensor(cc[:, h, n * 128:(n + 1) * 128], pv[:, hh * 128:(hh + 1) * 128], cs("gnB", 1, h),
                                                                               gT[:, hh, n * 128:(n + 1) * 128], op0=ALU.mult, op1=ALU.mult),
                           r=[pk2, "CS", ("gT", hh, n // 4)], w=[("cc", h, n // 4)])
            if deferred[0] is not None:
                deferred[0]()
            deferred[0] = (lambda dst_r=dst_r, sl=sl: transposes_to(dst_r, lambda i: retn[:, sl, i * 128:(i + 1) * 128], 2, BF16, lambda i: [("retn", sl), "identb"], None))
        deferred[0]()
        stst = Sv(44, 45, F32).rearrange("p (h e) -> p h e", h=2)
        for hh in range(2):
            h = hp * 2 + hh
            sc.add("dve", lambda e, hh=hh, h=h: e.tensor_scalar(stst[:, hh, :], Aacc[:, hh, :], cs("gfin", 1, h), None, op0=ALU.mult), r=["Aacc", "CS"], w=[("stst", hh)])
        sc.add("sp", lambda e, hp=hp: e.dma_start(out=stp_o[hp * 2:hp * 2 + 2].rearrange("h d e -> d h e"), in_=stst), r=[("stst", 0), ("stst", 1)], dma="out_stp")

    sc.barrier()
    def dst_qTs(i0, cnt, pv, pkey):
        sc.add("dve", lambda e: e.tensor_copy(qTs[:, i0:i0 + cnt, :], pv.rearrange("p (c t) -> p c t", c=cnt)), r=[pkey], w=[("qTs", i0 // 4)])
    transposes_to(dst_qTs, lambda i: qs_s[0:TS, i * 128:(i + 1) * 128], 8, F32, lambda i: [("qs_s", i // 2), "CS"], None, np_in=TS, np_out=128)
    Qsel = Sv(16, 24, F32).rearrange("p (h s c) -> p h s c", h=8, s=TS)
    sc.add("dve", lambda e: e.tensor_tensor(Qsel, vw(qTs[:, 0, 0:1], [[TS, 8], [1, TS], [0, TS]]),
                                            vw(cs("eyeq", 256), [[0, 8], [TS, TS], [1, TS]]), op=ALU.mult),
           r=[("qTs", 0), ("qTs", 1), "CS"], w=["Qsel"])
    qk = Mv(2192, 8, npart=TS)
    qkj = Sv(24, 28, F32)
    sc.add("dve", lambda e: e.tensor_tensor(qkj[0:TS, :], qs_s, ks_s[0:TS, :], op=ALU.mult), r=[("qs_s", i) for i in range(4)] + [("ks_s", i) for i in range(4)], w=["qkj"])
    sc.add("dve", lambda e: e.tensor_reduce(qk, qkj[0:TS, :].rearrange("p (h d) -> p h d", h=8), axis=AX.X, op=ALU.add), r=["qkj"], w=["qk"])
    rs_tok = Sv(28, 32, F32)
    sc.barrier()
    pcross = 5; pc2 = 6
    cracc = Sv(24, 28, F32)[0:TS, :]
    sc.add("dve", lambda e: e.memset(cracc, 0.0), w=["cracc"])
    Ksel = Sv(32, 36, F32)[0:TS, :]
    def state_iter(s_):
        sl = s_ % 3
        sc.add("sp", lambda e, s_=s_, sl=sl: e.dma_start(out=stsl[:, sl], in_=st_in[s_].rearrange("h d e -> d h e")), w=[("stsl", sl)], dma=("stin", sl))
        sc.add("dve", lambda e, s_=s_, sl=sl: e.tensor_scalar(Ksel, ks_s[0:TS, :], cs("eye16", 1, s_, TS), None, op0=ALU.mult),
               r=[("ks_s", i) for i in range(4)] + ["CS"], w=["Ksel"])
        for h in range(8):
            pb = pcross if h < 4 else pc2
            sc.add("pe", lambda e, s_=s_, sl=sl, h=h, pb=pb: e.matmul(P[pb][0:TS, (h % 4) * 128:(h % 4 + 1) * 128], Qsel[:, h, s_, :], stsl[:, sl, h, :],
                                                                     start=True, stop=True), r=["Qsel", ("stsl", sl)], w=[("P", pb)])
        for half in range(2):
            pb = (pcross, pc2)[half]
            sc.add("dve", lambda e, half=half, pb=pb: e.tensor_tensor(cracc[:, half * 512:(half + 1) * 512], cracc[:, half * 512:(half + 1) * 512], P[pb][0:TS, :], op=ALU.add),
                   r=[("P", pb), "cracc"], w=["cracc"])
        bo1 = 4; bo2 = 7
        for h in range(8):
            bo = bo1 if h < 4 else bo2
            sc.add("pe", lambda e, sl=sl, h=h, bo=bo: e.matmul(P[bo][:, (h % 4) * 128:(h % 4 + 1) * 128], Ksel[:, h * 128:(h + 1) * 128], vs_s[0:TS, h * 128:(h + 1) * 128],
                                                              start=True, stop=True), r=["Ksel"] + [("vs_s", i) for i in range(4)], w=[("P", bo)])
        for h in range(8):
            bo = bo1 if h < 4 else bo2
            sc.add("dve", lambda e, sl=sl, h=h, bo=bo: e.scalar_tensor_tensor(stsl[:, sl, h, :], stsl[:, sl, h, :], GAM[h], P[bo][:, (h % 4) * 128:(h % 4 + 1) * 128],
                                                                             op0=ALU.mult, op1=ALU.add), r=[("P", bo), ("stsl", sl)], w=[("stsl", sl)])
        sc.add("sp", lambda e, s_=s_, sl=sl: e.dma_start(out=sts_o[s_].rearrange("h d e -> d h e"), in_=stsl[:, sl]), r=[("stsl", sl)], dma=("stout", sl))

    pend2 = [(lambda s_=s_: state_iter(s_)) for s_ in range(TS)]
    tk2 = {"n": 0}

    def epi_res_t(f, th, ps, pkey):
        epi_res(f, th, ps, pkey)
        tk2["n"] += 1
        if tk2["n"] % 2 == 0 and pend2:
            pend2.pop(0)()

    pend2.pop(0)()
    pend2.pop(0)()
    gemm_b(w_out, 1024, 8, 0, 2048, xk_cc, epi_res_t)
    while pend2:
        pend2.pop(0)()
    sc.add("dve", lambda e: e.tensor_tensor(rs_tok[0:TS, :].rearrange("p (h e) -> p h e", h=8), vs_s[0:TS, :].rearrange("p (h e) -> p h e", h=8),
                                            vw(qk[:, 0:1], [[1, 8], [0, 128]]), op=ALU.mult), r=["qk"] + [("vs_s", i) for i in range(4)], w=["rs_tok"])
    for h in range(8):
        sc.add("dve", lambda e, h=h: e.scalar_tensor_tensor(rs_tok[0:TS, h * 128:(h + 1) * 128], cracc[:, h * 128:(h + 1) * 128], GAM[h],
                                                            rs_tok[0:TS, h * 128:(h + 1) * 128], op0=ALU.mult, op1=ALU.add), r=["cracc", "rs_tok"], w=["rs_tok"])
    rst_s = Mv(2200, 48, npart=TS).rearrange("p (h c) -> p h c", h=8)
    rmv_s = Mv(2248, 16, npart=TS).rearrange("p (h c) -> p h c", h=8)
    for h in range(8):
        sc.add("dve", lambda e, h=h: e.bn_stats(rst_s[:, h, :], rs_tok[0:TS, h * 128:(h + 1) * 128]), r=["rs_tok"], w=["rst_s"])
        sc.add("dve", lambda e, h=h: e.bn_aggr(rmv_s[:, h, :], rst_s[:, h, :]), r=["rst_s"], w=["rmv_s"])
    sc.add("act", lambda e: e.activation(out=rmv_s[:, :, 1:2], in_=rmv_s[:, :, 1:2], func=AF.Sqrt, bias=EPS, scale=1.0), r=["rmv_s"], w=["rmv_s"])
    sc.add("dve", lambda e: e.reciprocal(rmv_s[:, :, 1:2], rmv_s[:, :, 1:2]), r=["rmv_s"], w=["rmv_s"])
    for h in range(8):
        sc.add("dve", lambda e, h=h: e.tensor_scalar(rs_tok[0:TS, h * 128:(h + 1) * 128], rs_tok[0:TS, h * 128:(h + 1) * 128], rmv_s[:, h, 0:1], rmv_s[:, h, 1:2],
                                                     op0=ALU.subtract, op1=ALU.mult), r=["rmv_s", "rs_tok"], w=["rs_tok"])

    def dst_rs(i0, cnt, pv, pkey):
        for i in range(cnt):
            h = i0 + i
            sc.add("dve", lambda e, h=h, i=i: e.scalar_tensor_tensor(ccs[:, h, :], pv[:, i * TS:(i + 1) * TS], cs("gnB", 1, h), gTs[:, h, :], op0=ALU.mult, op1=ALU.mult),
                   r=[pkey, "CS", ("gTs", h)], w=[("ccs", h)])
    transposes_to(dst_rs, lambda i: rs_tok[0:TS, i * 128:(i + 1) * 128], 8, F32, lambda i: ["rs_tok", "CS"], None, np_in=TS, np_out=128)

    gemm_s(w_out, 1024, 8, 0, 2048, xs_ccs, epis_res)
    sc.barrier()
    if stop_after == "mixB":
        return finish(nc, es, sc, H, Hs, y_cur, y_smp, S, Sv, P, CS, transposes_to, load_transpose, rmsnorm_fm, debug=True)

    rmsnorm_fm(H, Hk, X, Xk, 1, 512, 2)
    rmsnorm_fm(Hs, Hsk, Xs, Xsk, 1, TS, 1)
    Q = Sv(0, 32).rearrange("p (k t) -> p k t", k=16)
    qtok_s = Sv(48, 56, F32)[0:TS, :]

    def epi_cq(f, th, ps, pkey):
        eng = evac_eng()
        sc.add(eng, copy_op(eng, Q[:, f, th * 512:(th + 1) * 512], ps), r=[pkey], w=[("Q", f, th)])

    def epis_cq(s, ps, pkey):
        sc.add("act", lambda e: e.copy(qtok_s[:, s * 256:(s + 1) * 256], ps), r=[pkey], w=[("qtok_s", s // 2)])

    gemm_b(w_cq, 0, 16, 0, 2048, xk_X, epi_cq, xs=xs_Xs, epis=epis_cq, sform="a")
    sc.barrier()
    if stop_after == "xq":
        return finish(nc, es, sc, H, Hs, y_cur, y_smp, S, Sv, P, CS, transposes_to, load_transpose, rmsnorm_fm, debug=True)
    Xf = X[:].rearrange("p k t -> p (k t)")
    mH = Xf[:, 0:8192].bitcast(F32).rearrange("p (k t) -> p k t", k=16)
    mnT = Xf[:, 8192:12288].rearrange("p (k t) -> p k t", k=16)
    mkT = Xf[:, 0:4096].rearrange("p (k t) -> p k t", k=16)
    mvb = Xf[:, 4096:8192].rearrange("p (m f) -> p m f", m=2)
    pT = Xf[:, 12288:14336].rearrange("p (m t) -> p m t", m=2)
    mkst = Xf[:, 14336:16384].bitcast(F32).rearrange("p (s f) -> p s f", s=4)
    pbuf = Mv(0, 512).rearrange("p (s f) -> p s f", s=2)
    pbb = Mv(512, 256, BF16).rearrange("p (s f) -> p s f", s=2)

    def dst_mem(i0, cnt, tt, pv, pkey):
        eng = evac_eng()
        sc.add(eng, copy_op(eng, mH[:, i0:i0 + cnt, tt * 128:(tt + 1) * 128], pv.rearrange("p (c t) -> p c t", c=cnt)), r=[pkey], w=[("mH", k) for k in range(i0, i0 + cnt)])
    load_transpose(mem, 2, 128, dst_mem, None, stage_kb=(32, 48))
    rmsnorm_fm(mH, lambda k, th: ("mH", k), mnT, lambda k, th: ("mnT", k), 2, 256, 1)
    sc.barrier()
    mtog = {"n": 0}
    dfm = Defer(1)

    def xt_mn(k, tt):
        return mnT[:, k, tt * 128:(tt + 1) * 128], [("mnT", k)]

    def epi_mk(s, tt, ps, pkey, which="k"):
        tick()
        mtog["n"] += 1
        sl = mtog["n"] % 4
        sc.add("act", lambda e: e.copy(mkst[:, sl, :], ps), r=[pkey], w=[("mkst", sl)])
        dsto = (mk_o if which == "k" else mv_o)[tt * 128:(tt + 1) * 128, s * 256:(s + 1) * 256]
        sc.add("sp", lambda e: e.dma_start(out=dsto, in_=mkst[:, sl, :]), r=[("mkst", sl)], dma=("mko", sl))
        if which == "v":
            sc.add("dve", lambda e: e.tensor_copy(mvb[:, tt, s * 256:(s + 1) * 256], mkst[:, sl, :]), r=[("mkst", sl)], w=[("mvb", tt, s)])
        else:
            sc.add("dve", lambda e: e.tensor_copy(pbb[:, tt, :], mkst[:, sl, :]), r=[("mkst", sl)], w=[("pbb", tt)])

            def dst_k(i0, cnt, pv, pk2):
                sc.add("act", lambda e: e.copy(mkT[:, s * 2:s * 2 + 2, tt * 128:(tt + 1) * 128], pv.rearrange("p (c t) -> p c t", c=2)), r=[pk2], w=[("mkT", s, tt)])
            dfm.push(lambda: transposes_to(dst_k, lambda i: pbb[:, tt, i * 128:(i + 1) * 128], 2, BF16, lambda i: [("pbb", tt), "identb"], None))

    from collections import deque
    pend = deque()
    tkc = {"n": 0}

    def tick(every=3):
        tkc["n"] += 1
        if tkc["n"] % every == 0 and pend:
            pend.popleft()()

    SCL = 512.0 ** -0.5
    KV = Sv(32, 48).rearrange("p (s f) -> p s f", s=4)
    KVf = Sv(32, 48, F32).rearrange("p (s f) -> p s f", s=2)
    scs = Mv(2048, 128).rearrange("p (m c) -> p m c", m=2)
    junk5 = Mv(768, 512)
    qm = Mv(1280, 512, npart=TS)
    smT = Mv(1792, 256, npart=64)
    amx_s = Mv(2376, 4, npart=64)
    pTs = Mv(2176, 64, BF16).rearrange("p (m c) -> p m c", m=2)
    oTs = Mv(2240, 128, BF16).rearrange("p (c s) -> p c s", c=16)
    sc.add("dve", lambda e: e.memset(scs, 0.0), w=["scs"])
    qb = {"n": 0}


    def kpass(s_):
        for mh in range(2):
            sc.add("sp", lambda e: e.dma_start(out=KVf[:, mh, :], in_=ck[s_, mh * 128:(mh + 1) * 128, :]), w=[("KV", 2 * mh), ("KV", 2 * mh + 1)], dma=("KVf", mh))
        for h in range(4):
            sc.add("dve", lambda e: e.tensor_scalar(qm, qtok_s[:, h * 512:(h + 1) * 512], cs("eye16", 1, s_, TS), None, op0=ALU.mult),
                   r=[("qtok_s", i) for i in range(4)] + ["CS"], w=["qm"])
            qb["n"] += 1
            b_ = (4, 7)[qb["n"] % 2]
            sc.add("pe", lambda e: e.matmul(P[b_][:], ones32[:], qm, start=True, stop=True), r=["qm", "ones32"], w=[("P", b_)])
            for mh in range(2):
                sc.add("dve", lambda e: e.scalar_tensor_tensor(junk5, KVf[:, mh, h * 512:(h + 1) * 512], 1.0, P[b_][:], op0=ALU.mult, op1=ALU.mult,
                                                               accum_out=scs[:, mh, s_ * 4 + h:s_ * 4 + h + 1]),
                       r=[("KV", 2 * mh), ("KV", 2 * mh + 1), ("P", b_), "scs"], w=["junk5", "scs"])

    def ssoftmax():
        b_ = miscbank()
        for mh in range(2):
            sc.add("pe", lambda e: e.transpose(P[b_][0:64, mh * 128:(mh + 1) * 128], scs[:, mh, :], ident), r=["scs", "CS"], w=[("P", b_)])
        sc.add("dve", lambda e: e.tensor_reduce(amx_s[:, 0:1], P[b_][0:64, 0:256], axis=AX.X, op=ALU.max), r=[("P", b_)], w=["amxs"])
        sc.add("dve", lambda e: e.tensor_scalar(amx_s[:, 1:2], amx_s[:, 0:1], -SCL, None, op0=ALU.mult), r=["amxs"], w=["amxs"])
        sc.add("act", lambda e: e.activation(out=smT, in_=P[b_][0:64, 0:256], func=AF.Exp, bias=amx_s[:, 1:2], scale=SCL, accum_out=amx_s[:, 2:3]),
               r=[("P", b_), "amxs"], w=["smT", "amxs"])
        sc.add("dve", lambda e: e.reciprocal(amx_s[:, 3:4], amx_s[:, 2:3]), r=["amxs"], w=["amxs"])
        sc.add("dve", lambda e: e.tensor_scalar(smT, smT, amx_s[:, 3:4], None, op0=ALU.mult), r=["smT", "amxs"], w=["smT"])
        b2_ = miscbank()
        for mh in range(2):
            sc.add("pe", lambda e: e.transpose(P[b2_][:, mh * 64:(mh + 1) * 64], smT[:, mh * 128:(mh + 1) * 128], CS[0:64, CO["ident"]:CO["ident"] + 64]),
                   r=["smT", "CS"], w=[("P", b2_)])
        sc.add("dve", lambda e: e.tensor_copy(Mv(2176, 64, BF16), P[b2_][:, 0:128]), r=[("P", b2_)], w=["pTs"])

    def vpass(s_):
        for mh in range(2):
            sl = (2 * s_ + mh) % 4
            sc.add("pool", lambda e: e.dma_start(out=KV[:, sl, :], in_=cv[s_, mh * 128:(mh + 1) * 128, :]), w=[("KV", sl)], dma=("KV", sl))
        for c in range(16):
            hh = c // 4
            for mh in range(2):
                sl = (2 * s_ + mh) % 4
                sc.add("pe", lambda e: e.matmul(P[4][:, c * TS + s_:c * TS + s_ + 1], KV[:, sl, c * 128:(c + 1) * 128], pTs[:, mh, s_ * 4 + hh:s_ * 4 + hh + 1],
                                                start=(mh == 0), stop=(mh == 1)), r=[("KV", sl), "pTs"], w=[("P", 4)])

    for s_ in range(TS):
        pend.append(lambda s_=s_: kpass(s_))
    pend.append(ssoftmax)
    for s_ in range(TS):
        pend.append(lambda s_=s_: vpass(s_))

    gemm_a(w_ck, 0, 16, 0, 2048, 2, xt_mn, epi_mk)
    dfm.flush()
    gemm_a(w_cv, 0, 16, 0, 2048, 2, xt_mn, lambda s, tt, ps, pkey: epi_mk(s, tt, ps, pkey, "v"))

    amx = Mv(2368, 4)
    dfp = Defer(1)
    for h in range(4):
        for tt in range(NT):
            b = mainbank()
            for dc in range(4):
                kq = 4 * h + dc
                sc.add("pe", lambda e: e.matmul(P[b][:, 0:256], Q[:, kq, tt * 128:(tt + 1) * 128], mkT[:, kq, :], start=(dc == 0), stop=(dc == 3)),
                       r=[("Q", kq, tt // 4)] + [("mkT", kq // 2, m_) for m_ in range(2)], w=[("P", b)])
            sl = tt % 2
            sc.add("dve", lambda e: e.tensor_reduce(amx[:, 0:1], P[b][:, 0:256], axis=AX.X, op=ALU.max), r=[("P", b)], w=["amx0"])
            sc.add("dve", lambda e: e.tensor_scalar(amx[:, 1:2], amx[:, 0:1], -SCL, None, op0=ALU.mult), r=["amx0"], w=["amx1"])
            sc.add("act", lambda e: e.activation(out=pbuf[:, sl, :], in_=P[b][:, 0:256], func=AF.Exp, bias=amx[:, 1:2], scale=SCL, accum_out=amx[:, 2:3]),
                   r=[("P", b), "amx1"], w=[("pbuf", sl), "amx2"])
            sc.add("dve", lambda e: e.reciprocal(amx[:, 3:4], amx[:, 2:3]), r=["amx2"], w=["amx3"])
            sc.add("dve", lambda e: e.tensor_scalar(pbb[:, sl, :], pbuf[:, sl, :], amx[:, 3:4], None, op0=ALU.mult), r=[("pbuf", sl), "amx3"], w=[("pbb", sl)])

            def dst_p(i0, cnt, pv, pk2, tt=tt):
                sc.add("act", lambda e: e.copy(pT[:, :, tt * 128:(tt + 1) * 128], pv.rearrange("p (m t) -> p m t", m=2)), r=[pk2], w=[("pT", tt // 4)])
            dfp.push(lambda dst_p=dst_p, sl=sl: transposes_to(dst_p, lambda i: pbb[:, sl, i * 128:(i + 1) * 128], 2, BF16, lambda i: [("pbb", sl), "identb"], None))
            tick()
        dfp.flush()
        for dc in range(4):
            kq = 4 * h + dc
            for th in range(2):
                b = mainbank()
                for mh in range(2):
                    sc.add("pe", lambda e: e.matmul(P[b][:], mvb[:, mh, kq * 128:(kq + 1) * 128], pT[:, mh, th * 512:(th + 1) * 512], start=(mh == 0), stop=(mh == 1)),
                           r=[("mvb", mh, kq // 2), ("pT", th)], w=[("P", b)])
                eng = evac_eng()
                sc.add(eng, copy_op(eng, Q[:, kq, th * 512:(th + 1) * 512], P[b][:]), r=[("P", b)], w=[("Q", kq, th)])
                tick()
    while pend:
        pend.popleft()()
    sc.add("dve", lambda e: e.tensor_copy(Mv(2240, 128, BF16), P[4][:, 0:256]), r=[("P", 4)], w=["oTs"])

    def xk_Q(k, th):
        return Q[:, k, th * 512:(th + 1) * 512], [("Q", k, th)]

    def xs_oTs(k):
        return oTs[:, k, :], ["oTs"]

    gemm_b(w_co, 0, 16, 0, 2048, xk_Q, epi_res, xs=xs_oTs, epis=epis_res)
    sc.barrier()
    if stop_after == "xattn":
        return finish(nc, es, sc, H, Hs, y_cur, y_smp, S, Sv, P, CS, transposes_to, load_transpose, rmsnorm_fm, debug=True)

    rmsnorm_fm(H, Hk, X, Xk, 3, 512, 2)
    rmsnorm_fm(Hs, Hsk, Xs, Xsk, 3, TS, 1)
    sc.barrier()
    hid = Sv(0, 32).rearrange("p (k t) -> p k t", k=16)
    rl = Sv(32, 36, F32).rearrange("p (s f) -> p s f", s=2)
    hids = Mv(0, 128, BF16).rearrange("p (c s) -> p c s", c=16)
    rls = Mv(128, 16)
    ftog = {"n": 0}
    for g in range(4):
        def epi_f1(f, th, ps, pkey):
            ftog["n"] += 1
            sl = ftog["n"] % 2
            sc.add("act", lambda e: e.activation(out=rl[:, sl, :], in_=ps, func=AF.Relu), r=[pkey], w=[("rl", sl)])
            sc.add("pool", lambda e: e.tensor_tensor(hid[:, f, th * 512:(th + 1) * 512], rl[:, sl, :], rl[:, sl, :], op=ALU.mult), r=[("rl", sl)], w=[("hid", f, th)])

        def epis_f1(f, ps, pkey):
            sc.add("act", lambda e: e.activation(out=rls, in_=ps, func=AF.Relu), r=[pkey], w=["rls"])
            sc.add("dve", lambda e: e.tensor_tensor(hids[:, f, :], rls, rls, op=ALU.mult), r=["rls"], w=[("hids", f)])

        gemm_b(w_ff1, 0, 16, g * 2048, 2048, xk_X, epi_f1, xs=xs_Xs, epis=epis_f1)

        def xk_hid(k, th):
            return hid[:, k, th * 512:(th + 1) * 512], [("hid", k, th)]

        def xs_hids(k):
            return hids[:, k, :], [("hids", k)]

        gemm_b(w_ff2, g * 2048, 16, 0, 2048, xk_hid, epi_res, xs=xs_hids, epis=epis_res)
    sc.barrier()
    return finish(nc, es, sc, H, Hs, y_cur, y_smp, S, Sv, P, CS, transposes_to, load_transpose, rmsnorm_fm, debug=False)


def finish(nc, es, sc, H, Hs, y_cur, y_smp, S, Sv, P, CS, transposes_to, load_transpose, rmsnorm_fm, debug):
    X_dummy = None
    if not debug:
        sq = Sv(0, 32).rearrange("p (k t) -> p k t", k=16)
        rmsnorm_fm_out(sc, H, sq, 512, 2, 4, "H", Sv, P, CS)
        sqs = Sv(32, 33).rearrange("p (k t) -> p k t", k=16)
        rmsnorm_fm_out(sc, Hs, sqs, TS, 1, 4, "Hs", Sv, P, CS)
    yst = Sv(36, 52, F32).rearrange("p (s f) -> p s f", s=2)
    cnt = {"n": 0}
    for tt in range(NT + 1):
        smp = (tt == NT)
        sl = tt % 2
        npo = TS if smp else 128

        def dst_y(i0, c, pv, pkey, sl=sl, npo=npo):
            cnt["n"] += 1
            eng = "act" if cnt["n"] % 2 else "dve"
            o = yst[0:npo, sl, i0 * 128:(i0 + c) * 128]
            if eng == "act":
                sc.add("act", lambda e: e.copy(o, pv), r=[pkey], w=[("yst", sl)])
            else:
                sc.add("dve", lambda e: e.tensor_copy(o, pv), r=[pkey], w=[("yst", sl)])
        if smp:
            src_of = lambda i: Hs[:, i, :]
            keys = lambda i: [("Hs", i), "CS"]
        else:
            src_of = lambda i, tt=tt: H[:, i, tt * 128:(tt + 1) * 128]
            keys = lambda i, tt=tt: [("H", i, tt // 4), "CS"]
        transposes_to(dst_y, src_of, 16, F32, keys, None, np_in=128, np_out=npo)
        dsto = y_smp[:, :] if smp else y_cur[tt * 128:(tt + 1) * 128, :]
        sc.add("sp", lambda e, dsto=dsto, sl=sl, npo=npo: e.dma_start(out=dsto, in_=yst[0:npo, sl, :]), r=[("yst", sl)], dma=("yout", sl))
    sc.emit(nc, es)
    return nc, es


def rmsnorm_fm_out(sc, Hbuf, sq, ncols, nhalf, gi, hname, Sv, P, CS):
    rst = Sv(52, 56, F32).rearrange("p (h t) -> p h t", h=2)
    for th in range(nhalf):
        cols = slice(th * ncols, (th + 1) * ncols)
        hkey = (lambda k: (hname, k, th)) if hname == "H" else (lambda k: (hname, k))
        b = 5 + th
        for k in range(16):
            sc.add("act", lambda e, o=sq[:, k, cols], i=Hbuf[:, k, cols]: e.activation(out=o, in_=i, func=AF.Square), r=[hkey(k)], w=[("sq", hname, k, th)])
            sc.add("pe", lambda e, o=P[b][:, 0:ncols], r_=sq[:, k, cols], st=(k == 0), sp=(k == 15): e.matmul(o, ONESB[0][:], r_, start=st, stop=sp),
                   r=[("sq", hname, k, th), "onesb"], w=[("P", b)])
        rv = rst[:, th, 0:ncols]
        sc.add("act", lambda e, o=rv, i=P[b][:, 0:ncols]: e.activation(out=o, in_=i, func=AF.Sqrt, bias=EPS, scale=1.0 / D), r=[("P", b)], w=[("rstf", th)])
        sc.add("dve", lambda e, o=rv: e.reciprocal(o, o), r=[("rstf", th)], w=[("rstf", th)])
        for k in range(16):
            g = CS[:, CO["gpk"] + gi * 16 + k:CO["gpk"] + gi * 16 + k + 1]
            sc.add("dve", lambda e, o=Hbuf[:, k, cols], g=g, rv=rv: e.scalar_tensor_tensor(o, o, g, rv, op0=ALU.mult, op1=ALU.mult),
                   r=[hkey(k), ("rstf", th), "CS"], w=[hkey(k)])


ONESB = [None]


def _consts(hf):
    c = np.zeros((128, NCST), np.float32)

    def put(name, arr):
        arr = np.asarray(arr, np.float32)
        c[:arr.shape[0], CO[name]:CO[name] + arr.shape[1]] = arr

    put("ident", np.eye(128))
    j = np.arange(128)
    m = (j[:, None] <= j[None, :]).astype(np.float32)
    put("maskT", np.concatenate([m, m], axis=1))
    half = 64
    freqs = (10000.0 ** (-np.arange(half, dtype=np.float32) / half)).astype(np.float32)

    def rope(pos):
        ang = pos.astype(np.float32)[:, None] * freqs[None, :]
        return np.cos(ang).astype(np.float32), np.sin(ang).astype(np.float32)

    pos_c = (hf * T + np.arange(T)).astype(np.float32)
    cc_, ss_ = rope(pos_c)
    put("cosc", cc_.reshape(8, 128, 64).transpose(1, 0, 2).reshape(128, 512))
    put("sinc", ss_.reshape(8, 128, 64).transpose(1, 0, 2).reshape(128, 512))
    cp, sp_ = rope(np.arange(T).astype(np.float32))
    ropep = np.concatenate([cp.reshape(8, 128, 64).transpose(1, 0, 2).reshape(128, 512),
                            sp_.reshape(8, 128, 64).transpose(1, 0, 2).reshape(128, 512)], axis=1).astype(np.float32)
    c16, s16 = rope(np.full((128,), 16384.0, np.float32))
    put("coss", c16)
    put("sins", s16)
    g = np.array(GAM, np.float64)
    t = np.arange(T, dtype=np.float64)
    qs = np.exp(np.log(g)[None, :] * t[:, None])
    ks = np.exp(-np.log(g)[None, :] * t[:, None]) * (128.0 ** -0.5)
    put("qs", qs.reshape(8, 128, 8).transpose(1, 0, 2).reshape(128, 64))
    put("ks", ks.reshape(8, 128, 8).transpose(1, 0, 2).reshape(128, 64))
    put("gfin", np.tile((g ** 1023)[None, :], (128, 1)))
    put("gini", np.tile((g ** 1024)[None, :], (128, 1)))
    put("eye16", np.eye(16))
    put("eyeq", np.tile(np.eye(16).reshape(1, 256), (128, 1)))
    return c, ropep


_CACHE = {}


def kernel(x_prompt, x_sample, mem_prompt, cache_mem_k, cache_mem_v, state_ret,
           norm1_g, w_in, sgu_norm_g, sgu_w_s, sgu_b, ret_gn_g, w_out, norm2_g,
           mem_norm_g, w_cq, w_ck, w_cv, w_co, norm3_g, w_ff1, w_ff2, final_norm_g):
    f = lambda a: np.ascontiguousarray(np.asarray(a, dtype=np.float32))
    x_prompt, x_sample, mem_prompt = f(x_prompt), f(x_sample), f(mem_prompt)
    cache_mem_k, cache_mem_v, state_ret = f(cache_mem_k), f(cache_mem_v), f(state_ret)
    if "nc" not in _CACHE:
        _CACHE["nc"] = build()
    nc, _es = _CACHE["nc"]
    gains = [f(norm1_g)[0], f(norm2_g)[0], f(mem_norm_g)[0], f(norm3_g)[0], f(final_norm_g)]
    gpk = np.stack([gn.reshape(16, 128).T for gn in gains], axis=1).reshape(128, 80)
    gA = f(sgu_norm_g)[0].reshape(8, 128).T
    gnB = f(ret_gn_g)[0].reshape(8, 128).T
    ws = f(sgu_w_s)[0]
    sbias = f(sgu_b)[0]
    ws00 = np.tile(ws[:, 0, 0][None, :], (128, 1))
    b0 = np.tile(sbias[:, 0][None, :], (128, 1))
    wsT = np.ascontiguousarray(ws.transpose(2, 0, 1).reshape(128, 1024))
    b_bc = np.ascontiguousarray(np.tile(sbias.reshape(1, 1024), (128, 1)))
    zeros_prev = np.zeros((T, D), np.float32)
    shared = dict(w_in=f(w_in)[0], w_out=f(w_out)[0], w_cq=f(w_cq)[0], w_ck=f(w_ck)[0], w_cv=f(w_cv)[0], w_co=f(w_co)[0],
                  w_ff1=f(w_ff1)[0], w_ff2=f(w_ff2)[0], wsT=wsT, b_bc=b_bc)
    in_maps = []
    for c in range(8):
        b, hf = c // 2, c % 2
        cst, ropep = _consts(hf)
        for name, arr in (("gpk", gpk), ("gA", gA), ("gnB", gnB), ("ws00", ws00), ("b0", b0)):
            cst[:, CO[name]:CO[name] + arr.shape[1]] = arr
        m = dict(shared)
        m.update(
            x_cur=np.ascontiguousarray(x_prompt[b, hf * T:(hf + 1) * T]),
            x_prev=np.ascontiguousarray(x_prompt[b, 0:T]) if hf == 1 else zeros_prev,
            x_smp=np.ascontiguousarray(x_sample[c * TS:(c + 1) * TS, 0]),
            mem=np.ascontiguousarray(mem_prompt[b]),
            ck=np.ascontiguousarray(cache_mem_k[0, c * TS:(c + 1) * TS].reshape(TS, 256, D)),
            cv=np.ascontiguousarray(cache_mem_v[0, c * TS:(c + 1) * TS].reshape(TS, 256, D)),
            st_in=np.ascontiguousarray(state_ret[0, c * TS:(c + 1) * TS]),
            cst=cst, ropep=ropep,
        )
        in_maps.append(m)
    if _CACHE.get("test_cores"):
        n = _CACHE["test_cores"]
        res = run_bass_kernel_spmd(nc, in_maps[:n], core_ids=list(range(n)), trace=bool(_CACHE.get("trace")))
        _CACHE["exec_ns"] = res.exec_time_ns
        return res.results
    res = run_bass_kernel_spmd(nc, in_maps, core_ids=list(range(8)))
    R = res.results
    y_prompt = np.zeros((4, 2048, D), np.float32)
    y_sample = np.zeros((128, 1, D), np.float32)
    mk = np.zeros((1, 4, 256, 4, 512), np.float32)
    mv = np.zeros((1, 4, 256, 4, 512), np.float32)
    sp = np.zeros((1, 4, 8, 128, 128), np.float32)
    ss = np.zeros((1, 128, 8, 128, 128), np.float32)
    cvs = np.zeros((1, 128, 1, 8, 128), np.float32)
    for c in range(8):
        b, hf = c // 2, c % 2
        r = R[c]
        y_prompt[b, hf * T:(hf + 1) * T] = r["y_cur"]
        y_sample[c * TS:(c + 1) * TS, 0] = r["y_smp"]
        ss[0, c * TS:(c + 1) * TS] = r["sts_o"]
        cvs[0, c * TS:(c + 1) * TS, 0] = r["cvs_o"].reshape(TS, 8, 128)
        if hf == 0:
            mk[0, b] = r["mk_o"].reshape(256, 4, 512)
            mv[0, b] = r["mv_o"].reshape(256, 4, 512)
        else:
            sp[0, b] = r["stp_o"]
    return (y_prompt, y_sample, mk, mv, sp, ss, cvs)
```

```python
import contextlib
import numpy as np
import concourse.bass as bass
import concourse.mybir as mybir
from concourse.bass_utils import run_bass_kernel_spmd

F32 = mybir.dt.float32
BF16 = mybir.dt.bfloat16
AF = mybir.ActivationFunctionType
ALU = mybir.AluOpType
AX = mybir.AxisListType

D = 2048
T = 1024
NT = 8
TS = 16
EPS = 1e-6
GAM = [1.0 - 2.0 ** (-5 - h) for h in range(8)]
ENGS = ("pe", "act", "dve", "pool", "sp")

CO = {}
_c = 0
for _n, _w in (("ident", 128), ("maskT", 256), ("gpk", 80), ("gA", 8), ("gnB", 8),
               ("ws00", 8), ("b0", 8), ("cosc", 512), ("sinc", 512),
               ("coss", 64), ("sins", 64), ("qs", 64), ("ks", 64), ("gfin", 8), ("gini", 8),
               ("eye16", 16), ("eyeq", 256)):
    CO[_n] = _c
    _c += _w
NCST = _c


class Op:
    __slots__ = ("eng", "fn", "deps", "signal", "sigval", "dma", "dval", "idx")


class _Rec:
    def __init__(self):
        self.call = None

    def __getattr__(self, name):
        def f(*a, **k):
            self.call = (name, a, k)
            return None
        return f


class Sched:
    def __init__(self):
        self.ops = []
        self.by_eng = {e: [] for e in ENGS}
        self.lastw = {}
        self.readers = {}
        self.dcount = {}
        self.last_real = {e: None for e in ENGS}
        self.dma_since = []

    def add(self, eng, fn, r=(), w=(), dma=None):
        op = Op()
        if fn is not None:
            rec = _Rec()
            fn(rec)
            assert rec.call is not None
            call = rec.call
            fn = lambda e, call=call: getattr(e, call[0])(*call[1], **call[2])
        op.eng, op.fn, op.dma, op.signal, op.sigval, op.dval = eng, fn, dma, False, 0, 0
        op.idx = len(self.ops)
        deps = {}

        def adddep(d):
            if d is None or d is op:
                return
            if d.dma is None and dma is None and d.eng == "pe" and eng == "pe":
                return
            key = ("d", id(d)) if d.dma is not None else ("e", d.eng)
            o = deps.get(key)
            if o is None or o.idx < d.idx:
                deps[key] = d

        for k in list(r) + list(w):
            adddep(self.lastw.get(k))
        for k in w:
            rd = self.readers.get(k)
            if rd:
                for d in rd.values():
                    adddep(d)
        op.deps = list(deps.values())
        for d in op.deps:
            d.signal = True
        for k in w:
            self.lastw[k] = op
            self.readers[k] = {}
        for k in r:
            rk = ("d", op.idx) if dma is not None else ("e", eng)
            self.readers.setdefault(k, {})[rk] = op
        if dma is not None:
            self.dcount[dma] = self.dcount.get(dma, 0) + 1
            op.dval = 16 * self.dcount[dma]
            self.dma_since.append(op)
        self.ops.append(op)
        self.by_eng[eng].append(op)
        if fn is not None:
            self.last_real[eng] = op
        return op

    def barrier(self):
        lasts = [self.last_real[e] for e in ENGS if self.last_real[e] is not None]
        dmas = list(self.dma_since)
        self.dma_since = []
        for e in ENGS:
            op = Op()
            op.eng, op.fn, op.dma, op.signal, op.sigval, op.dval = e, None, None, False, 0, 0
            op.idx = len(self.ops)
            deps = []
            for d in lasts:
                if d.eng != e or d.dma is not None:
                    deps.append(d)
            chan = {}
            for d in dmas:
                if d.dma not in chan or chan[d.dma].idx < d.idx:
                    chan[d.dma] = d
            deps += [d for d in chan.values() if d not in deps]
            op.deps = deps
            for d in deps:
                d.signal = True
            self.ops.append(op)
            self.by_eng[e].append(op)

    def emit(self, nc, es):
        esem = {e: es.enter_context(nc.semaphore("s_" + e)) for e in ENGS}
        dsem = {ch: es.enter_context(nc.semaphore("d_%s" % str(ch))) for ch in self.dcount}
        for e in ENGS:
            c = 0
            for op in self.by_eng[e]:
                if op.signal and op.dma is None and op.fn is not None:
                    c += 1
                    op.sigval = c
        block = es.enter_context(nc.Block())

        def run(ename, eng):
            waited = {}
            for op in self.by_eng[ename]:
                for d in op.deps:
                    if d.dma is not None:
                        sem, val = dsem[d.dma], d.dval
                    else:
                        sem, val = esem[d.eng], d.sigval
                    if waited.get(id(sem), 0) < val:
                        eng.wait_ge(sem, val)
                        waited[id(sem)] = val
                if op.fn is None:
                    continue
                ins = op.fn(eng)
                if op.dma is not None:
                    ins.then_inc(dsem[op.dma], 16)
                elif op.signal:
                    ins.then_inc(esem[ename], 1)
            if ename == "sp":
                for ch, n in self.dcount.items():
                    eng.wait_ge(dsem[ch], 16 * n)

        block.tensor(lambda t: run("pe", t))
        block.scalar(lambda t: run("act", t))
        block.vector(lambda t: run("dve", t))
        block.gpsimd(lambda t: run("pool", t))
        block.sync(lambda t: run("sp", t))


class Defer:
    def __init__(self, depth):
        self.q = []
        self.depth = depth

    def push(self, thunk):
        self.q.append(thunk)
        while len(self.q) > self.depth:
            self.q.pop(0)()

    def flush(self):
        while self.q:
            self.q.pop(0)()


def vw(base, dims, npart=None):
    p = base.ap[0]
    return bass.AP(base.tensor, base.offset, [[p[0], npart if npart else p[1]]] + [list(d) for d in dims])


def build(stop_after=None):
    nc = bass.Bass("TRN2", target_bir_lowering=False)
    es = contextlib.ExitStack()
    sc = Sched()

    def din(name, shape):
        return nc.dram_tensor(name, list(shape), F32, kind="ExternalInput").ap()

    def dout(name, shape):
        return nc.dram_tensor(name, list(shape), F32, kind="ExternalOutput").ap()

    x_cur = din("x_cur", (T, D)); x_prev = din("x_prev", (T, D)); x_smp = din("x_smp", (TS, D))
    mem = din("mem", (256, D)); ck = din("ck", (TS, 256, D)); cv = din("cv", (TS, 256, D))
    st_in = din("st_in", (TS, 8, 128, 128)); cst_d = din("cst", (128, NCST)); wsT_d = din("wsT", (128, 1024)); ropep_d = din("ropep", (128, 1024)); bbc_d = din("b_bc", (128, 1024))
    w_in = din("w_in", (D, 6144)); w_out = din("w_out", (D, D)); w_cq = din("w_cq", (D, D))
    w_ck = din("w_ck", (D, D)); w_cv = din("w_cv", (D, D)); w_co = din("w_co", (D, D))
    w_ff1 = din("w_ff1", (D, 8192)); w_ff2 = din("w_ff2", (8192, D))
    y_cur = dout("y_cur", (T, D)); y_smp = dout("y_smp", (TS, D)); mk_o = dout("mk_o", (256, D)); mv_o = dout("mv_o", (256, D))
    stp_o = dout("stp_o", (8, 128, 128)); sts_o = dout("sts_o", (TS, 8, 128, 128)); cvs_o = dout("cvs_o", (TS, 1024))

    def sb(name, shape, dt):
        return es.enter_context(nc.sbuf_tensor(name, list(shape), dt))

    H = sb("H", (128, 16, T), F32)
    X = sb("X", (128, 16, T), BF16)
    S = sb("S", (128, 28672), BF16)
    W = sb("W", (128, 3, 4096), BF16)
    CS = sb("CS", (128, NCST), F32)
    wsTb = sb("wsTb", (128, 8, 128), BF16)
    identb = sb("identb", (128, 128), BF16)
    onesb = sb("onesb", (128, 128), BF16)
    ones32 = sb("ones32", (16, 128), F32)
    Hs = sb("Hs", (128, 16, TS), F32)
    Xs = sb("Xs", (128, 16, TS), BF16)
    ccs = sb("ccs", (128, 8, TS), BF16)
    smx = sb("smx", (128, 64), F32)
    M = sb("M", (128, 3328), F32)
    rstb = sb("rstb", (128, 512), F32)
    ONESB[0] = onesb

    def Mv(o, n, dt=F32, npart=128):
        a = M[0:npart, o:o + n]
        return a.bitcast(BF16) if dt == BF16 else a
    P = [es.enter_context(nc.psum_tensor("P%d" % i, [128, 512], F32)) for i in range(8)]

    def cs(name, w, c0=0, npart=128):
        o = CO[name] + c0
        return CS[0:npart, o:o + w]

    ident = cs("ident", 128)

    def Sv(kb0, kb1, dt=BF16):
        a = S[:, kb0 * 512:kb1 * 512]
        return a.bitcast(F32) if dt == F32 else a

    rr = {"main": 0, "misc": 0, "smp": 0}

    def mainbank():
        b = rr["main"] % 4
        rr["main"] += 1
        return b

    def miscbank():
        b = 5 + rr["misc"] % 2
        rr["misc"] += 1
        return b

    def smpreg():
        r_ = rr["smp"] % 2
        rr["smp"] += 1
        return (4, 7)[r_]

    tog = {"n": 0}

    def evac_eng():
        tog["n"] += 1
        return "act" if tog["n"] % 2 else "dve"

    def copy_op(eng, out, in_):
        if eng == "act":
            return lambda e: e.copy(out, in_)
        return lambda e: e.tensor_copy(out, in_)

    wplan = []
    wstate = {"issued": 0, "used": 0}

    def wdeclare(dram, r0, kc, c0, fc):
        wplan.append((dram, r0, kc, c0, fc))

    def wissue_upto(n):
        while wstate["issued"] < min(n, len(wplan)):
            i = wstate["issued"]
            dram, r0, kc, c0, fc = wplan[i]
            slot = i % 3
            dst = W[:, slot, :].rearrange("p (k f) -> p k f", k=kc)
            src = dram[r0:r0 + kc * 128, c0:c0 + fc].rearrange("(k p) f -> p k f", p=128)
            sc.add("pool", lambda e, dst=dst, src=src: e.dma_start(out=dst, in_=src), w=[("W", slot)], dma=("W", slot))
            wstate["issued"] += 1

    def wnext(dram, r0, kc, c0, fc):
        i = wstate["used"]
        assert wplan[i][1:] == (r0, kc, c0, fc) and wplan[i][0].tensor.name == dram.tensor.name, (i, wplan[i][1:], (r0, kc, c0, fc))
        wissue_upto(i + 3)
        wstate["used"] += 1
        slot = i % 3
        return W[:, slot, :].rearrange("p (k f) -> p k f", k=kc), ("W", slot)

    def gemm_b(dram, r0, kc, c0, ncols, xk, epi, xs=None, epis=None, sform="b"):
        fc = 4096 // kc
        for s in range(ncols // fc):
            wv, wkey = wnext(dram, r0, kc, c0 + s * fc, fc)
            for j in range(fc // 128):
                f = s * (fc // 128) + j
                banks = [mainbank(), mainbank()]
                sreg = smpreg() if (xs is not None and sform == "b") else None
                for k in range(kc):
                    lhsT = wv[:, k, j * 128:(j + 1) * 128]
                    for th in range(2):
                        xa, xkeys = xk(k, th)
                        sc.add("pe", lambda e, o=P[banks[th]][:], l=lhsT, r_=xa, st=(k == 0), sp=(k == kc - 1):
                               e.matmul(o, l, r_, start=st, stop=sp), r=[wkey] + xkeys, w=[("P", banks[th])])
                    if sreg is not None:
                        xa, xkeys = xs(k)
                        sc.add("pe", lambda e, o=P[sreg][:, 0:16], l=lhsT, r_=xa, st=(k == 0), sp=(k == kc - 1):
                               e.matmul(o, l, r_, start=st, stop=sp), r=[wkey] + xkeys, w=[("P", sreg)])
                for th in range(2):
                    epi(f, th, P[banks[th]][:], ("P", banks[th]))
                if sreg is not None:
                    epis(f, P[sreg][:, 0:16], ("P", sreg))
            if xs is not None and sform == "a":
                b = miscbank()
                for k in range(kc):
                    xa, xkeys = xs(k)
                    sc.add("pe", lambda e, o=P[b][0:TS, 0:fc], l=xa, r_=wv[:, k, :], st=(k == 0), sp=(k == kc - 1):
                           e.matmul(o, l, r_, start=st, stop=sp), r=[wkey] + xkeys, w=[("P", b)])
                epis(s, P[b][0:TS, 0:fc], ("P", b))

    def gemm_a(dram, r0, kc, c0, ncols, ntiles, xt, epi, xs=None, epis=None):
        fc = 4096 // kc
        for s in range(ncols // fc):
            wv, wkey = wnext(dram, r0, kc, c0 + s * fc, fc)
            for tt in range(ntiles):
                b = mainbank()
                for k in range(kc):
                    xa, xkeys = xt(k, tt)
                    sc.add("pe", lambda e, o=P[b][:, 0:fc], l=xa, r_=wv[:, k, :], st=(k == 0), sp=(k == kc - 1):
                           e.matmul(o, l, r_, start=st, stop=sp), r=[wkey] + xkeys, w=[("P", b)])
                epi(s, tt, P[b][:, 0:fc], ("P", b))
            if xs is not None:
                b = miscbank()
                for k in range(kc):
                    xa, xkeys = xs(k)
                    sc.add("pe", lambda e, o=P[b][0:TS, 0:fc], l=xa, r_=wv[:, k, :], st=(k == 0), sp=(k == kc - 1):
                           e.matmul(o, l, r_, start=st, stop=sp), r=[wkey] + xkeys, w=[("P", b)])
                epis(s, P[b][0:TS, 0:fc], ("P", b))

    def gemm_s(dram, r0, kc, c0, ncols, xs, epis):
        fc = 4096 // kc
        for s in range(ncols // fc):
            wv, wkey = wnext(dram, r0, kc, c0 + s * fc, fc)
            for j in range(fc // 128):
                f = s * (fc // 128) + j
                sreg = smpreg()
                for k in range(kc):
                    xa, xkeys = xs(k)
                    sc.add("pe", lambda e: e.matmul(P[sreg][:, 0:16], wv[:, k, j * 128:(j + 1) * 128], xa, start=(k == 0), stop=(k == kc - 1)),
                           r=[wkey] + xkeys, w=[("P", sreg)])
                epis(f, P[sreg][:, 0:16], ("P", sreg))

    def transposes_to(dst_of, src_of, n, dt, srckeys, dstkeys, np_in=128, np_out=128, evac=None):
        grp = 4
        for i0 in range(0, n, grp):
            cnt = min(grp, n - i0)
            b = miscbank()
            if dt == F32:
                pv = P[b][0:np_out, :]
                idn = CS[0:np_in, CO["ident"]:CO["ident"] + np_in]
            else:
                pv = P[b][:].bitcast(BF16)[0:np_out, 0:512]
                idn = identb[0:np_in, 0:np_in]
            for i in range(cnt):
                sc.add("pe", lambda e, o=pv[:, i * np_in:(i + 1) * np_in], s_=src_of(i0 + i), idn=idn: e.transpose(o, s_, idn),
                       r=srckeys(i0 + i), w=[("P", b)])
            dst_of(i0, cnt, pv[:, 0:cnt * np_in], ("P", b))

    sc.add("sp", lambda e: e.dma_start(out=CS[:], in_=cst_d[:, :]), w=["CS"], dma="cst")
    wst32 = Sv(16, 20, F32)
    sc.add("sp", lambda e: e.dma_start(out=wst32, in_=wsT_d[:, :]), w=["wst32"], dma="cst2")
    sc.add("dve", lambda e: e.tensor_copy(identb[:], ident), r=["CS"], w=["identb"])
    sc.add("dve", lambda e: e.memset(onesb[:], 1.0), w=["onesb"])
    sc.add("dve", lambda e: e.memset(ones32[:], 1.0), w=["ones32"])
    sc.add("dve", lambda e: e.tensor_tensor(wsTb[:], wst32.rearrange("p (h i) -> p h i", h=8),
                                            vw(cs("maskT", 128), [[0, 8], [1, 128]]), op=ALU.mult), r=["CS", "wst32"], w=["wsTb"])

    sc.barrier()
    ropep = Sv(32, 36, F32)
    sc.add("sp", lambda e: e.dma_start(out=ropep, in_=ropep_d[:, :]), w=["ropep"], dma="cst3")
    for hp in range(4):
        wdeclare(w_in, 0, 16, 3072 + hp * 256, 256)
        wdeclare(w_in, 0, 16, 4096 + hp * 256, 256)
    for s in range(4):
        wdeclare(w_in, 0, 16, 1024 + s * 256, 256)
    for s in range(4):
        wdeclare(w_in, 0, 16, s * 256, 256)
    for s in range(4):
        wdeclare(w_out, 0, 8, s * 512, 512)
    for hp in range(4):
        for base in (2048, 3072, 4096, 5120):
            wdeclare(w_in, 0, 16, base + hp * 256, 256)
    for s in range(4):
        wdeclare(w_out, 1024, 8, s * 512, 512)
    for s in range(4):
        wdeclare(w_out, 1024, 8, s * 512, 512)
    for wd in (w_cq, w_ck, w_cv, w_co):
        for s in range(8):
            wdeclare(wd, 0, 16, s * 256, 256)
    for g in range(4):
        for s in range(8):
            wdeclare(w_ff1, 0, 16, g * 2048 + s * 256, 256)
        for s in range(8):
            wdeclare(w_ff2, g * 2048, 16, s * 256, 256)
    wissue_upto(2)

    def load_transpose(src, ntok_tiles, rows_per_tile, dst, dkey, stage_kb=(0, 16), gain=None, ssq=None):
        st = Sv(stage_kb[0], stage_kb[1], F32).rearrange("p (s f) -> p s f", s=2)
        for tt in range(ntok_tiles):
            slot = tt % 2
            stv = st[0:rows_per_tile, slot, :]
            sc.add("sp", lambda e, o=stv, i=src[tt * rows_per_tile:(tt + 1) * rows_per_tile, :]: e.dma_start(out=o, in_=i),
                   w=[("xst", slot)], dma=("xst", slot))
            if ssq is not None:
                ssq(tt, stv, ("xst", slot))

            def dst_of(i0, cnt, pv, pkey, tt=tt):
                dst(i0, cnt, tt, pv, pkey)
            transposes_to(dst_of, lambda i, stv=stv: stv[:, i * 128:(i + 1) * 128], 16, F32,
                          lambda i, slot=slot: [("xst", slot), "CS"], None, np_in=rows_per_tile, np_out=128)

    def rmsnorm_fm(Hbuf, hkey, Xbuf, xkey, gi, ncols, nhalf, out32=False):
        for th in range(nhalf):
            cols = slice(th * ncols, (th + 1) * ncols)
            b = miscbank()
            for k in range(16):
                sc.add("act", lambda e, o=Xbuf[:, k, cols], i=Hbuf[:, k, cols]: e.activation(out=o, in_=i, func=AF.Square),
                       r=[hkey(k, th)], w=[xkey(k, th)])
                sc.add("pe", lambda e, o=P[b][:, 0:ncols], r_=Xbuf[:, k, cols], st=(k == 0), sp=(k == 15):
                       e.matmul(o, onesb[:], r_, start=st, stop=sp), r=[xkey(k, th), "onesb"], w=[("P", b)])
            rv = rstb[:, 0:ncols]
            sc.add("act", lambda e, o=rv, i=P[b][:, 0:ncols]: e.activation(out=o, in_=i, func=AF.Sqrt, bias=EPS, scale=1.0 / D),
                   r=[("P", b)], w=[("rst", 0)])
            sc.add("dve", lambda e, o=rv: e.reciprocal(o, o), r=[("rst", 0)], w=[("rst", 0)])
            for k in range(16):
                g = cs("gpk", 1, gi * 16 + k)
                if out32:
                    sc.add("dve", lambda e, o=Hbuf[:, k, cols], g=g, rv=rv: e.scalar_tensor_tensor(o, o, g, rv, op0=ALU.mult, op1=ALU.mult),
                           r=[hkey(k, th), ("rst", 0), "CS"], w=[hkey(k, th)])
                else:
                    sc.add("dve", lambda e, o=Xbuf[:, k, cols], i=Hbuf[:, k, cols], g=g, rv=rv:
                           e.scalar_tensor_tensor(o, i, g, rv, op0=ALU.mult, op1=ALU.mult),
                           r=[hkey(k, th), ("rst", 0), "CS"], w=[xkey(k, th)])

    Hk = lambda k, th: ("H", k, th)
    Xk = lambda k, th: ("X", k, th)
    Hsk = lambda k, th: ("Hs", k)
    Xsk = lambda k, th: ("Xs", k)

    def xk_X(k, th):
        return X[:, k, th * 512:(th + 1) * 512], [("X", k, th)]

    def xt_X(k, tt):
        return X[:, k, tt * 128:(tt + 1) * 128], [("X", k, tt // 4)]

    def xs_Xs(k):
        return Xs[:, k, :], [("Xs", k)]

    def rope_scale(dst_bf, ps, pkey, cosA, sinA, scaleA, nparts, tmpv, tmpkey, outkeys, imm_scale=None, extra=()):
        xs_ = tmpv[0:nparts, 0, :]
        t1 = tmpv[0:nparts, 1, 0:128]
        t2 = tmpv[0:nparts, 2, 0:128]
        if scaleA is not None:
            sc.add("dve", lambda e: e.tensor_tensor(xs_.rearrange("p (h d) -> p h d", h=2), ps.rearrange("p (h d) -> p h d", h=2),
                                                    vw(scaleA, [[1, 2], [0, 128]]), op=ALU.mult), r=[pkey, "CS"], w=[tmpkey])
        else:
            sc.add("act", lambda e: e.activation(out=xs_, in_=ps, func=AF.Copy, scale=(imm_scale or 1.0)), r=[pkey], w=[tmpkey])
        xv = xs_.rearrange("p (h t d) -> p h t d", h=2, t=2)
        dv = dst_bf.rearrange("p (h t d) -> p h t d", h=2, t=2)
        cb = vw(cosA, [[0, 2], [1, 64]])
        sb_ = vw(sinA, [[0, 2], [1, 64]])
        t1v = t1.rearrange("p (h d) -> p h d", h=2)
        t2v = t2.rearrange("p (h d) -> p h d", h=2)
        rk = [tmpkey, "CS"] + list(extra)
        sc.add("dve", lambda e: e.tensor_tensor(t1v, xv[:, :, 0, :], cb, op=ALU.mult), r=rk, w=[tmpkey + ("a",)])
        sc.add("dve", lambda e: e.tensor_tensor(t2v, xv[:, :, 1, :], sb_, op=ALU.mult), r=rk, w=[tmpkey + ("b",)])
        sc.add("dve", lambda e: e.tensor_tensor(dv[:, :, 0, :], t1v, t2v, op=ALU.subtract), r=[tmpkey + ("a",), tmpkey + ("b",)], w=outkeys)
        t3 = tmpv[0:nparts, 1, 128:256].rearrange("p (h d) -> p h d", h=2)
        t4 = tmpv[0:nparts, 2, 128:256].rearrange("p (h d) -> p h d", h=2)
        sc.add("dve", lambda e: e.tensor_tensor(t3, xv[:, :, 0, :], sb_, op=ALU.mult), r=rk, w=[tmpkey + ("c",)])
        sc.add("dve", lambda e: e.tensor_tensor(t4, xv[:, :, 1, :], cb, op=ALU.mult), r=rk, w=[tmpkey + ("d",)])
        sc.add("dve", lambda e: e.tensor_tensor(dv[:, :, 1, :], t3, t4, op=ALU.add), r=[tmpkey + ("c",), tmpkey + ("d",)], w=outkeys)

    ropetmp = [Mv(0, 768).rearrange("p (a f) -> p a f", a=3), Mv(2560, 768).rearrange("p (a f) -> p a f", a=3)]
    rtog = {"n": 0}

    def ropeslot():
        rtog["n"] += 1
        i_ = rtog["n"] % 2
        return ropetmp[i_], ("rtmp", i_)

    Ainit = Sv(52, 56, F32).rearrange("p (h e) -> p h e", h=8)
    kp_tok = Sv(16, 20).rearrange("p (t f) -> p t f", t=8)
    vp_tok = Sv(20, 24).rearrange("p (t f) -> p t f", t=8)
    rstd_p = smx[:, 0:8]
    ssq_p = smx[:, 8:16]
    junkp = Sv(24, 32, F32)

    def ssq_prev(tt, stv, skey):
        sc.add("act", lambda e: e.activation(out=junkp, in_=stv, func=AF.Square, accum_out=ssq_p[:, tt:tt + 1]), r=[skey], w=["junkp", ("ssqp", tt)])

    def dst_prev(i0, cnt, tt, pv, pkey):
        eng = evac_eng()
        for i in range(cnt):
            k = i0 + i
            if eng == "act":
                sc.add("act", lambda e, o=X[:, k, tt * 128:(tt + 1) * 128], i_=pv[:, i * 128:(i + 1) * 128], g=cs("gpk", 1, k):
                       e.activation(out=o, in_=i_, func=AF.Copy, scale=g), r=[pkey, "CS"], w=[("X", k, tt // 4)])
            else:
                sc.add("dve", lambda e, o=X[:, k, tt * 128:(tt + 1) * 128], i_=pv[:, i * 128:(i + 1) * 128], g=cs("gpk", 1, k):
                       e.tensor_scalar(o, i_, g, None, op0=ALU.mult), r=[pkey, "CS"], w=[("X", k, tt // 4)])

    sc.add("dve", lambda e: e.memset(ssq_p, 0.0), w=[("ssqp", t_) for t_ in range(8)])
    load_transpose(x_prev, NT, 128, dst_prev, None, ssq=ssq_prev)
    sc.add("act", lambda e: e.activation(out=rstd_p, in_=ssq_p, func=AF.Sqrt, bias=EPS, scale=1.0 / D), r=[("ssqp", t_) for t_ in range(8)], w=["rstdp"])
    sc.add("dve", lambda e: e.reciprocal(rstd_p, rstd_p), r=["rstdp"], w=["rstdp"])

    for hp in range(4):
        def epi_kp2(s, tt, ps, pkey, hp=hp):
            tv, tk = ropeslot()
            scl = smx[:, 16 + 2 * (tt % 2):18 + 2 * (tt % 2)]
            sc.add("dve", lambda e: e.tensor_scalar(scl, cs("ks", 2, tt * 8 + hp * 2), rstd_p[:, tt:tt + 1], None, op0=ALU.mult),
                   r=["CS", "rstdp"], w=[tk])
            rope_scale(kp_tok[:, tt, :], ps, pkey, ropep[:, tt * 64:(tt + 1) * 64], ropep[:, 512 + tt * 64:512 + (tt + 1) * 64], scl, 128, tv, tk, [("kp", tt)], extra=["ropep"])

        def epi_vp(s, tt, ps, pkey):
            sc.add("act", lambda e: e.activation(out=vp_tok[:, tt, :], in_=ps, func=AF.Copy, scale=rstd_p[:, tt:tt + 1]), r=[pkey, "rstdp"], w=[("vp", tt)])

        gemm_a(w_in, 0, 16, 3072 + hp * 256, 256, NT, xt_X, epi_kp2)
        gemm_a(w_in, 0, 16, 4096 + hp * 256, 256, NT, xt_X, epi_vp)
        b = miscbank()
        for hh in range(2):
            for tt in range(NT):
                sc.add("pe", lambda e, o=P[b][:, hh * 128:(hh + 1) * 128], l=kp_tok[:, tt, hh * 128:(hh + 1) * 128], r_=vp_tok[:, tt, hh * 128:(hh + 1) * 128],
                       st=(tt == 0), sp=(tt == NT - 1): e.matmul(o, l, r_, start=st, stop=sp), r=[("kp", tt), ("vp", tt)], w=[("P", b)])
        for hh in range(2):
            h = hp * 2 + hh
            sc.add("dve", lambda e, o=Ainit[:, h, :], i=P[b][:, hh * 128:(hh + 1) * 128], g=cs("gini", 1, h):
                   e.tensor_scalar(o, i, g, None, op0=ALU.mult), r=[("P", b), "CS"], w=[("Ainit", h)])
    sc.barrier()
    if stop_after == "prefix":
        return finish(nc, es, sc, H, Hs, y_cur, y_smp, S, Sv, P, CS, transposes_to, load_transpose, rmsnorm_fm, debug=True)

    def dst_cur(i0, cnt, tt, pv, pkey):
        eng = evac_eng()
        o = H[:, i0:i0 + cnt, tt * 128:(tt + 1) * 128]
        i_ = pv.rearrange("p (c t) -> p c t", c=cnt)
        sc.add(eng, copy_op(eng, o, i_), r=[pkey], w=[("H", k, tt // 4) for k in range(i0, i0 + cnt)])

    load_transpose(x_cur, NT, 128, dst_cur, None)

    def dst_smp(i0, cnt, tt, pv, pkey):
        o = Hs[:, i0:i0 + cnt, :]
        i_ = pv.rearrange("p (c t) -> p c t", c=cnt)
        sc.add("dve", copy_op("dve", o, i_), r=[pkey], w=[("Hs", k) for k in range(i0, i0 + cnt)])

    load_transpose(x_smp, 1, TS, dst_smp, None)
    rmsnorm_fm(H, Hk, X, Xk, 0, 512, 2)
    rmsnorm_fm(Hs, Hsk, Xs, Xsk, 0, TS, 1)
    sc.barrier()
    if stop_after == "norm1":
        return finish(nc, es, sc, H, Hs, y_cur, y_smp, S, Sv, P, CS, transposes_to, load_transpose, rmsnorm_fm, debug=True)

    cc = Sv(0, 16).rearrange("p (j t) -> p j t", j=8)
    vrows = Sv(16, 32).rearrange("p (t f) -> p t f", t=8)
    g32 = Sv(36, 38, F32).rearrange("p (s f) -> p s f", s=2)
    ug = Sv(38, 42, F32).rearrange("p (s f) -> p s f", s=2)
    t32 = Sv(42, 44, F32)
    vs32 = Sv(32, 36, F32)
    bst = Mv(768, 192).rearrange("p (t s c) -> p t s c", t=8, s=4)
    bsts = Mv(1216, 24, npart=TS).rearrange("p (s c) -> p s c", s=4)
    ugs = Mv(960, 128).rearrange("p (h t) -> p h t", h=8)
    vgs = Mv(1088, 128).rearrange("p (h t) -> p h t", h=8)
    b_bc = Sv(48, 52, F32)
    sc.add("sp", lambda e: e.dma_start(out=b_bc, in_=bbc_d[:, :]), w=["b_bc"], dma="cst4")

    gtog = {"n": 0}

    def epi_vA(s, tt, ps, pkey):
        gtog["n"] += 1
        sl = gtog["n"] % 2
        sc.add("act", lambda e: e.activation(out=g32[:, sl, :], in_=ps, func=AF.Gelu_apprx_tanh), r=[pkey], w=[("g32", sl)])
        sc.add("dve", lambda e: e.bn_stats(bst[:, tt, s, :], g32[:, sl, :]), r=[("g32", sl)], w=[("bst", tt)])
        sc.add("dve", lambda e: e.tensor_copy(vrows[:, tt, s * 256:(s + 1) * 256], g32[:, sl, :]), r=[("g32", sl)], w=[("vrows", tt)])

    def epis_vA(s, ps, pkey):
        sc.add("act", lambda e: e.activation(out=vs32[0:TS, s * 256:(s + 1) * 256], in_=ps, func=AF.Gelu_apprx_tanh), r=[pkey], w=["vs32"])
        sc.add("dve", lambda e: e.bn_stats(bsts[:, s, :], vs32[0:TS, s * 256:(s + 1) * 256]), r=["vs32"], w=["bsts"])

    gemm_a(w_in, 0, 16, 1024, 1024, NT, xt_X, epi_vA, xs=xs_Xs, epis=epis_vA)

    def ln_finish(stats, np_, data_in, data_out, keys_r, keys_w):
        mv = smx[0:np_, 0:2]
        sc.add("dve", lambda e: e.bn_aggr(mv, stats), r=keys_r, w=["mvA"])
        sc.add("act", lambda e: e.activation(out=mv[:, 1:2], in_=mv[:, 1:2], func=AF.Sqrt, bias=EPS, scale=1.0), r=["mvA"], w=["mvA"])
        sc.add("dve", lambda e: e.reciprocal(mv[:, 1:2], mv[:, 1:2]), r=["mvA"], w=["mvA"])
        sc.add("dve", lambda e: e.tensor_scalar(data_out, data_in, mv[:, 0:1], mv[:, 1:2], op0=ALU.subtract, op1=ALU.mult),
               r=["mvA"] + keys_r, w=keys_w)

    for tt in range(NT):
        ln_finish(bst[:, tt, :, :].rearrange("p s c -> p (s c)"), 128, vrows[:, tt, :], vrows[:, tt, :], [("bst", tt), ("vrows", tt)], [("vrows", tt)])
    ln_finish(bsts.rearrange("p s c -> p (s c)"), TS, vs32[0:TS, :], vs32[0:TS, :], ["bsts", "vs32"], ["vs32"])
    def dst_vgs(i0, cnt, pv, pkey):
        for i in range(cnt):
            h = i0 + i
            sc.add("dve", lambda e, o=vgs[:, h, :], i_=pv[:, i * TS:(i + 1) * TS], g=cs("gA", 1, h): e.tensor_scalar(o, i_, g, None, op0=ALU.mult),
                   r=[pkey, "CS"], w=[("vgs", h)])
    transposes_to(dst_vgs, lambda i: vs32[0:TS, i * 128:(i + 1) * 128], 8, F32, lambda i: ["vs32", "CS"], None, np_in=TS, np_out=128)
    cvst = Sv(44, 48, F32)
    def dst_cvs(i0, cnt, pv, pkey):
        sc.add("act", lambda e: e.copy(cvst[0:TS, i0 * 128:(i0 + cnt) * 128], pv), r=[pkey], w=["cvst"])
    transposes_to(dst_cvs, lambda i: vgs[:, i, :], 8, F32, lambda i: [("vgs", i), "CS"], None, np_in=128, np_out=TS)
    sc.add("sp", lambda e: e.dma_start(out=cvs_o[:, :], in_=cvst[0:TS, :]), r=["cvst"], dma="out_cvs")

    utog = {"n": 0}

    def epi_u(f, th, ps, pkey):
        utog["n"] += 1
        sl = utog["n"] % 2
        sc.add("act", lambda e: e.activation(out=ug[:, sl, :], in_=ps, func=AF.Gelu_apprx_tanh), r=[pkey], w=[("ug", sl)])
        b = miscbank()
        for n in range(4):
            tt = th * 4 + n
            sc.add("pe", lambda e, o=P[b][:, n * 128:(n + 1) * 128], l=vrows[:, tt, f * 128:(f + 1) * 128], r_=wsTb[:, f, :]:
                   e.matmul(o, l, r_, start=True, stop=True), r=[("vrows", tt), "wsTb"], w=[("P", b)])
        sc.add("dve", lambda e: e.scalar_tensor_tensor(t32.rearrange("p (n i) -> p n i", n=4), P[b][:].rearrange("p (n i) -> p n i", n=4), cs("gA", 1, f),
                                                       vw(b_bc[:, f * 128:(f + 1) * 128], [[0, 4], [1, 128]]), op0=ALU.mult, op1=ALU.add),
               r=[("P", b), "CS", "b_bc"], w=["t32"])
        sc.add("dve", lambda e: e.tensor_tensor(cc[:, f, th * 512:(th + 1) * 512], t32, ug[:, sl, :], op=ALU.mult), r=["t32", ("ug", sl)], w=[("cc", f, th)])

    def epis_u(f, ps, pkey):
        sc.add("act", lambda e: e.activation(out=ugs[:, f, :], in_=ps, func=AF.Gelu_apprx_tanh), r=[pkey], w=[("ugs", f)])
        tmp = smx[:, 32:48]
        sc.add("dve", lambda e: e.tensor_scalar(tmp, vgs[:, f, :], cs("ws00", 1, f), cs("b0", 1, f), op0=ALU.mult, op1=ALU.add),
               r=[("vgs", f), "CS"], w=["tmps"])
        sc.add("dve", lambda e: e.tensor_tensor(ccs[:, f, :], tmp, ugs[:, f, :], op=ALU.mult), r=["tmps", ("ugs", f)], w=[("ccs", f)])

    gemm_b(w_in, 0, 16, 0, 1024, xk_X, epi_u, xs=xs_Xs, epis=epis_u)

    def xk_cc(k, th):
        return cc[:, k, th * 512:(th + 1) * 512], [("cc", k, th)]

    def xs_ccs(k):
        return ccs[:, k, :], [("ccs", k)]

    def epi_res(f, th, ps, pkey):
        eng = "dve"
        o = H[:, f, th * 512:(th + 1) * 512]
        sc.add(eng, lambda e: e.tensor_tensor(o, o, ps, op=ALU.add), r=[pkey, ("H", f, th)], w=[("H", f, th)])

    def epis_res(f, ps, pkey):
        o = Hs[:, f, :]
        sc.add("dve", lambda e: e.tensor_tensor(o, o, ps, op=ALU.add), r=[pkey, ("Hs", f)], w=[("Hs", f)])

    gemm_b(w_out, 0, 8, 0, 2048, xk_cc, epi_res, xs=xs_ccs, epis=epis_res)
    sc.barrier()
    if stop_after == "mixA":
        return finish(nc, es, sc, H, Hs, y_cur, y_smp, S, Sv, P, CS, transposes_to, load_transpose, rmsnorm_fm, debug=True)

    qT = Sv(16, 20).rearrange("p (h t) -> p h t", h=2)
    kT = Sv(20, 24).rearrange("p (h t) -> p h t", h=2)
    ktok = Sv(24, 28).rearrange("p (t f) -> p t f", t=8)
    vtok = Sv(28, 32).rearrange("p (t f) -> p t f", t=8)
    gT = Sv(32, 36).rearrange("p (h t) -> p h t", h=2)
    ks_s = Sv(36, 40, F32)
    vs_s = Sv(40, 44, F32)
    stsl = Sv(44, 56, F32).rearrange("p (s h e) -> p s h e", s=3, h=8)
    qtok_t = Mv(768, 256, BF16).rearrange("p (s f) -> p s f", s=2)
    qs_s = Sv(48, 52, F32)[0:TS, :]
    qTs = Mv(1920, 128).rearrange("p (h t) -> p h t", h=8)
    gTs = Mv(2048, 128).rearrange("p (h t) -> p h t", h=8)
    Aacc = Mv(1536, 256).rearrange("p (h e) -> p h e", h=2)
    Abf = Mv(1792, 128, BF16).rearrange("p (h e) -> p h e", h=2)
    scm = Mv(1024, 256, BF16).rearrange("p (s f) -> p s f", s=2)
    retn = Mv(1280, 256, BF16).rearrange("p (s f) -> p s f", s=2)
    rstat = Mv(2176, 12).rearrange("p (h c) -> p h c", h=2)
    rmv = Mv(2188, 4).rearrange("p (h c) -> p h c", h=2)
    qtt = {"n": 0}
    dfq = Defer(1)

    for hp in range(4):
        def epi_q(s, tt, ps, pkey, hp=hp, which="q"):
            tv, tk = ropeslot()
            qtt["n"] += 1
            sl = qtt["n"] % 2
            if which == "q":
                dstb, dkey, scl = qtok_t[:, sl, :], ("qtokt", sl), cs("qs", 2, tt * 8 + hp * 2)
            else:
                dstb, dkey, scl = ktok[:, tt, :], ("ktok", tt), cs("ks", 2, tt * 8 + hp * 2)
            rope_scale(dstb, ps, pkey, cs("cosc", 64, tt * 64), cs("sinc", 64, tt * 64), scl, 128, tv, tk, [dkey])
            dT = qT if which == "q" else kT
            nm = "qT" if which == "q" else "kT"

            def dst_t(i0, cnt, pv, pk2):
                sc.add("act", lambda e: e.copy(dT[:, :, tt * 128:(tt + 1) * 128], pv.rearrange("p (h t) -> p h t", h=2)), r=[pk2], w=[(nm, tt)])
            dfq.push(lambda: transposes_to(dst_t, lambda i: dstb[:, i * 128:(i + 1) * 128], 2, BF16, lambda i: [dkey, "identb"], None))

        def epis_q(s, ps, pkey, hp=hp, which="q"):
            tv, tk = ropeslot()
            dsts = (qs_s if which == "q" else ks_s)[0:TS, hp * 256:(hp + 1) * 256]
            rope_scale(dsts, ps, pkey, cs("coss", 64, 0, TS), cs("sins", 64, 0, TS), None, TS, tv, tk, [("qs_s" if which == "q" else "ks_s", hp)],
                       imm_scale=(1.0 if which == "q" else 128.0 ** -0.5))

        gemm_a(w_in, 0, 16, 2048 + hp * 256, 256, NT, xt_X, epi_q, xs=xs_Xs, epis=epis_q)
        dfq.flush()
        gemm_a(w_in, 0, 16, 3072 + hp * 256, 256, NT, xt_X, lambda s, tt, ps, pkey, hp=hp: epi_q(s, tt, ps, pkey, hp, "k"),
               xs=xs_Xs, epis=lambda s, ps, pkey, hp=hp: epis_q(s, ps, pkey, hp, "k"))
        dfq.flush()

        def epi_v(s, tt, ps, pkey):
            eng = evac_eng()
            sc.add(eng, copy_op(eng, vtok[:, tt, :], ps), r=[pkey], w=[("vtok", tt)])

        def epis_v(s, ps, pkey, hp=hp):
            sc.add("act", lambda e: e.copy(vs_s[0:TS, hp * 256:(hp + 1) * 256], ps), r=[pkey], w=[("vs_s", hp)])

        gemm_a(w_in, 0, 16, 4096 + hp * 256, 256, NT, xt_X, epi_v, xs=xs_Xs, epis=epis_v)

        def epi_g(f, th, ps, pkey):
            sc.add("act", lambda e: e.activation(out=gT[:, f, th * 512:(th + 1) * 512], in_=ps, func=AF.Silu), r=[pkey], w=[("gT", f, th)])

        def epis_g(f, ps, pkey, hp=hp):
            sc.add("act", lambda e: e.activation(out=gTs[:, hp * 2 + f, :], in_=ps, func=AF.Silu), r=[pkey], w=[("gTs", hp * 2 + f)])

        gemm_b(w_in, 0, 16, 5120 + hp * 256, 256, xk_X, epi_g, xs=xs_Xs, epis=epis_g)

        for hh in range(2):
            h = hp * 2 + hh
            sc.add("dve", lambda e, hh=hh, h=h: e.tensor_copy(Aacc[:, hh, :], Ainit[:, h, :]), r=[("Ainit", h)], w=["Aacc"])
        sc.add("act", lambda e: e.copy(Mv(1792, 128, BF16), Mv(1536, 256)), r=["Aacc"], w=["Abf"])
        deferred = [None]
        for n in range(NT):
            b1 = miscbank()
            for hh in range(2):
                sc.add("pe", lambda e, hh=hh: e.matmul(P[b1][:, hh * 128:(hh + 1) * 128], kT[:, hh, n * 128:(n + 1) * 128], qT[:, hh, n * 128:(n + 1) * 128],
                                                      start=True, stop=True), r=[("kT", n), ("qT", n)], w=[("P", b1)])
            sl = n % 2
            sc.add("dve", lambda e, sl=sl: e.tensor_tensor(scm[:, sl, :], P[b1][:, 0:256], cs("maskT", 256), op=ALU.mult), r=[("P", b1), "CS"], w=[("scm", sl)])
            b2 = mainbank()
            for hh in range(2):
                sc.add("pe", lambda e, hh=hh, sl=sl: e.matmul(P[b2][:, hh * 128:(hh + 1) * 128], scm[:, sl, hh * 128:(hh + 1) * 128], vtok[:, n, hh * 128:(hh + 1) * 128],
                                                             start=True, stop=False), r=[("scm", sl), ("vtok", n)], w=[("P", b2)])
                sc.add("pe", lambda e, hh=hh: e.matmul(P[b2][:, hh * 128:(hh + 1) * 128], qT[:, hh, n * 128:(n + 1) * 128], Abf[:, hh, :],
                                                      start=False, stop=True), r=[("qT", n), "Abf"], w=[("P", b2)])
            b3 = miscbank()
            for hh in range(2):
                sc.add("pe", lambda e, hh=hh: e.matmul(P[b3][:, hh * 128:(hh + 1) * 128], ktok[:, n, hh * 128:(hh + 1) * 128], vtok[:, n, hh * 128:(hh + 1) * 128],
                                                      start=True, stop=True), r=[("ktok", n), ("vtok", n)], w=[("P", b3)])
            sc.add("dve", lambda e: e.tensor_tensor(Mv(1536, 256), Mv(1536, 256), P[b3][:, 0:256], op=ALU.add),
                   r=[("P", b3), "Aacc"], w=["Aacc"])
            sc.add("act", lambda e: e.copy(Mv(1792, 128, BF16), Mv(1536, 256)), r=["Aacc"], w=["Abf"])
            for hh in range(2):
                sc.add("dve", lambda e, hh=hh: e.bn_stats(rstat[:, hh, :], P[b2][:, hh * 128:(hh + 1) * 128]), r=[("P", b2)], w=[("rstat", hh)])
                sc.add("dve", lambda e, hh=hh: e.bn_aggr(rmv[:, hh, :], rstat[:, hh, :]), r=[("rstat", hh)], w=[("rmv", hh)])
            sc.add("act", lambda e: e.activation(out=rmv[:, :, 1:2], in_=rmv[:, :, 1:2], func=AF.Sqrt, bias=EPS, scale=1.0), r=[("rmv", 0), ("rmv", 1)], w=[("rmv", 0), ("rmv", 1)])
            sc.add("dve", lambda e: e.reciprocal(rmv[:, :, 1:2], rmv[:, :, 1:2]), r=[("rmv", 0), ("rmv", 1)], w=[("rmv", 0), ("rmv", 1)])
            sc.add("dve", lambda e: e.scalar_tensor_tensor(rmv[:, :, 0:1], rmv[:, :, 0:1], -1.0, rmv[:, :, 1:2], op0=ALU.mult, op1=ALU.mult),
                   r=[("rmv", 0), ("rmv", 1)], w=[("rmv", 0), ("rmv", 1)])
            for hh in range(2):
                sc.add("act", lambda e, hh=hh, sl=sl: e.activation(out=retn[:, sl, hh * 128:(hh + 1) * 128], in_=P[b2][:, hh * 128:(hh + 1) * 128], func=AF.Identity,
                                                                  bias=rmv[:, hh, 0:1], scale=rmv[:, hh, 1:2]), r=[("P", b2), ("rmv", hh)], w=[("retn", sl)])

            def dst_r(i0, cnt, pv, pk2, n=n, hp=hp):
                for hh in range(2):
                    h = hp * 2 + hh
                    sc.add("dve", lambda e, hh=hh, h=h: e.scalar_tensor_tensor(cc[:, h, n * 128:(n + 1) * 128], pv[:, hh * 128:(hh + 1) * 128], cs("gnB", 1, h),
                                                                               gT[:, hh, n * 128:(n + 1) * 128], op0=ALU.mult, op1=ALU.mult),
                           r=[pk2, "CS", ("gT", hh, n // 4)], w=[("cc", h, n // 4)])
            if deferred[0] is not None:
                deferred[0]()
            deferred[0] = (lambda dst_r=dst_r, sl=sl: transposes_to(dst_r, lambda i: retn[:, sl, i * 128:(i + 1) * 128], 2, BF16, lambda i: [("retn", sl), "identb"], None))
        deferred[0]()
        stst = Sv(44, 45, F32).rearrange("p (h e) -> p h e", h=2)
        for hh in range(2):
            h = hp * 2 + hh
            sc.add("dve", lambda e, hh=hh, h=h: e.tensor_scalar(stst[:, hh, :], Aacc[:, hh, :], cs("gfin", 1, h), None, op0=ALU.mult), r=["Aacc", "CS"], w=[("stst", hh)])
        sc.add("sp", lambda e, hp=hp: e.dma_start(out=stp_o[hp * 2:hp * 2 + 2].rearrange("h d e -> d h e"), in_=stst), r=[("stst", 0), ("stst", 1)], dma="out_stp")

    sc.barrier()
    def dst_qTs(i0, cnt, pv, pkey):
        sc.add("dve", lambda e: e.tensor_copy(qTs[:, i0:i0 + cnt, :], pv.rearrange("p (c t) -> p c t", c=cnt)), r=[pkey], w=[("qTs", i0 // 4)])
    transposes_to(dst_qTs, lambda i: qs_s[0:TS, i * 128:(i + 1) * 128], 8, F32, lambda i: [("qs_s", i // 2), "CS"], None, np_in=TS, np_out=128)
    Qsel = Sv(16, 24, F32).rearrange("p (h s c) -> p h s c", h=8, s=TS)
    sc.add("dve", lambda e: e.tensor_tensor(Qsel, vw(qTs[:, 0, 0:1], [[TS, 8], [1, TS], [0, TS]]),
                                            vw(cs("eyeq", 256), [[0, 8], [TS, TS], [1, TS]]), op=ALU.mult),
           r=[("qTs", 0), ("qTs", 1), "CS"], w=["Qsel"])
    qk = Mv(2192, 8, npart=TS)
    qkj = Sv(24, 28, F32)
    sc.add("dve", lambda e: e.tensor_tensor(qkj[0:TS, :], qs_s, ks_s[0:TS, :], op=ALU.mult), r=[("qs_s", i) for i in range(4)] + [("ks_s", i) for i in range(4)], w=["qkj"])
    sc.add("dve", lambda e: e.tensor_reduce(qk, qkj[0:TS, :].rearrange("p (h d) -> p h d", h=8), axis=AX.X, op=ALU.add), r=["qkj"], w=["qk"])
    rs_tok = Sv(28, 32, F32)
    sc.barrier()
    pcross = 5; pc2 = 6
    cracc = Sv(24, 28, F32)[0:TS, :]
    sc.add("dve", lambda e: e.memset(cracc, 0.0), w=["cracc"])
    Ksel = Sv(32, 36, F32)[0:TS, :]
    def state_iter(s_):
        sl = s_ % 3
        sc.add("sp", lambda e, s_=s_, sl=sl: e.dma_start(out=stsl[:, sl], in_=st_in[s_].rearrange("h d e -> d h e")), w=[("stsl", sl)], dma=("stin", sl))
        sc.add("dve", lambda e, s_=s_, sl=sl: e.tensor_scalar(Ksel, ks_s[0:TS, :], cs("eye16", 1, s_, TS), None, op0=ALU.mult),
               r=[("ks_s", i) for i in range(4)] + ["CS"], w=["Ksel"])
        for h in range(8):
            pb = pcross if h < 4 else pc2
            sc.add("pe", lambda e, s_=s_, sl=sl, h=h, pb=pb: e.matmul(P[pb][0:TS, (h % 4) * 128:(h % 4 + 1) * 128], Qsel[:, h, s_, :], stsl[:, sl, h, :],
                                                                     start=True, stop=True), r=["Qsel", ("stsl", sl)], w=[("P", pb)])
        for half in range(2):
            pb = (pcross, pc2)[half]
            sc.add("dve", lambda e, half=half, pb=pb: e.tensor_tensor(cracc[:, half * 512:(half + 1) * 512], cracc[:, half * 512:(half + 1) * 512], P[pb][0:TS, :], op=ALU.add),
                   r=[("P", pb), "cracc"], w=["cracc"])
        bo1 = 4; bo2 = 7
        for h in range(8):
            bo = bo1 if h < 4 else bo2
            sc.add("pe", lambda e, sl=sl, h=h, bo=bo: e.matmul(P[bo][:, (h % 4) * 128:(h % 4 + 1) * 128], Ksel[:, h * 128:(h + 1) * 128], vs_s[0:TS, h * 128:(h + 1) * 128],
                                                              start=True, stop=True), r=["Ksel"] + [("vs_s", i) for i in range(4)], w=[("P", bo)])
        for h in range(8):
            bo = bo1 if h < 4 else bo2
            sc.add("dve", lambda e, sl=sl, h=h, bo=bo: e.scalar_tensor_tensor(stsl[:, sl, h, :], stsl[:, sl, h, :], GAM[h], P[bo][:, (h % 4) * 128:(h % 4 + 1) * 128],
                                                                             op0=ALU.mult, op1=ALU.add), r=[("P", bo), ("stsl", sl)], w=[("stsl", sl)])
        sc.add("sp", lambda e, s_=s_, sl=sl: e.dma_start(out=sts_o[s_].rearrange("h d e -> d h e"), in_=stsl[:, sl]), r=[("stsl", sl)], dma=("stout", sl))

    pend2 = [(lambda s_=s_: state_iter(s_)) for s_ in range(TS)]
    tk2 = {"n": 0}

    def epi_res_t(f, th, ps, pkey):
        epi_res(f, th, ps, pkey)
        tk2["n"] += 1
        if tk2["n"] % 2 == 0 and pend2:
            pend2.pop(0)()

    pend2.pop(0)()
    pend2.pop(0)()
    gemm_b(w_out, 1024, 8, 0, 2048, xk_cc, epi_res_t)
    while pend2:
        pend2.pop(0)()
    sc.add("dve", lambda e: e.tensor_tensor(rs_tok[0:TS, :].rearrange("p (h e) -> p h e", h=8), vs_s[0:TS, :].rearrange("p (h e) -> p h e", h=8),
                                            vw(qk[:, 0:1], [[1, 8], [0, 128]]), op=ALU.mult), r=["qk"] + [("vs_s", i) for i in range(4)], w=["rs_tok"])
    for h in range(8):
        sc.add("dve", lambda e, h=h: e.scalar_tensor_tensor(rs_tok[0:TS, h * 128:(h + 1) * 128], cracc[:, h * 128:(h + 1) * 128], GAM[h],
                                                            rs_tok[0:TS, h * 128:(h + 1) * 128], op0=ALU.mult, op1=ALU.add), r=["cracc", "rs_tok"], w=["rs_tok"])
    rst_s = Mv(2200, 48, npart=TS).rearrange("p (h c) -> p h c", h=8)
    rmv_s = Mv(2248, 16, npart=TS).rearrange("p (h c) -> p h c", h=8)
    for h in range(8):
        sc.add("dve", lambda e, h=h: e.bn_stats(rst_s[:, h, :], rs_tok[0:TS, h * 128:(h + 1) * 128]), r=["rs_tok"], w=["rst_s"])
        sc.add("dve", lambda e, h=h: e.bn_aggr(rmv_s[:, h, :], rst_s[:, h, :]), r=["rst_s"], w=["rmv_s"])
    sc.add("act", lambda e: e.activation(out=rmv_s[:, :, 1:2], in_=rmv_s[:, :, 1:2], func=AF.Sqrt, bias=EPS, scale=1.0), r=["rmv_s"], w=["rmv_s"])
    sc.add("dve", lambda e: e.reciprocal(rmv_s[:, :, 1:2], rmv_s[:, :, 1:2]), r=["rmv_s"], w=["rmv_s"])
    for h in range(8):
        sc.add("dve", lambda e, h=h: e.tensor_scalar(rs_tok[0:TS, h * 128:(h + 1) * 128], rs_tok[0:TS, h * 128:(h + 1) * 128], rmv_s[:, h, 0:1], rmv_s[:, h, 1:2],
                                                     op0=ALU.subtract, op1=ALU.mult), r=["rmv_s", "rs_tok"], w=["rs_tok"])

    def dst_rs(i0, cnt, pv, pkey):
        for i in range(cnt):
            h = i0 + i
            sc.add("dve", lambda e, h=h, i=i: e.scalar_tensor_tensor(ccs[:, h, :], pv[:, i * TS:(i + 1) * TS], cs("gnB", 1, h), gTs[:, h, :], op0=ALU.mult, op1=ALU.mult),
                   r=[pkey, "CS", ("gTs", h)], w=[("ccs", h)])
    transposes_to(dst_rs, lambda i: rs_tok[0:TS, i * 128:(i + 1) * 128], 8, F32, lambda i: ["rs_tok", "CS"], None, np_in=TS, np_out=128)

    gemm_s(w_out, 1024, 8, 0, 2048, xs_ccs, epis_res)
    sc.barrier()
    if stop_after == "mixB":
        return finish(nc, es, sc, H, Hs, y_cur, y_smp, S, Sv, P, CS, transposes_to, load_transpose, rmsnorm_fm, debug=True)

    rmsnorm_fm(H, Hk, X, Xk, 1, 512, 2)
    rmsnorm_fm(Hs, Hsk, Xs, Xsk, 1, TS, 1)
    Q = Sv(0, 32).rearrange("p (k t) -> p k t", k=16)
    qtok_s = Sv(48, 56, F32)[0:TS, :]

    def epi_cq(f, th, ps, pkey):
        eng = evac_eng()
        sc.add(eng, copy_op(eng, Q[:, f, th * 512:(th + 1) * 512], ps), r=[pkey], w=[("Q", f, th)])

    def epis_cq(s, ps, pkey):
        sc.add("act", lambda e: e.copy(qtok_s[:, s * 256:(s + 1) * 256], ps), r=[pkey], w=[("qtok_s", s // 2)])

    gemm_b(w_cq, 0, 16, 0, 2048, xk_X, epi_cq, xs=xs_Xs, epis=epis_cq, sform="a")
    sc.barrier()
    if stop_after == "xq":
        return finish(nc, es, sc, H, Hs, y_cur, y_smp, S, Sv, P, CS, transposes_to, load_transpose, rmsnorm_fm, debug=True)
    Xf = X[:].rearrange("p k t -> p (k t)")
    mH = Xf[:, 0:8192].bitcast(F32).rearrange("p (k t) -> p k t", k=16)
    mnT = Xf[:, 8192:12288].rearrange("p (k t) -> p k t", k=16)
    mkT = Xf[:, 0:4096].rearrange("p (k t) -> p k t", k=16)
    mvb = Xf[:, 4096:8192].rearrange("p (m f) -> p m f", m=2)
    pT = Xf[:, 12288:14336].rearrange("p (m t) -> p m t", m=2)
    mkst = Xf[:, 14336:16384].bitcast(F32).rearrange("p (s f) -> p s f", s=4)
    pbuf = Mv(0, 512).rearrange("p (s f) -> p s f", s=2)
    pbb = Mv(512, 256, BF16).rearrange("p (s f) -> p s f", s=2)

    def dst_mem(i0, cnt, tt, pv, pkey):
        eng = evac_eng()
        sc.add(eng, copy_op(eng, mH[:, i0:i0 + cnt, tt * 128:(tt + 1) * 128], pv.rearrange("p (c t) -> p c t", c=cnt)), r=[pkey], w=[("mH", k) for k in range(i0, i0 + cnt)])
    load_transpose(mem, 2, 128, dst_mem, None, stage_kb=(32, 48))
    rmsnorm_fm(mH, lambda k, th: ("mH", k), mnT, lambda k, th: ("mnT", k), 2, 256, 1)
    sc.barrier()
    mtog = {"n": 0}
    dfm = Defer(1)

    def xt_mn(k, tt):
        return mnT[:, k, tt * 128:(tt + 1) * 128], [("mnT", k)]

    def epi_mk(s, tt, ps, pkey, which="k"):
        tick()
        mtog["n"] += 1
        sl = mtog["n"] % 4
        sc.add("act", lambda e: e.copy(mkst[:, sl, :], ps), r=[pkey], w=[("mkst", sl)])
        dsto = (mk_o if which == "k" else mv_o)[tt * 128:(tt + 1) * 128, s * 256:(s + 1) * 256]
        sc.add("sp", lambda e: e.dma_start(out=dsto, in_=mkst[:, sl, :]), r=[("mkst", sl)], dma=("mko", sl))
        if which == "v":
            sc.add("dve", lambda e: e.tensor_copy(mvb[:, tt, s * 256:(s + 1) * 256], mkst[:, sl, :]), r=[("mkst", sl)], w=[("mvb", tt, s)])
        else:
            sc.add("dve", lambda e: e.tensor_copy(pbb[:, tt, :], mkst[:, sl, :]), r=[("mkst", sl)], w=[("pbb", tt)])

            def dst_k(i0, cnt, pv, pk2):
                sc.add("act", lambda e: e.copy(mkT[:, s * 2:s * 2 + 2, tt * 128:(tt + 1) * 128], pv.rearrange("p (c t) -> p c t", c=2)), r=[pk2], w=[("mkT", s, tt)])
            dfm.push(lambda: transposes_to(dst_k, lambda i: pbb[:, tt, i * 128:(i + 1) * 128], 2, BF16, lambda i: [("pbb", tt), "identb"], None))

    from collections import deque
    pend = deque()
    tkc = {"n": 0}

    def tick(every=3):
        tkc["n"] += 1
        if tkc["n"] % every == 0 and pend:
            pend.popleft()()

    SCL = 512.0 ** -0.5
    KV = Sv(32, 48).rearrange("p (s f) -> p s f", s=4)
    KVf = Sv(32, 48, F32).rearrange("p (s f) -> p s f", s=2)
    scs = Mv(2048, 128).rearrange("p (m c) -> p m c", m=2)
    junk5 = Mv(768, 512)
    qm = Mv(1280, 512, npart=TS)
    smT = Mv(1792, 256, npart=64)
    amx_s = Mv(2376, 4, npart=64)
    pTs = Mv(2176, 64, BF16).rearrange("p (m c) -> p m c", m=2)
    oTs = Mv(2240, 128, BF16).rearrange("p (c s) -> p c s", c=16)
    sc.add("dve", lambda e: e.memset(scs, 0.0), w=["scs"])
    qb = {"n": 0}


    def kload(s_):
        for mh in range(2):
            sc.add("sp", lambda e: e.dma_start(out=KVf[:, mh, :], in_=ck[s_, mh * 128:(mh + 1) * 128, :]), w=[("KV", 2 * mh), ("KV", 2 * mh + 1)], dma=("KVf", mh))

    def kpass(s_):
        if s_ == 0:
            kload(0)
        sel = vw(cs("eye16", 1, s_, TS), [[0, 128]])
        for h in range(4):
            qb["n"] += 1
            b_ = (4, 7)[qb["n"] % 2]
            sc.add("pe", lambda e: e.matmul(P[b_][:], sel, qtok_s[:, h * 512:(h + 1) * 512], start=True, stop=True),
                   r=[("qtok_s", i) for i in range(4)] + ["CS"], w=[("P", b_)])
            for mh in range(2):
                sc.add("dve", lambda e: e.scalar_tensor_tensor(junk5, KVf[:, mh, h * 512:(h + 1) * 512], 1.0, P[b_][:], op0=ALU.mult, op1=ALU.mult,
                                                               accum_out=scs[:, mh, s_ * 4 + h:s_ * 4 + h + 1]),
                       r=[("KV", 2 * mh), ("KV", 2 * mh + 1), ("P", b_), "scs"], w=["junk5", "scs"])
        if s_ + 1 < TS:
            kload(s_ + 1)

    def ssoftmax():
        b_ = miscbank()
        for mh in range(2):
            sc.add("pe", lambda e: e.transpose(P[b_][0:64, mh * 128:(mh + 1) * 128], scs[:, mh, :], ident), r=["scs", "CS"], w=[("P", b_)])
        sc.add("dve", lambda e: e.tensor_reduce(amx_s[:, 0:1], P[b_][0:64, 0:256], axis=AX.X, op=ALU.max), r=[("P", b_)], w=["amxs"])
        sc.add("dve", lambda e: e.tensor_scalar(amx_s[:, 1:2], amx_s[:, 0:1], -SCL, None, op0=ALU.mult), r=["amxs"], w=["amxs"])
        sc.add("act", lambda e: e.activation(out=smT, in_=P[b_][0:64, 0:256], func=AF.Exp, bias=amx_s[:, 1:2], scale=SCL, accum_out=amx_s[:, 2:3]),
               r=[("P", b_), "amxs"], w=["smT", "amxs"])
        sc.add("dve", lambda e: e.reciprocal(amx_s[:, 3:4], amx_s[:, 2:3]), r=["amxs"], w=["amxs"])
        sc.add("dve", lambda e: e.tensor_scalar(smT, smT, amx_s[:, 3:4], None, op0=ALU.mult), r=["smT", "amxs"], w=["smT"])
        b2_ = miscbank()
        for mh in range(2):
            sc.add("pe", lambda e: e.transpose(P[b2_][:, mh * 64:(mh + 1) * 64], smT[:, mh * 128:(mh + 1) * 128], CS[0:64, CO["ident"]:CO["ident"] + 64]),
                   r=["smT", "CS"], w=[("P", b2_)])
        sc.add("dve", lambda e: e.tensor_copy(Mv(2176, 64, BF16), P[b2_][:, 0:128]), r=[("P", b2_)], w=["pTs"])

    def vpass(s_):
        for mh in range(2):
            sl = (2 * s_ + mh) % 4
            sc.add("pool", lambda e: e.dma_start(out=KV[:, sl, :], in_=cv[s_, mh * 128:(mh + 1) * 128, :]), w=[("KV", sl)], dma=("KV", sl))
        for c in range(16):
            hh = c // 4
            for mh in range(2):
                sl = (2 * s_ + mh) % 4
                sc.add("pe", lambda e: e.matmul(P[4][:, c * TS + s_:c * TS + s_ + 1], KV[:, sl, c * 128:(c + 1) * 128], pTs[:, mh, s_ * 4 + hh:s_ * 4 + hh + 1],
                                                start=(mh == 0), stop=(mh == 1)), r=[("KV", sl), "pTs"], w=[("P", 4)])

    for s_ in range(TS):
        pend.append(lambda s_=s_: kpass(s_))
    pend.append(ssoftmax)
    for s_ in range(TS):
        pend.append(lambda s_=s_: vpass(s_))

    gemm_a(w_ck, 0, 16, 0, 2048, 2, xt_mn, epi_mk)
    dfm.flush()
    gemm_a(w_cv, 0, 16, 0, 2048, 2, xt_mn, lambda s, tt, ps, pkey: epi_mk(s, tt, ps, pkey, "v"))

    amx = Mv(2368, 4)
    dfp = Defer(1)
    for h in range(4):
        for tt in range(NT):
            b = mainbank()
            for dc in range(4):
                kq = 4 * h + dc
                sc.add("pe", lambda e: e.matmul(P[b][:, 0:256], Q[:, kq, tt * 128:(tt + 1) * 128], mkT[:, kq, :], start=(dc == 0), stop=(dc == 3)),
                       r=[("Q", kq, tt // 4)] + [("mkT", kq // 2, m_) for m_ in range(2)], w=[("P", b)])
            sl = tt % 2
            sc.add("dve", lambda e: e.tensor_reduce(amx[:, 0:1], P[b][:, 0:256], axis=AX.X, op=ALU.max), r=[("P", b)], w=["amx0"])
            sc.add("dve", lambda e: e.tensor_scalar(amx[:, 1:2], amx[:, 0:1], -SCL, None, op0=ALU.mult), r=["amx0"], w=["amx1"])
            sc.add("act", lambda e: e.activation(out=pbuf[:, sl, :], in_=P[b][:, 0:256], func=AF.Exp, bias=amx[:, 1:2], scale=SCL, accum_out=amx[:, 2:3]),
                   r=[("P", b), "amx1"], w=[("pbuf", sl), "amx2"])
            sc.add("dve", lambda e: e.reciprocal(amx[:, 3:4], amx[:, 2:3]), r=["amx2"], w=["amx3"])
            sc.add("dve", lambda e: e.tensor_scalar(pbb[:, sl, :], pbuf[:, sl, :], amx[:, 3:4], None, op0=ALU.mult), r=[("pbuf", sl), "amx3"], w=[("pbb", sl)])

            def dst_p(i0, cnt, pv, pk2, tt=tt):
                sc.add("act", lambda e: e.copy(pT[:, :, tt * 128:(tt + 1) * 128], pv.rearrange("p (m t) -> p m t", m=2)), r=[pk2], w=[("pT", tt // 4)])
            dfp.push(lambda dst_p=dst_p, sl=sl: transposes_to(dst_p, lambda i: pbb[:, sl, i * 128:(i + 1) * 128], 2, BF16, lambda i: [("pbb", sl), "identb"], None))
            tick()
        dfp.flush()
        for dc in range(4):
            kq = 4 * h + dc
            for th in range(2):
                b = mainbank()
                for mh in range(2):
                    sc.add("pe", lambda e: e.matmul(P[b][:], mvb[:, mh, kq * 128:(kq + 1) * 128], pT[:, mh, th * 512:(th + 1) * 512], start=(mh == 0), stop=(mh == 1)),
                           r=[("mvb", mh, kq // 2), ("pT", th)], w=[("P", b)])
                eng = evac_eng()
                sc.add(eng, copy_op(eng, Q[:, kq, th * 512:(th + 1) * 512], P[b][:]), r=[("P", b)], w=[("Q", kq, th)])
                tick()
    while pend:
        pend.popleft()()
    sc.add("dve", lambda e: e.tensor_copy(Mv(2240, 128, BF16), P[4][:, 0:256]), r=[("P", 4)], w=["oTs"])

    def xk_Q(k, th):
        return Q[:, k, th * 512:(th + 1) * 512], [("Q", k, th)]

    def xs_oTs(k):
        return oTs[:, k, :], ["oTs"]

    gemm_b(w_co, 0, 16, 0, 2048, xk_Q, epi_res, xs=xs_oTs, epis=epis_res)
    sc.barrier()
    if stop_after == "xattn":
        return finish(nc, es, sc, H, Hs, y_cur, y_smp, S, Sv, P, CS, transposes_to, load_transpose, rmsnorm_fm, debug=True)

    rmsnorm_fm(H, Hk, X, Xk, 3, 512, 2)
    rmsnorm_fm(Hs, Hsk, Xs, Xsk, 3, TS, 1)
    sc.barrier()
    hid = Sv(0, 32).rearrange("p (k t) -> p k t", k=16)
    rl = Sv(32, 36, F32).rearrange("p (s f) -> p s f", s=2)
    hids = Mv(0, 128, BF16).rearrange("p (c s) -> p c s", c=16)
    rls = Mv(128, 16)
    ftog = {"n": 0}
    for g in range(4):
        def epi_f1(f, th, ps, pkey):
            ftog["n"] += 1
            sl = ftog["n"] % 2
            sc.add("act", lambda e: e.activation(out=rl[:, sl, :], in_=ps, func=AF.Relu), r=[pkey], w=[("rl", sl)])
            sc.add("pool", lambda e: e.tensor_tensor(hid[:, f, th * 512:(th + 1) * 512], rl[:, sl, :], rl[:, sl, :], op=ALU.mult), r=[("rl", sl)], w=[("hid", f, th)])

        def epis_f1(f, ps, pkey):
            sc.add("act", lambda e: e.activation(out=rls, in_=ps, func=AF.Relu), r=[pkey], w=["rls"])
            sc.add("dve", lambda e: e.tensor_tensor(hids[:, f, :], rls, rls, op=ALU.mult), r=["rls"], w=[("hids", f)])

        gemm_b(w_ff1, 0, 16, g * 2048, 2048, xk_X, epi_f1, xs=xs_Xs, epis=epis_f1)

        def xk_hid(k, th):
            return hid[:, k, th * 512:(th + 1) * 512], [("hid", k, th)]

        def xs_hids(k):
            return hids[:, k, :], [("hids", k)]

        gemm_b(w_ff2, g * 2048, 16, 0, 2048, xk_hid, epi_res, xs=xs_hids, epis=epis_res)
    sc.barrier()
    return finish(nc, es, sc, H, Hs, y_cur, y_smp, S, Sv, P, CS, transposes_to, load_transpose, rmsnorm_fm, debug=False)


def finish(nc, es, sc, H, Hs, y_cur, y_smp, S, Sv, P, CS, transposes_to, load_transpose, rmsnorm_fm, debug):
    X_dummy = None
    if not debug:
        sq = Sv(0, 32).rearrange("p (k t) -> p k t", k=16)
        rmsnorm_fm_out(sc, H, sq, 512, 2, 4, "H", Sv, P, CS)
        sqs = Sv(32, 33).rearrange("p (k t) -> p k t", k=16)
        rmsnorm_fm_out(sc, Hs, sqs, TS, 1, 4, "Hs", Sv, P, CS)
    yst = Sv(36, 52, F32).rearrange("p (s f) -> p s f", s=2)
    cnt = {"n": 0}
    for tt in range(NT + 1):
        smp = (tt == NT)
        sl = tt % 2
        npo = TS if smp else 128

        def dst_y(i0, c, pv, pkey, sl=sl, npo=npo):
            cnt["n"] += 1
            eng = "act" if cnt["n"] % 2 else "dve"
            o = yst[0:npo, sl, i0 * 128:(i0 + c) * 128]
            if eng == "act":
                sc.add("act", lambda e: e.copy(o, pv), r=[pkey], w=[("yst", sl)])
            else:
                sc.add("dve", lambda e: e.tensor_copy(o, pv), r=[pkey], w=[("yst", sl)])
        if smp:
            src_of = lambda i: Hs[:, i, :]
            keys = lambda i: [("Hs", i), "CS"]
        else:
            src_of = lambda i, tt=tt: H[:, i, tt * 128:(tt + 1) * 128]
            keys = lambda i, tt=tt: [("H", i, tt // 4), "CS"]
        transposes_to(dst_y, src_of, 16, F32, keys, None, np_in=128, np_out=npo)
        dsto = y_smp[:, :] if smp else y_cur[tt * 128:(tt + 1) * 128, :]
        sc.add("sp", lambda e, dsto=dsto, sl=sl, npo=npo: e.dma_start(out=dsto, in_=yst[0:npo, sl, :]), r=[("yst", sl)], dma=("yout", sl))
    sc.emit(nc, es)
    return nc, es


def rmsnorm_fm_out(sc, Hbuf, sq, ncols, nhalf, gi, hname, Sv, P, CS):
    rst = Sv(52, 56, F32).rearrange("p (h t) -> p h t", h=2)
    for th in range(nhalf):
        cols = slice(th * ncols, (th + 1) * ncols)
        hkey = (lambda k: (hname, k, th)) if hname == "H" else (lambda k: (hname, k))
        b = 5 + th
        for k in range(16):
            sc.add("act", lambda e, o=sq[:, k, cols], i=Hbuf[:, k, cols]: e.activation(out=o, in_=i, func=AF.Square), r=[hkey(k)], w=[("sq", hname, k, th)])
            sc.add("pe", lambda e, o=P[b][:, 0:ncols], r_=sq[:, k, cols], st=(k == 0), sp=(k == 15): e.matmul(o, ONESB[0][:], r_, start=st, stop=sp),
                   r=[("sq", hname, k, th), "onesb"], w=[("P", b)])
        rv = rst[:, th, 0:ncols]
        sc.add("act", lambda e, o=rv, i=P[b][:, 0:ncols]: e.activation(out=o, in_=i, func=AF.Sqrt, bias=EPS, scale=1.0 / D), r=[("P", b)], w=[("rstf", th)])
        sc.add("dve", lambda e, o=rv: e.reciprocal(o, o), r=[("rstf", th)], w=[("rstf", th)])
        for k in range(16):
            g = CS[:, CO["gpk"] + gi * 16 + k:CO["gpk"] + gi * 16 + k + 1]
            sc.add("dve", lambda e, o=Hbuf[:, k, cols], g=g, rv=rv: e.scalar_tensor_tensor(o, o, g, rv, op0=ALU.mult, op1=ALU.mult),
                   r=[hkey(k), ("rstf", th), "CS"], w=[hkey(k)])


ONESB = [None]


def _consts(hf):
    c = np.zeros((128, NCST), np.float32)

    def put(name, arr):
        arr = np.asarray(arr, np.float32)
        c[:arr.shape[0], CO[name]:CO[name] + arr.shape[1]] = arr

    put("ident", np.eye(128))
    j = np.arange(128)
    m = (j[:, None] <= j[None, :]).astype(np.float32)
    put("maskT", np.concatenate([m, m], axis=1))
    half = 64
    freqs = (10000.0 ** (-np.arange(half, dtype=np.float32) / half)).astype(np.float32)

    def rope(pos):
        ang = pos.astype(np.float32)[:, None] * freqs[None, :]
        return np.cos(ang).astype(np.float32), np.sin(ang).astype(np.float32)

    pos_c = (hf * T + np.arange(T)).astype(np.float32)
    cc_, ss_ = rope(pos_c)
    put("cosc", cc_.reshape(8, 128, 64).transpose(1, 0, 2).reshape(128, 512))
    put("sinc", ss_.reshape(8, 128, 64).transpose(1, 0, 2).reshape(128, 512))
    cp, sp_ = rope(np.arange(T).astype(np.float32))
    ropep = np.concatenate([cp.reshape(8, 128, 64).transpose(1, 0, 2).reshape(128, 512),
                            sp_.reshape(8, 128, 64).transpose(1, 0, 2).reshape(128, 512)], axis=1).astype(np.float32)
    c16, s16 = rope(np.full((128,), 16384.0, np.float32))
    put("coss", c16)
    put("sins", s16)
    g = np.array(GAM, np.float64)
    t = np.arange(T, dtype=np.float64)
    qs = np.exp(np.log(g)[None, :] * t[:, None])
    ks = np.exp(-np.log(g)[None, :] * t[:, None]) * (128.0 ** -0.5)
    put("qs", qs.reshape(8, 128, 8).transpose(1, 0, 2).reshape(128, 64))
    put("ks", ks.reshape(8, 128, 8).transpose(1, 0, 2).reshape(128, 64))
    put("gfin", np.tile((g ** 1023)[None, :], (128, 1)))
    put("gini", np.tile((g ** 1024)[None, :], (128, 1)))
    put("eye16", np.eye(16))
    put("eyeq", np.tile(np.eye(16).reshape(1, 256), (128, 1)))
    return c, ropep


_CACHE = {}


def kernel(x_prompt, x_sample, mem_prompt, cache_mem_k, cache_mem_v, state_ret,
           norm1_g, w_in, sgu_norm_g, sgu_w_s, sgu_b, ret_gn_g, w_out, norm2_g,
           mem_norm_g, w_cq, w_ck, w_cv, w_co, norm3_g, w_ff1, w_ff2, final_norm_g):
    f = lambda a: np.ascontiguousarray(np.asarray(a, dtype=np.float32))
    x_prompt, x_sample, mem_prompt = f(x_prompt), f(x_sample), f(mem_prompt)
    cache_mem_k, cache_mem_v, state_ret = f(cache_mem_k), f(cache_mem_v), f(state_ret)
    if "nc" not in _CACHE:
        _CACHE["nc"] = build()
    nc, _es = _CACHE["nc"]
    gains = [f(norm1_g)[0], f(norm2_g)[0], f(mem_norm_g)[0], f(norm3_g)[0], f(final_norm_g)]
    gpk = np.stack([gn.reshape(16, 128).T for gn in gains], axis=1).reshape(128, 80)
    gA = f(sgu_norm_g)[0].reshape(8, 128).T
    gnB = f(ret_gn_g)[0].reshape(8, 128).T
    ws = f(sgu_w_s)[0]
    sbias = f(sgu_b)[0]
    ws00 = np.tile(ws[:, 0, 0][None, :], (128, 1))
    b0 = np.tile(sbias[:, 0][None, :], (128, 1))
    wsT = np.ascontiguousarray(ws.transpose(2, 0, 1).reshape(128, 1024))
    b_bc = np.ascontiguousarray(np.tile(sbias.reshape(1, 1024), (128, 1)))
    zeros_prev = np.zeros((T, D), np.float32)
    shared = dict(w_in=f(w_in)[0], w_out=f(w_out)[0], w_cq=f(w_cq)[0], w_ck=f(w_ck)[0], w_cv=f(w_cv)[0], w_co=f(w_co)[0],
                  w_ff1=f(w_ff1)[0], w_ff2=f(w_ff2)[0], wsT=wsT, b_bc=b_bc)
    in_maps = []
    for c in range(8):
        b, hf = c // 2, c % 2
        cst, ropep = _consts(hf)
        for name, arr in (("gpk", gpk), ("gA", gA), ("gnB", gnB), ("ws00", ws00), ("b0", b0)):
            cst[:, CO[name]:CO[name] + arr.shape[1]] = arr
        m = dict(shared)
        m.update(
            x_cur=np.ascontiguousarray(x_prompt[b, hf * T:(hf + 1) * T]),
            x_prev=np.ascontiguousarray(x_prompt[b, 0:T]) if hf == 1 else zeros_prev,
            x_smp=np.ascontiguousarray(x_sample[c * TS:(c + 1) * TS, 0]),
            mem=np.ascontiguousarray(mem_prompt[b]),
            ck=np.ascontiguousarray(cache_mem_k[0, c * TS:(c + 1) * TS].reshape(TS, 256, D)),
            cv=np.ascontiguousarray(cache_mem_v[0, c * TS:(c + 1) * TS].reshape(TS, 256, D)),
            st_in=np.ascontiguousarray(state_ret[0, c * TS:(c + 1) * TS]),
            cst=cst, ropep=ropep,
        )
        in_maps.append(m)
    if _CACHE.get("test_cores"):
        n = _CACHE["test_cores"]
        res = run_bass_kernel_spmd(nc, in_maps[:n], core_ids=list(range(n)), trace=bool(_CACHE.get("trace")))
        _CACHE["exec_ns"] = res.exec_time_ns
        return res.results
    res = run_bass_kernel_spmd(nc, in_maps, core_ids=list(range(8)))
    R = res.results
    y_prompt = np.zeros((4, 2048, D), np.float32)
    y_sample = np.zeros((128, 1, D), np.float32)
    mk = np.zeros((1, 4, 256, 4, 512), np.float32)
    mv = np.zeros((1, 4, 256, 4, 512), np.float32)
    sp = np.zeros((1, 4, 8, 128, 128), np.float32)
    ss = np.zeros((1, 128, 8, 128, 128), np.float32)
    cvs = np.zeros((1, 128, 1, 8, 128), np.float32)
    for c in range(8):
        b, hf = c // 2, c % 2
        r = R[c]
        y_prompt[b, hf * T:(hf + 1) * T] = r["y_cur"]
        y_sample[c * TS:(c + 1) * TS, 0] = r["y_smp"]
        ss[0, c * TS:(c + 1) * TS] = r["sts_o"]
        cvs[0, c * TS:(c + 1) * TS, 0] = r["cvs_o"].reshape(TS, 8, 128)
        if hf == 0:
            mk[0, b] = r["mk_o"].reshape(256, 4, 512)
            mv[0, b] = r["mv_o"].reshape(256, 4, 512)
        else:
            sp[0, b] = r["stp_o"]
    return (y_prompt, y_sample, mk, mv, sp, ss, cvs)
```
